# Optimizing a Trainium2 kernel written in Bass

```python
import jax, jax.numpy as jnp
from jax import lax
import numpy as np


D_MODEL = 2048
BATCH = 16
SEQ = 2048
DEPTH = 4

N_A_LAYERS = DEPTH // 2
N_B_LAYERS = DEPTH - N_A_LAYERS
HEAD_DIM = 128
N_MIX_HEADS = D_MODEL // HEAD_DIM
MEM_HEADS = 4
SB_HEADS = N_MIX_HEADS - MEM_HEADS
MLA_HEADS = N_MIX_HEADS - MEM_HEADS
MLA_NOPE_DIM = 128
MLA_ROPE_DIM = 64
MLA_V_DIM = 128
Q_LORA_RANK = 512
KV_LORA_RANK = 512
MEM_LEN = 256
FFN_HIDDEN = -(-8 * D_MODEL // (3 * 256)) * 256
BLOCK_Q = 128
ROPE_THETA = 10000.0
RMS_EPS = 1e-6

A_IN_WIDTH = 3 * SB_HEADS * HEAD_DIM + MEM_HEADS * HEAD_DIM
B_IN_WIDTH = Q_LORA_RANK + MEM_HEADS * HEAD_DIM
A_OUT_WIDTH = SB_HEADS * HEAD_DIM + MEM_HEADS * HEAD_DIM
B_OUT_WIDTH = MLA_HEADS * MLA_V_DIM + MEM_HEADS * HEAD_DIM

kernel_name = "yoco_stickbreak_mla_hybrid"


def rmsnorm(x, g):
    x32 = x.astype(jnp.float32)
    y = x32 * lax.rsqrt(jnp.mean(x32 * x32, axis=-1, keepdims=True) + RMS_EPS)
    return y.astype(x.dtype) * g


def rope_tables(positions, dtype):
    half = MLA_ROPE_DIM // 2
    inv_freq = ROPE_THETA ** (-jnp.arange(half, dtype=jnp.float32) / half)
    ang = positions.astype(jnp.float32)[..., None] * inv_freq
    return jnp.cos(ang).astype(dtype), jnp.sin(ang).astype(dtype)


def apply_rope(x, cos, sin):
    half = x.shape[-1] // 2
    x1, x2 = x[..., :half], x[..., half:]
    return jnp.concatenate([x1 * cos - x2 * sin, x2 * cos + x1 * sin], axis=-1)


def to_query_blocks(t):
    b, s, h, d = t.shape
    return t.reshape(b, s // BLOCK_Q, BLOCK_Q, h, d).transpose(1, 0, 3, 2, 4)


def from_query_blocks(t):
    nb, b, h, bq, d = t.shape
    return t.transpose(1, 0, 3, 2, 4).reshape(b, nb * bq, h * d)


def stick_breaking_attention(q, k, v):
    seq = q.shape[1]
    scale = q.shape[-1] ** -0.5
    kh = k.transpose(0, 2, 1, 3)
    vh = v.transpose(0, 2, 1, 3)
    key_pos = jnp.arange(seq)
    starts = jnp.arange(seq // BLOCK_Q) * BLOCK_Q

    def one_block(args):
        q_blk, start = args
        z = jnp.einsum("bhqd,bhkd->bhqk", q_blk, kh).astype(jnp.float32) * scale
        query_pos = start + jnp.arange(BLOCK_Q)
        strictly_past = key_pos[None, :] < query_pos[:, None]
        log_keep = jnp.where(strictly_past, -jax.nn.softplus(z), 0.0)
        log_stick = lax.cumsum(log_keep, axis=3, reverse=True) - log_keep
        w = jnp.where(strictly_past, jnp.exp(jax.nn.log_sigmoid(z) + log_stick), 0.0)
        return jnp.einsum("bhqk,bhkd->bhqd", w.astype(vh.dtype), vh)

    out = lax.map(one_block, (to_query_blocks(q), starts))
    return from_query_blocks(out)


def mla_attention(q_nope, q_rope, k_nope, k_rope, v):
    seq = q_nope.shape[1]
    scale = (MLA_NOPE_DIM + MLA_ROPE_DIM) ** -0.5
    knh = k_nope.transpose(0, 2, 1, 3)
    vh = v.transpose(0, 2, 1, 3)
    key_pos = jnp.arange(seq)
    starts = jnp.arange(seq // BLOCK_Q) * BLOCK_Q

    def one_block(args):
        qn, qr, start = args
        s = (jnp.einsum("bhqd,bhkd->bhqk", qn, knh)
             + jnp.einsum("bhqr,bkr->bhqk", qr, k_rope)).astype(jnp.float32) * scale
        causal = key_pos[None, :] <= (start + jnp.arange(BLOCK_Q))[:, None]
        p = jax.nn.softmax(jnp.where(causal, s, -jnp.inf), axis=-1)
        return jnp.einsum("bhqk,bhkd->bhqd", p.astype(vh.dtype), vh)

    out = lax.map(one_block, (to_query_blocks(q_nope), to_query_blocks(q_rope), starts))
    return from_query_blocks(out)


def memory_attention(q, k, v):
    b, s, hm, d = q.shape
    sc = jnp.einsum("bshd,bmhd->bhsm", q, k).astype(jnp.float32) * d ** -0.5
    p = jax.nn.softmax(sc, axis=-1)
    o = jnp.einsum("bhsm,bmhd->bshd", p.astype(v.dtype), v)
    return o.reshape(b, s, hm * d)


def swiglu(x, w_gate_up, w_down):
    gate, up = jnp.split(x @ w_gate_up, 2, axis=-1)
    return (jax.nn.silu(gate) * up) @ w_down


def shared_latent_kv(h, kv_norm_g, w_dkv, kv_latent_g, w_ukv, cos, sin):
    b, s, _ = h.shape
    ckv = rmsnorm(h, kv_norm_g) @ w_dkv
    c_latent = rmsnorm(ckv[..., :KV_LORA_RANK], kv_latent_g)
    k_rope = apply_rope(ckv[..., KV_LORA_RANK:], cos, sin)
    kv = (c_latent @ w_ukv).reshape(b, s, MLA_HEADS, MLA_NOPE_DIM + MLA_V_DIM)
    return kv[..., :MLA_NOPE_DIM], k_rope, kv[..., MLA_NOPE_DIM:]


def setup_inputs(seed: int = 0) -> dict:
    key = jax.random.key(seed)
    ks = jax.random.split(key, 24)

    def dense(k, shape):
        return jax.random.normal(k, shape, jnp.float32) * shape[-2] ** -0.5

    def gain(k, shape):
        return 1.0 + 0.02 * jax.random.normal(k, shape, jnp.float32)

    start = jax.random.randint(ks[2], (BATCH, 1), 0, 4096, dtype=jnp.int32)
    positions = start + jnp.arange(SEQ, dtype=jnp.int32)[None, :]
    return {
        "x": jax.random.normal(ks[0], (BATCH, SEQ, D_MODEL), jnp.float32),
        "mem": jax.random.normal(ks[1], (BATCH, MEM_LEN, D_MODEL), jnp.float32),
        "positions": positions,
        "attn_norm_g": gain(ks[3], (DEPTH, D_MODEL)),
        "ffn_norm_g": gain(ks[4], (DEPTH, D_MODEL)),
        "a_w_in": dense(ks[5], (N_A_LAYERS, D_MODEL, A_IN_WIDTH)),
        "a_w_out": dense(ks[6], (N_A_LAYERS, A_OUT_WIDTH, D_MODEL)),
        "b_w_in": dense(ks[7], (N_B_LAYERS, D_MODEL, B_IN_WIDTH)),
        "b_q_norm_g": gain(ks[8], (N_B_LAYERS, Q_LORA_RANK)),
        "b_w_uq": dense(ks[9], (N_B_LAYERS, Q_LORA_RANK, MLA_HEADS * (MLA_NOPE_DIM + MLA_ROPE_DIM))),
        "b_w_out": dense(ks[10], (N_B_LAYERS, B_OUT_WIDTH, D_MODEL)),
        "mem_norm_g": gain(ks[11], (D_MODEL,)),
        "w_mem_kv": dense(ks[12], (DEPTH, D_MODEL, 2 * MEM_HEADS * HEAD_DIM)),
        "kv_norm_g": gain(ks[13], (D_MODEL,)),
        "w_dkv": dense(ks[14], (D_MODEL, KV_LORA_RANK + MLA_ROPE_DIM)),
        "kv_latent_g": gain(ks[15], (KV_LORA_RANK,)),
        "w_ukv": dense(ks[16], (KV_LORA_RANK, MLA_HEADS * (MLA_NOPE_DIM + MLA_V_DIM))),
        "ffn_w_gu": dense(ks[17], (DEPTH, D_MODEL, 2 * FFN_HIDDEN)),
        "ffn_w_down": dense(ks[18], (DEPTH, FFN_HIDDEN, D_MODEL)),
        "final_norm_g": gain(ks[19], (D_MODEL,)),
    }


def reference(x, mem, positions, attn_norm_g, ffn_norm_g, a_w_in, a_w_out, b_w_in,
              b_q_norm_g, b_w_uq, b_w_out, mem_norm_g, w_mem_kv, kv_norm_g, w_dkv,
              kv_latent_g, w_ukv, ffn_w_gu, ffn_w_down, final_norm_g):
    b, s, _ = x.shape
    cos, sin = rope_tables(positions, x.dtype)
    mem_n = rmsnorm(mem, mem_norm_g)
    sb_w = SB_HEADS * HEAD_DIM
    mq_w = MEM_HEADS * HEAD_DIM
    h = x
    for layer in range(DEPTH):
        if layer == N_A_LAYERS:
            k_nope, k_rope, v_lat = shared_latent_kv(h, kv_norm_g, w_dkv, kv_latent_g, w_ukv, cos, sin)
        xn = rmsnorm(h, attn_norm_g[layer])
        mkv = (mem_n @ w_mem_kv[layer]).reshape(b, MEM_LEN, 2, MEM_HEADS, HEAD_DIM)
        if layer < N_A_LAYERS:
            proj = xn @ a_w_in[layer]
            q = proj[..., :sb_w].reshape(b, s, SB_HEADS, HEAD_DIM)
            k = proj[..., sb_w:2 * sb_w].reshape(b, s, SB_HEADS, HEAD_DIM)
            v = proj[..., 2 * sb_w:3 * sb_w].reshape(b, s, SB_HEADS, HEAD_DIM)
            q_mem = proj[..., 3 * sb_w:].reshape(b, s, MEM_HEADS, HEAD_DIM)
            mix = stick_breaking_attention(q, k, v)
            w_out = a_w_out[layer]
        else:
            i = layer - N_A_LAYERS
            proj = xn @ b_w_in[i]
            c_q = rmsnorm(proj[..., :Q_LORA_RANK], b_q_norm_g[i])
            q_mem = proj[..., Q_LORA_RANK:].reshape(b, s, MEM_HEADS, HEAD_DIM)
            q = (c_q @ b_w_uq[i]).reshape(b, s, MLA_HEADS, MLA_NOPE_DIM + MLA_ROPE_DIM)
            q_rope = apply_rope(q[..., MLA_NOPE_DIM:], cos[:, :, None, :], sin[:, :, None, :])
            mix = mla_attention(q[..., :MLA_NOPE_DIM], q_rope, k_nope, k_rope, v_lat)
            w_out = b_w_out[i]
        mem_out = memory_attention(q_mem, mkv[:, :, 0], mkv[:, :, 1])
        h = h + jnp.concatenate([mix, mem_out], axis=-1) @ w_out
        h = h + swiglu(rmsnorm(h, ffn_norm_g[layer]), ffn_w_gu[layer], ffn_w_down[layer])
    return rmsnorm(h, final_norm_g)
```

```python
import contextlib
import math
import numpy as np
import ml_dtypes
import concourse.bass as bass
import concourse.mybir as mybir
from concourse.bass_utils import run_bass_kernel_spmd

F32 = mybir.dt.float32
F32R = mybir.dt.float32r
BF16 = mybir.dt.bfloat16
I32 = mybir.dt.int32
AF = mybir.ActivationFunctionType
ALU = mybir.AluOpType

D = 2048
KC = 16
HD = 128
SBH = 12
MH = 4
FF = 5632
FC = 44
QL = 512
DEPTH = 4
NA = 2
EPS = 1e-6
N_CORES = 8

SAME_ENGINE_SYNC = True


class Buf:
    __slots__ = ("name", "w", "r", "dsem", "dcount", "dlast")

    def __init__(self, name=""):
        self.name = name
        self.w = None
        self.r = []
        self.dsem = None
        self.dcount = 0
        self.dlast = None


class Eng:
    def __init__(self, name, sem, inorder_skip=False):
        self.name = name
        self.sem = sem
        self.count = 0
        self.ops = []
        self.waited = {}
        self.inorder_skip = inorder_skip


class Sched:
    def __init__(self, nc, stack):
        self.nc = nc
        self.stack = stack
        self.nsem = 0
        self.E = {}
        for n in ("tensor", "vector", "scalar", "gpsimd", "sync"):
            self.E[n] = Eng(n, self.new_sem("e_" + n), inorder_skip=(n == "tensor"))

    def new_sem(self, name):
        self.nsem += 1
        assert self.nsem < 230, "too many semaphores"
        return self.stack.enter_context(self.nc.semaphore(name))

    def _deps(self, eng, reads, writes):
        deps = []
        for b in reads:
            if b.w is not None:
                deps.append(b.w)
        for b in writes:
            deps.extend(b.r)
            if b.w is not None:
                deps.append(b.w)
        need = {}
        for (s, v) in deps:
            if s is eng.sem and (eng.inorder_skip or not SAME_ENGINE_SYNC):
                continue
            k = id(s)
            if eng.waited.get(k, 0) >= v:
                continue
            if k not in need or need[k][1] < v:
                need[k] = (s, v)
        for k, (s, v) in need.items():
            eng.waited[k] = v
        return list(need.values())

    def op(self, engname, method, kwargs, reads=(), writes=(), signal=True):
        eng = self.E[engname]
        waits = self._deps(eng, reads, writes)
        if signal:
            eng.count += 1
            tok = (eng.sem, eng.count)
        else:
            tok = (eng.sem, eng.count + 1)
        sem = eng.sem

        def emit(e, waits=waits, method=method, kwargs=kwargs, signal=signal, sem=sem):
            for (s, v) in waits:
                e.wait_ge(s, v)
            ins = getattr(e, method)(**kwargs)
            if signal:
                ins.then_inc(sem, 1)
        eng.ops.append(emit)
        for b in reads:
            b.r.append(tok)
        for b in writes:
            b.w = tok
            b.r = []
        return tok

    def dma(self, engname, out_ap, in_ap, sembuf, reads=(), writes=(), chain=True):
        eng = self.E[engname]
        if sembuf.dsem is None:
            sembuf.dsem = self.new_sem("d_" + sembuf.name)
        waits = self._deps(eng, reads, writes)
        if chain and sembuf.dlast is not None:
            s, v = sembuf.dlast
            if eng.waited.get(id(s), 0) < v:
                eng.waited[id(s)] = v
                waits.append((s, v))
        sembuf.dcount += 16
        tok = (sembuf.dsem, sembuf.dcount)
        sembuf.dlast = tok
        dsem = sembuf.dsem

        def emit(e, waits=waits, dsem=dsem, out_ap=out_ap, in_ap=in_ap):
            for (s, v) in waits:
                e.wait_ge(s, v)
            e.dma_start(out=out_ap, in_=in_ap).then_inc(dsem, 16)
        eng.ops.append(emit)
        for b in reads:
            b.r.append(tok)
        for b in writes:
            b.w = tok
            b.r = []
        return tok

    def fence(self, old_bufs, new_bufs):
        best = {}
        for b in old_bufs:
            for t in list(b.r) + ([b.w] if b.w is not None else []):
                k = id(t[0])
                if k not in best or best[k][1] < t[1]:
                    best[k] = t
        for b in new_bufs:
            mine = dict(best)
            for t in b.r:
                k = id(t[0])
                if k not in mine or mine[k][1] < t[1]:
                    mine[k] = t
            b.r = list(mine.values())

    def final_wait(self, engname, bufs):
        eng = self.E[engname]
        toks = []
        for b in bufs:
            if b.w is not None:
                toks.append(b.w)

        def emit(e, toks=toks):
            for (s, v) in toks:
                e.wait_ge(s, v)
        eng.ops.append(emit)

    def run(self):
        nc = self.nc
        E = self.E
        with nc.Block() as block:
            @block.sync
            def _(e):
                for f in E["sync"].ops:
                    f(e)

            @block.scalar
            def _(e):
                for f in E["scalar"].ops:
                    f(e)

            @block.vector
            def _(e):
                for f in E["vector"].ops:
                    f(e)

            @block.gpsimd
            def _(e):
                for f in E["gpsimd"].ops:
                    f(e)

            @block.tensor
            def _(e):
                for f in E["tensor"].ops:
                    f(e)


def host_consts():
    p = np.arange(128)
    cst = np.zeros((128, 7, 128), np.float32)
    cst[:, 0, :] = np.eye(128)
    cst[:, 1, :] = (p[:, None] > p[None, :])
    cst[:, 2, :] = (p[:, None] <= p[None, :])
    cst[:, 3, :] = 1.0
    cst[:, 4, :] = 0.0
    cst[:, 5, :] = (p[:, None] < p[None, :])
    cst[:, 6, :] = (p[:, None] <= p[None, :])
    half = 32
    inv_freq = (10000.0 ** (-np.arange(half, dtype=np.float32) / half)).astype(np.float32)
    rc = np.zeros((128, 4), np.float32)
    rc[:, 0] = inv_freq[p % 32]
    rc[:, 1] = np.where((p % 64) < 32, -1.0, 1.0)
    return cst.astype(ml_dtypes.bfloat16), rc


def build(NS, S, ML, depth=DEPTH, dbg=None, only=None, cast_filter=None):
    T = NS * S
    TW = 512
    assert S % TW == 0
    NT = T // TW
    TPS = S // TW
    NBS = S // 128
    MT = NS * ML
    assert MT <= 512 and MT % 128 == 0
    na = depth // 2

    nc = bass.Bass("TRN2", target_bir_lowering=False)

    def din(name, shape, dt=F32):
        return nc.dram_tensor(name, list(shape), dt, kind="ExternalInput").ap()

    def dscr(name, shape, dt):
        kind = "ExternalOutput" if (dbg and not name.startswith("wb_")) else "Internal"
        return nc.dram_tensor(name, list(shape), dt, kind=kind).ap()

    x = din("x", [T, D])
    mem = din("mem", [MT, D])
    pos = din("positions", [T], I32)
    attn_norm_g = din("attn_norm_g", [DEPTH, D])
    ffn_norm_g = din("ffn_norm_g", [DEPTH, D])
    a_w_in = din("a_w_in", [NA, D, 5120])
    a_w_out = din("a_w_out", [NA, D, D])
    b_w_in = din("b_w_in", [NA, D, 1024])
    b_q_norm_g = din("b_q_norm_g", [NA, QL])
    b_w_uq = din("b_w_uq", [NA, QL, 2304])
    b_w_out = din("b_w_out", [NA, D, D])
    mem_norm_g = din("mem_norm_g", [1, D])
    w_mem_kv = din("w_mem_kv", [DEPTH, D, 1024])
    kv_norm_g = din("kv_norm_g", [1, D])
    w_dkv = din("w_dkv", [D, 576])
    kv_latent_g = din("kv_latent_g", [1, QL])
    w_ukv = din("w_ukv", [QL, 3072])
    ffn_w_gu = din("ffn_w_gu", [DEPTH, D, 2 * FF])
    ffn_w_down = din("ffn_w_down", [DEPTH, FF, D])
    final_norm_g = din("final_norm_g", [1, D])
    cst_d = din("cst", [128, 7, 128], BF16)
    rc_d = din("rc", [128, 4], F32)
    out = nc.dram_tensor("out", [T, D], F32, kind="ExternalOutput").ap()

    wb = {}
    wb["a_w_in"] = dscr("wb_a_w_in", [NA, D, 5120], BF16)
    wb["a_w_out"] = dscr("wb_a_w_out", [NA, D, D], BF16)
    wb["b_w_in"] = dscr("wb_b_w_in", [NA, D, 1024], BF16)
    wb["b_w_uq"] = dscr("wb_b_w_uq", [NA, QL, 2304], BF16)
    wb["b_w_out"] = dscr("wb_b_w_out", [NA, D, D], BF16)
    wb["w_mem_kv"] = dscr("wb_w_mem_kv", [DEPTH, D, 1024], BF16)
    wb["w_dkv"] = dscr("wb_w_dkv", [D, 576], BF16)
    wb["w_ukv"] = dscr("wb_w_ukv", [QL, 3072], BF16)
    wb["ffn_w_gu"] = dscr("wb_ffn_w_gu", [DEPTH, D, 2 * FF], BF16)
    wb["ffn_w_down"] = dscr("wb_ffn_w_down", [DEPTH, FF, D], BF16)

    Hs = dscr("Hs", [T, D], F32)
    H1 = dscr("H1", [T, D], F32)
    QA = dscr("QA", [SBH, 128, T], BF16)
    KA = dscr("KA", [SBH, 128, T], BF16)
    VA = dscr("VA", [T, SBH * 128], BF16)
    KN = dscr("KN", [SBH, 128, T], BF16)
    VL = dscr("VL", [T, SBH * 128], BF16)
    KR = dscr("KR", [128, T], BF16)
    QR = dscr("QR", [6, 128, T], BF16)
    QM = dscr("QM", [MH, 128, T], BF16)
    MK = dscr("MK", [DEPTH, MH, 128, MT], BF16)
    MV = dscr("MV", [DEPTH, MT, MH * 128], BF16)
    MIXT = dscr("MIXT", [D, T], BF16)
    CQ = dscr("CQ", [QL, T], BF16)
    COS = dscr("COS", [128, T], F32)
    SIN = dscr("SIN", [128, T], F32)

    with contextlib.ExitStack() as st:
        S_ = Sched(nc, st)

        def sb(name, shape, dt):
            return st.enter_context(nc.sbuf_tensor(name, list(shape), dt))

        def ps(name, shape, dt):
            return st.enter_context(nc.psum_tensor(name, list(shape), dt))

        cst = sb("cst_sb", [128, 7, 128], BF16); b_cst = Buf("cst")
        rc = sb("rc_sb", [128, 4], F32); b_rc = Buf("rc")
        NW = 4
        WS = [sb("ws%d" % i, [128, 8192], BF16) for i in range(NW)]
        b_WS = [Buf("ws%d" % i) for i in range(NW)]
        AT = sb("AT", [128, 8192], BF16); b_AT = Buf("AT")
        ARENA = sb("arena", [128, 22528], BF16)
        b_FA = Buf("FA")
        HB = [sb("hb%d" % i, [128, D], F32) for i in range(2)]
        b_HB = [Buf("hb%d" % i) for i in range(2)]
        XN = [sb("xn%d" % i, [128, D], BF16) for i in range(2)]
        b_XN = [Buf("xn%d" % i) for i in range(2)]
        GB = sb("gb", [128, D], F32); b_GB = Buf("gb")
        GQ = sb("gq", [128, QL], F32); b_GQ = Buf("gq")
        SS = [sb("ss%d" % i, [128, 1], F32) for i in range(2)]
        b_SS = [Buf("ss%d" % i) for i in range(2)]
        RS = [sb("rs%d" % i, [128, 1], F32) for i in range(2)]
        b_RS = [Buf("rs%d" % i) for i in range(2)]
        EVF = [sb("evf%d" % i, [128, 512], F32) for i in range(4)]
        b_EVF = [Buf("evf%d" % i) for i in range(4)]
        EVH = [sb("evh%d" % i, [128, 512], F32) for i in range(2)]
        b_EVH = [Buf("evh%d" % i) for i in range(2)]
        EVB = [sb("evb%d" % i, [128, 4, 512], BF16) for i in range(2)]
        b_EVB = [Buf("evb%d" % i) for i in range(2)]

        NPB = 6
        PB = [ps("pb%d" % i, [128, 512], F32) for i in range(NPB)]
        b_PB = [Buf("pb%d" % i) for i in range(NPB)]
        PTS = [ps("pt%d" % i, [128, 8, 128], BF16) for i in range(2)]
        b_PT = [Buf("pt0"), Buf("pt1")]
        ptctr = [0]

        ident = cst[:, 0, :]
        tri_lo = cst[:, 1, :]
        tri_up = cst[:, 2, :]
        ones_m = cst[:, 3, :]
        zeros_m = cst[:, 4, :]
        mask_st = cst[:, 5, :]
        mask_in = cst[:, 6, :]

        S_.dma("sync", cst[:], cst_d[:, :, :], b_cst, writes=[b_cst])
        S_.dma("sync", rc[:], rc_d[:, :], b_rc, writes=[b_rc])

        wbuf = {}

        def cast(name, idx, src2d, dst2d, rows, rchunk):
            b = Buf("c_%s%s" % (name, "" if idx is None else str(idx)))
            if cast_filter is not None and name not in cast_filter:
                wbuf[(name, idx)] = b
                return
            r = 0
            while r < rows:
                r1 = min(rows, r + rchunk)
                S_.dma("gpsimd", dst2d[r:r1, :], src2d[r:r1, :], b, writes=[b], chain=False)
                r = r1
            wbuf[(name, idx)] = b

        def cast_layer(l):
            if l < na:
                if l > 0:
                    cast("a_w_in", l, a_w_in[l], wb["a_w_in"][l], D, 512)
                cast("a_w_out", l, a_w_out[l], wb["a_w_out"][l], D, 1024)
            else:
                i = l - na
                if i == 0:
                    cast("w_dkv", None, w_dkv, wb["w_dkv"], D, 2048)
                    cast("w_ukv", None, w_ukv, wb["w_ukv"], QL, 512)
                cast("b_w_in", i, b_w_in[i], wb["b_w_in"][i], D, 2048)
                cast("b_w_uq", i, b_w_uq[i], wb["b_w_uq"][i], QL, 512)
                cast("b_w_out", i, b_w_out[i], wb["b_w_out"][i], D, 1024)
            cast("ffn_w_gu", l, ffn_w_gu[l], wb["ffn_w_gu"][l], D, 256)
            cast("ffn_w_down", l, ffn_w_down[l], wb["ffn_w_down"][l], FF, 1408)

        if na > 0:
            cast("a_w_in", 0, a_w_in[0], wb["a_w_in"][0], D, 512)
        for l in range(depth):
            cast("w_mem_kv", l, w_mem_kv[l], wb["w_mem_kv"][l], D, 2048)
        lazy_cast = (only is None)
        cast_layer(0)
        if not lazy_cast:
            for l in range(1, depth):
                cast_layer(l)

        STQ = "gpsimd"
        wctr = [0]

        def wslot_view(i, kc, n):
            return WS[i][:, 0:kc * n].rearrange("p (k n) -> p k n", n=n)

        def load_w(src3, kc, n, reads):
            i = wctr[0] % NW
            wctr[0] += 1
            v = wslot_view(i, kc, n)
            S_.dma("sync", v, src3, b_WS[i], reads=reads, writes=[b_WS[i]])
            return v, b_WS[i]

        def wsrc(w2d, c0, n, r0=0, kc=KC):
            return w2d[r0:r0 + kc * 128, c0:c0 + n].rearrange("(k p) n -> p k n", p=128)

        pbctr = [0]

        def next_pb():
            i = pbctr[0] % NPB
            pbctr[0] += 1
            return PB[i], b_PB[i]

        evf_ctr = [0]

        def next_evf():
            i = evf_ctr[0] % 4
            evf_ctr[0] += 1
            return EVF[i], b_EVF[i]

        evh_ctr = [0]

        def next_evh():
            i = evh_ctr[0] % 2
            evh_ctr[0] += 1
            return EVH[i], b_EVH[i]

        evb_ctr = [0]

        def next_evb():
            i = evb_ctr[0] % 2
            evb_ctr[0] += 1
            return EVB[i], b_EVB[i]

        alt = [0]

        def copy_alt(out_ap, in_ap, reads, writes):
            alt[0] ^= 1
            if alt[0]:
                S_.op("scalar", "activation", dict(out=out_ap, in_=in_ap, func=AF.Copy), reads=reads, writes=writes)
            else:
                S_.op("vector", "tensor_copy", dict(out=out_ap, in_=in_ap), reads=reads, writes=writes)

        def rstd_of(src_ap, n, b_src, j, junk_ap, b_junk):
            S_.op("scalar", "activation", dict(out=junk_ap, in_=src_ap, func=AF.Square, accum_out=SS[j][:]),
                  reads=[b_src], writes=[b_junk, b_SS[j]])
            S_.op("scalar", "activation", dict(out=RS[j][:], in_=SS[j][:], func=AF.Ln, scale=1.0 / n, bias=EPS),
                  reads=[b_SS[j]], writes=[b_RS[j]])
            S_.op("scalar", "activation", dict(out=RS[j][:], in_=RS[j][:], func=AF.Exp, scale=-0.5),
                  reads=[b_RS[j]], writes=[b_RS[j]])

        hb_ctr = [0]

        def transpose_into(src_bf, b_src, nk, dstT, b_dst, col0, ncol=128):
            for k0 in range(0, nk, 8):
                h = ptctr[0] % 2
                ptctr[0] += 1
                PT = PTS[h]
                kk = min(8, nk - k0)
                for k in range(k0, k0 + kk):
                    pt_ap = PT[:, k - k0, 0:ncol]
                    in_ap = src_bf[0:ncol, k * 128:(k + 1) * 128]
                    S_.op("tensor", "transpose", dict(out=pt_ap, in_=in_ap, identity=ident[0:ncol, 0:ncol]),
                          reads=[b_src, b_cst], writes=[b_PT[h]], signal=(k == k0 + kk - 1))
                o_ap = dstT[:, k0:k0 + kk, col0:col0 + ncol]
                i_ap = PT[:, 0:kk, 0:ncol]
                S_.op("vector", "tensor_copy", dict(out=o_ap, in_=i_ap),
                      reads=[b_PT[h]], writes=[b_dst])

        def load_gain(g_row, n=D):
            S_.dma("sync", GB[:, 0:n], g_row.partition_broadcast(128), b_GB, writes=[b_GB])

        def norm_tile(src, row0, nrows, b_srcs, dstT, b_dst):
            for blk in range(nrows // 128):
                j = hb_ctr[0] % 2
                hb_ctr[0] += 1
                r0 = row0 + blk * 128
                S_.dma("sync", HB[j][:], src[r0:r0 + 128, :], b_HB[j], reads=b_srcs, writes=[b_HB[j]])
                rstd_of(HB[j][:], D, b_HB[j], j, XN[j][:], b_XN[j])
                S_.op("vector", "scalar_tensor_tensor", dict(out=XN[j][:], in0=HB[j][:], scalar=RS[j][:, 0:1], in1=GB[:],
                                                                     op0=ALU.mult, op1=ALU.mult),
                      reads=[b_HB[j], b_RS[j], b_GB], writes=[b_XN[j]])
                transpose_into(XN[j], b_XN[j], KC, dstT, b_dst, blk * 128)

        def lin_ws(wv, b_w, kc, nch, actT, b_act, ncols, evac, col_off=0, m=128):
            for c in range(nch):
                bank, b_bank = next_pb()
                for k in range(kc):
                    l_ap = wv[:, k, col_off + c * m:col_off + (c + 1) * m]
                    r_ap = actT[:, k, 0:ncols]
                    o_ap = bank[0:m, 0:ncols]
                    S_.op("tensor", "matmul", dict(out=o_ap, lhsT=l_ap, rhs=r_ap, start=(k == 0), stop=(k == kc - 1)),
                          reads=[b_w, b_act], writes=[b_bank], signal=(k == kc - 1))
                evac(c, bank, b_bank)

        def lin_as(wv, b_w, kc, n, actT, b_act, nblk, evac, k_off=0):
            for blk in range(nblk):
                bank, b_bank = next_pb()
                for k in range(kc):
                    l_ap = actT[:, k_off + k, blk * 128:(blk + 1) * 128]
                    r_ap = wv[:, k, 0:n]
                    o_ap = bank[:, 0:n]
                    S_.op("tensor", "matmul", dict(out=o_ap, lhsT=l_ap, rhs=r_ap, start=(k == 0), stop=(k == kc - 1)),
                          reads=[b_w, b_act], writes=[b_bank], signal=(k == kc - 1))
                evac(blk, bank, b_bank)

        dbufs = {}

        def db(name, i=0):
            k = (name, i)
            if k not in dbufs:
                dbufs[k] = Buf("%s_%s" % (name, i))
            return dbufs[k]

        ATv = AT[:].rearrange("p (k n) -> p k n", n=512)

        def phase_rope():
            for tt in range(NT):
                t0 = tt * TW
                pi_t, b_pi = next_evf()
                S_.dma("sync", pi_t[:].bitcast(I32), pos[t0:t0 + TW].partition_broadcast(128), b_pi, writes=[b_pi])
                ang, b_ang = next_evf()
                S_.op("vector", "tensor_copy", dict(out=ang[:], in_=pi_t[:].bitcast(I32)),
                      reads=[b_pi], writes=[b_ang])
                S_.op("vector", "tensor_scalar", dict(out=ang[:], in0=ang[:], scalar1=rc[:, 0:1], scalar2=None, op0=ALU.mult),
                      reads=[b_ang, b_rc], writes=[b_ang])
                for which, shift, dst in ((0, 0.0, SIN), (1, 0.5 * math.pi, COS)):
                    y, b_y = next_evf()
                    kf, b_kf = next_evf()
                    TWO_PI = 2.0 * math.pi
                    S_.op("vector", "tensor_scalar", dict(out=y[:], in0=ang[:], scalar1=shift, scalar2=None, op0=ALU.add),
                          reads=[b_ang], writes=[b_y])
                    S_.op("vector", "tensor_scalar", dict(out=kf[:], in0=y[:], scalar1=1.0 / TWO_PI, scalar2=None, op0=ALU.mult),
                          reads=[b_y], writes=[b_kf])
                    S_.op("vector", "tensor_copy", dict(out=kf[:].bitcast(I32), in_=kf[:]), reads=[b_kf], writes=[b_kf])
                    S_.op("vector", "tensor_copy", dict(out=kf[:], in_=kf[:].bitcast(I32)), reads=[b_kf], writes=[b_kf])
                    S_.op("vector", "scalar_tensor_tensor", dict(out=y[:], in0=kf[:], scalar=-TWO_PI, in1=y[:], op0=ALU.mult, op1=ALU.add),
                          reads=[b_kf, b_y], writes=[b_y])
                    S_.op("vector", "tensor_scalar", dict(out=kf[:], in0=y[:], scalar1=math.pi, scalar2=TWO_PI, op0=ALU.is_gt, op1=ALU.mult),
                          reads=[b_y], writes=[b_kf])
                    S_.op("vector", "tensor_tensor", dict(out=y[:], in0=y[:], in1=kf[:], op=ALU.subtract), reads=[b_y, b_kf], writes=[b_y])
                    S_.op("vector", "tensor_scalar", dict(out=y[:], in0=y[:], scalar1=3.1415925, scalar2=-3.1415925, op0=ALU.min, op1=ALU.max),
                          reads=[b_y], writes=[b_y])
                    S_.op("scalar", "activation", dict(out=y[:], in_=y[:], func=AF.Sin), reads=[b_y], writes=[b_y])
                    if which == 0:
                        S_.op("vector", "tensor_scalar", dict(out=y[:], in0=y[:], scalar1=rc[:, 1:2], scalar2=None, op0=ALU.mult),
                              reads=[b_y, b_rc], writes=[b_y])
                    S_.dma(STQ, dst[:, t0:t0 + TW], y[:], b_y, reads=[b_y], writes=[db("rope", tt)])

        def phase_mem():
            load_gain(mem_norm_g[0])
            norm_tile(mem, 0, MT, [], ATv, b_AT)
            for l in range(depth):
                wsrc2d = wb["w_mem_kv"][l]
                rd = [wbuf[("w_mem_kv", l)]]
                wk, b_wk = load_w(wsrc(wsrc2d, 0, 512), KC, 512, rd)
                evb, b_evb = next_evb()

                def evac_k(c, bank, b_bank, evb=evb, b_evb=b_evb):
                    copy_alt(evb[:, c, 0:MT], bank[:, 0:MT], [b_bank], [b_evb])
                lin_ws(wk, b_wk, KC, MH, ATv, b_AT, MT, evac_k)
                S_.dma(STQ, MK[l].rearrange("h p t -> p h t"), evb[:, :, 0:MT], b_evb, reads=[b_evb], writes=[db("MK", l)])
                wvv, b_wv = load_w(wsrc(wsrc2d, 512, 512), KC, 512, rd)
                evb2, b_evb2 = next_evb()

                def evac_v(blk, bank, b_bank, evb2=evb2, b_evb2=b_evb2):
                    copy_alt(evb2[:, blk, :], bank[:, :], [b_bank], [b_evb2])
                lin_as(wvv, b_wv, KC, 512, ATv, b_AT, MT // 128, evac_v)
                S_.dma(STQ, MV[l].rearrange("(b p) c -> p b c", p=128), evb2[:, 0:MT // 128, :], b_evb2, reads=[b_evb2], writes=[db("MV", l)])

        def phase_p1a(l, hsrc, hsrc_name):
            load_gain(attn_norm_g[l])
            w2d = wb["a_w_in"][l]
            rd = [wbuf[("a_w_in", l)]]
            for tt in range(NT):
                t0 = tt * TW
                norm_tile(hsrc, t0, TW, [db(hsrc_name, tt)], ATv, b_AT)
                for wt_i in list(range(6)) + [9]:
                    wv, b_w = load_w(wsrc(w2d, wt_i * 512, 512), KC, 512, rd)
                    evb, b_evb = next_evb()

                    def evac(c, bank, b_bank, evb=evb, b_evb=b_evb):
                        copy_alt(evb[:, c, :], bank[:, :], [b_bank], [b_evb])
                    lin_ws(wv, b_w, KC, 4, ATv, b_AT, TW, evac)
                    if wt_i < 3:
                        dst = QA[wt_i * 4:(wt_i + 1) * 4, :, t0:t0 + TW]; dn = "QA"
                    elif wt_i < 6:
                        dst = KA[(wt_i - 3) * 4:(wt_i - 2) * 4, :, t0:t0 + TW]; dn = "KA"
                    else:
                        dst = QM[:, :, t0:t0 + TW]; dn = "QM"
                    S_.dma(STQ, dst.rearrange("h p t -> p h t"), evb[:], b_evb, reads=[b_evb], writes=[db(dn + str(wt_i), tt)])
                for wt_i in range(6, 9):
                    wv, b_w = load_w(wsrc(w2d, wt_i * 512, 512), KC, 512, rd)
                    evb, b_evb = next_evb()

                    def evac(blk, bank, b_bank, evb=evb, b_evb=b_evb):
                        copy_alt(evb[:, blk, :], bank[:, :], [b_bank], [b_evb])
                    lin_as(wv, b_w, KC, 512, ATv, b_AT, TW // 128, evac)
                    c0 = (wt_i - 6) * 512
                    S_.dma(STQ, VA[t0:t0 + TW, c0:c0 + 512].rearrange("(b p) c -> p b c", p=128), evb[:], b_evb,
                           reads=[b_evb], writes=[db("VA" + str(wt_i), tt)])

        def seq_reads(names, s):
            r = []
            for n in names:
                for tt in range(s * TPS, (s + 1) * TPS):
                    r.append(db(n, tt))
            return r

        SM = 2048
        assert S <= SM

        def arena_view(slot, n=S):
            return ARENA[:, slot * SM:slot * SM + n]
        QT_ = [arena_view(i) for i in range(2)]
        KT_ = [arena_view(2 + i) for i in range(2)]
        VV_ = [arena_view(4 + i) for i in range(2)]
        QR_ = [arena_view(6 + i) for i in range(2)]
        KR_ = [arena_view(8), arena_view(9)]
        SPSR_t = [sb("spsr%d" % q, [128, 512], F32R) for q in range(2)]
        SPSR = [t[:, :] for t in SPSR_t]
        SPS = [t[:, :].bitcast(F32) for t in SPSR_t]
        b_SPS = [Buf("sps%d" % q) for q in range(2)]
        b_Q = [Buf("aq0"), Buf("aq1")]
        b_K = [Buf("ak0"), Buf("ak1")]
        b_V = [Buf("av0"), Buf("av1")]
        b_QR = [Buf("aqr0"), Buf("aqr1")]
        b_KR = [Buf("akr0"), Buf("akr1")]
        TE = [ARENA[:, 18432 + i * 1024:18432 + (i + 1) * 1024].bitcast(F32) for i in range(2)]; b_TE = [Buf("te0"), Buf("te1")]
        TSPR_t = [sb("tspr%d" % i, [128, 512], F32R) for i in range(2)]; b_TSP = [Buf("tsp0"), Buf("tsp1")]
        TSPR = [t[:, :] for t in TSPR_t]
        TSP = [t[:, :].bitcast(F32) for t in TSPR_t]
        TA2_t = sb("ta2", [128, 512], F32)
        TA = [ARENA[:, 20480 + i * 1024:20480 + (i + 1) * 1024].bitcast(F32) for i in range(2)] + [TA2_t[:, :]]
        b_TA = [Buf("ta0"), Buf("ta1"), Buf("ta2")]
        TW_ = [sb("tw%d" % i, [128, 512], BF16) for i in range(2)]; b_TW = [Buf("tw0"), Buf("tw1")]
        RD = sb("rden", [128, 512], F32); b_RD = Buf("rden")
        OB = [sb("ob%d" % i, [128, 512], BF16) for i in range(2)]; b_OB = [Buf("ob0"), Buf("ob1")]
        C32 = sb("c32", [128, 2, 128], F32R); b_C32 = Buf("c32")
        S_.op("vector", "tensor_copy", dict(out=C32[:, 0, :], in_=tri_lo), reads=[b_cst], writes=[b_C32])
        S_.op("vector", "tensor_copy", dict(out=C32[:, 1, :], in_=ones_m), reads=[b_cst], writes=[b_C32])
        TRI32R = C32[:, 0, :]
        ONES32R = C32[:, 1, :]
        att_bufs = b_Q + b_K + b_V + b_QR + b_KR + b_TE + b_TA

        def att_begin():
            S_.fence([b_FA] + att_bufs, att_bufs)

        def att_end():
            S_.fence(att_bufs, [b_FA])

        def phase_p2a(l):
            scale = 128 ** -0.5
            att_begin()
            heads = [(s, h) for s in range(NS) for h in range(SBH)]

            def load_head(hix):
                s, h = heads[hix]
                j = hix % 2
                c0 = s * S
                S_.dma("sync", QT_[j], QA[h, :, c0:c0 + S], b_Q[j], reads=seq_reads(["QA%d" % (h // 4)], s), writes=[b_Q[j]])
                S_.dma("sync", KT_[j], KA[h, :, c0:c0 + S], b_K[j], reads=seq_reads(["KA%d" % (3 + h // 4)], s), writes=[b_K[j]])
                S_.dma("sync", VV_[j].rearrange("p (b d) -> p b d", d=128),
                       VA[c0:c0 + S, h * 128:(h + 1) * 128].rearrange("(b p) d -> p b d", p=128), b_V[j],
                       reads=seq_reads(["VA%d" % (6 + h // 4)], s), writes=[b_V[j]])

            jobs = []
            tix = 0
            for hix in range(len(heads)):
                for qt in range(TPS):
                    nkb = (qt * TW + TW) // 128
                    NP = nkb
                    for n, kb in enumerate(range(nkb - 1, -1, -1)):
                        jj = kb - qt * 4
                        cs = max(0, jj) * 128
                        jobs.append((hix, qt, n, kb, cs, jj, NP, tix))
                    tix += 1
            NJ = len(jobs)
            CUMS = [(PB[2], b_PB[2]), (PB[5], b_PB[5])]
            load_head(0)
            if len(heads) > 1:
                load_head(1)

            def after_job(g):
                hix = jobs[g][0]
                if (g + 1 == NJ or jobs[g + 1][0] != hix) and hix + 2 < len(heads):
                    load_head(hix + 2)

            def stA(g):
                hix, qt, n, kb, cs, jj, NP, tix = jobs[g]
                j = hix % 2
                q0 = qt * TW
                if n == 0:
                    OO, b_OO = PB[3 + (tix % 2)], b_PB[3 + (tix % 2)]
                    S_.op("tensor", "matmul", dict(out=OO[:, :], lhsT=zeros_m, rhs=QT_[j][:, q0:q0 + TW], start=True, stop=False),
                          reads=[b_cst, b_Q[j]], writes=[b_OO], signal=False)
                Z, b_Z = PB[g % 2], b_PB[g % 2]
                S_.op("tensor", "matmul", dict(out=Z[:, cs:TW], lhsT=KT_[j][:, kb * 128:(kb + 1) * 128], rhs=QT_[j][:, q0 + cs:q0 + TW],
                                               start=True, stop=True),
                      reads=[b_K[j], b_Q[j]], writes=[b_Z])

            def stB(g):
                hix, qt, n, kb, cs, jj, NP, tix = jobs[g]
                m = g % 2
                Z, b_Z = PB[m], b_PB[m]
                S_.op("scalar", "activation", dict(out=TE[m][:, cs:TW], in_=Z[:, cs:TW], func=AF.Exp, scale=scale),
                      reads=[b_Z], writes=[b_TE[m]])
                S_.op("scalar", "activation", dict(out=TSPR[m][:, cs:TW], in_=TE[m][:, cs:TW], func=AF.Ln, bias=1.0, scale=1.0),
                      reads=[b_TE[m]], writes=[b_TSP[m]])
                S_.op("vector", "scalar_tensor_tensor", dict(out=TA[g % 3][:, cs:TW], in0=Z[:, cs:TW], scalar=scale, in1=TSP[m][:, cs:TW],
                                                             op0=ALU.mult, op1=ALU.subtract),
                      reads=[b_Z, b_TSP[m]], writes=[b_TA[g % 3]])
                if jj >= 0:
                    S_.op("gpsimd", "tensor_tensor", dict(out=TSPR[m][:, cs:cs + 128], in0=TSP[m][:, cs:cs + 128], in1=mask_st, op=ALU.mult),
                          reads=[b_TSP[m], b_cst, b_TA[g % 3]], writes=[b_TSP[m]])

            def stC(g):
                hix, qt, n, kb, cs, jj, NP, tix = jobs[g]
                m = g % 2
                CUM, b_CUM = CUMS[m]
                sps = SPS
                spsr = SPSR
                b_sps = b_SPS
                if n == 0:
                    for q in range(2):
                        S_.op("vector", "tensor_scalar", dict(out=SPSR[q], in0=cst[:, 3:7, :].rearrange("p a b -> p (a b)"), scalar1=0.0, scalar2=None, op0=ALU.mult),
                              reads=[b_cst], writes=[b_SPS[q]])
                S_.op("tensor", "matmul", dict(out=CUM[:, cs:TW], lhsT=TRI32R, rhs=TSPR[m][:, cs:TW], start=True, stop=(n == 0)),
                      reads=[b_C32, b_TSP[m]], writes=[b_CUM], signal=(n == 0))
                if n > 0:
                    S_.op("tensor", "matmul", dict(out=CUM[:, cs:TW], lhsT=ONES32R, rhs=spsr[n % 2][:, cs:TW], start=False, stop=True),
                          reads=[b_C32, b_sps[n % 2]], writes=[b_CUM])
                if n + 1 < NP:
                    S_.op("gpsimd", "tensor_tensor", dict(out=spsr[(n + 1) % 2][:, cs:TW], in0=sps[n % 2][:, cs:TW], in1=TSP[m][:, cs:TW], op=ALU.add),
                          reads=[b_sps[n % 2], b_TSP[m]], writes=[b_sps[(n + 1) % 2]])

            def stD(g):
                hix, qt, n, kb, cs, jj, NP, tix = jobs[g]
                m = g % 2
                CUM, b_CUM = CUMS[m]
                S_.op("vector", "tensor_tensor", dict(out=TA[g % 3][:, cs:TW], in0=TA[g % 3][:, cs:TW], in1=CUM[:, cs:TW], op=ALU.subtract),
                      reads=[b_TA[g % 3], b_CUM], writes=[b_TA[g % 3]])

            def stF(g):
                hix, qt, n, kb, cs, jj, NP, tix = jobs[g]
                m = g % 2
                S_.op("scalar", "activation", dict(out=TW_[m][:, cs:TW], in_=TA[g % 3][:, cs:TW], func=AF.Exp),
                      reads=[b_TA[g % 3]], writes=[b_TW[m]])
                if jj >= 0:
                    S_.op("gpsimd", "tensor_tensor", dict(out=TW_[m][:, cs:cs + 128], in0=TW_[m][:, cs:cs + 128], in1=mask_st, op=ALU.mult),
                          reads=[b_TW[m], b_cst], writes=[b_TW[m]])

            def stG(g):
                hix, qt, n, kb, cs, jj, NP, tix = jobs[g]
                m = g % 2
                j = hix % 2
                s, h = heads[hix]
                OO, b_OO = PB[3 + (tix % 2)], b_PB[3 + (tix % 2)]
                S_.op("tensor", "matmul", dict(out=OO[:, cs:TW], lhsT=VV_[j][:, kb * 128:(kb + 1) * 128], rhs=TW_[m][:, cs:TW],
                                               start=False, stop=(n == NP - 1)),
                      reads=[b_V[j], b_TW[m]], writes=[b_OO], signal=True)
                if n == NP - 1:
                    ob, b_ob = OB[tix % 2], b_OB[tix % 2]
                    copy_alt(ob[:, :], OO[:, :], [b_OO], [b_ob])
                    c0 = s * S + qt * TW
                    S_.dma("sync", MIXT[h * 128:(h + 1) * 128, c0:c0 + TW], ob[:, :], b_ob, reads=[b_ob],
                           writes=[db("MIXT%d" % h, s * TPS + qt)])

            for i in range(-2, NJ + 1):
                if 0 <= i + 2 < NJ:
                    stA(i + 2)
                    stB(i + 2)
                if 0 <= i + 1 < NJ:
                    stC(i + 1)
                    stD(i + 1)
                if 0 <= i - 1 < NJ:
                    stG(i - 1)
                    after_job(i - 1)
                if 0 <= i < NJ:
                    stF(i)
            att_end()

        def softmax_heads(heads, load_head, tile_jobs, z_mms, v_of, mask, scale, out_rows):
            jobs = []
            tix = 0
            for hix in range(len(heads)):
                for qt in range(TPS):
                    kbs = tile_jobs(qt)
                    for idx, (kb, cs, dg) in enumerate(kbs):
                        jobs.append((hix, qt, kb, cs, dg, idx == 0, idx == len(kbs) - 1, tix))
                    tix += 1
            NJ = len(jobs)
            DNS = [(PB[5], b_PB[5]), (PB[2], b_PB[2])]
            load_head(0)
            if len(heads) > 1:
                load_head(1)

            def after_job(g):
                hix = jobs[g][0]
                if (g + 1 == NJ or jobs[g + 1][0] != hix) and hix + 2 < len(heads):
                    load_head(hix + 2)

            def stZ(g):
                hix, qt, kb, cs, dg, first, last, tix = jobs[g]
                z_mms(heads[hix], hix % 2, qt, kb, cs, PB[g % 2], b_PB[g % 2])

            def stE(g):
                hix, qt, kb, cs, dg, first, last, tix = jobs[g]
                m = g % 2
                Z, b_Z = PB[m], b_PB[m]
                S_.op("scalar", "activation", dict(out=TW_[m][:, cs:TW], in_=Z[:, cs:TW], func=AF.Exp, scale=scale),
                      reads=[b_Z], writes=[b_TW[m]])
                if dg:
                    S_.op("gpsimd", "tensor_tensor", dict(out=TW_[m][:, cs:cs + 128], in0=TW_[m][:, cs:cs + 128], in1=mask, op=ALU.mult),
                          reads=[b_TW[m], b_cst], writes=[b_TW[m]])

            def stP(g):
                hix, qt, kb, cs, dg, first, last, tix = jobs[g]
                m = g % 2
                j = hix % 2
                OO, b_OO = PB[3 + (tix % 2)], b_PB[3 + (tix % 2)]
                DN, b_DN = DNS[tix % 2]
                v_ap, b_v = v_of(j, kb)
                S_.op("tensor", "matmul", dict(out=OO[:, cs:TW], lhsT=v_ap, rhs=TW_[m][:, cs:TW], start=first, stop=last),
                      reads=[b_v, b_TW[m]], writes=[b_OO], signal=False)
                S_.op("tensor", "matmul", dict(out=DN[:, cs:TW], lhsT=ones_m, rhs=TW_[m][:, cs:TW], start=first, stop=last),
                      reads=[b_cst, b_TW[m]], writes=[b_DN, b_OO], signal=True)
                if last:
                    S_.op("vector", "reciprocal", dict(out=RD[:, :], in_=DN[:, :]), reads=[b_DN], writes=[b_RD])
                    ob, b_ob = OB[tix % 2], b_OB[tix % 2]
                    S_.op("vector", "tensor_tensor", dict(out=ob[:, :], in0=OO[:, :], in1=RD[:, :], op=ALU.mult),
                          reads=[b_OO, b_RD], writes=[b_ob])
                    r0, c0, nm, ti = out_rows(heads[hix], qt)
                    S_.dma("sync", MIXT[r0:r0 + 128, c0:c0 + TW], ob[:, :], b_ob, reads=[b_ob], writes=[db(nm, ti)])

            for i in range(-1, NJ):
                if i + 1 < NJ:
                    stZ(i + 1)
                if i >= 0:
                    stP(i)
                    after_job(i)
                if i + 1 < NJ:
                    stE(i + 1)

        def phase_mem_attn(l):
            att_begin()
            nmb = ML // 128
            heads = [(s, h) for s in range(NS) for h in range(MH)]

            def load_head(hix):
                s, h = heads[hix]
                j = hix % 2
                c0 = s * S
                S_.dma("sync", QT_[j], QM[h, :, c0:c0 + S], b_Q[j], reads=seq_reads(["QM9", "QMB"], s), writes=[b_Q[j]])
                S_.dma("sync", KT_[j][:, 0:ML], MK[l, h, :, s * ML:(s + 1) * ML], b_K[j], reads=[db("MK", l)], writes=[b_K[j]])
                S_.dma("sync", VV_[j][:, 0:nmb * 128].rearrange("p (b d) -> p b d", d=128),
                       MV[l, s * ML:(s + 1) * ML, h * 128:(h + 1) * 128].rearrange("(b p) d -> p b d", p=128), b_V[j],
                       reads=[db("MV", l)], writes=[b_V[j]])

            def tile_jobs(qt):
                return [(mb, 0, False) for mb in range(nmb)]

            def z_mms(hd, j, qt, kb, cs, Z, b_Z):
                q0 = qt * TW
                S_.op("tensor", "matmul", dict(out=Z[:, :], lhsT=KT_[j][:, kb * 128:(kb + 1) * 128], rhs=QT_[j][:, q0:q0 + TW], start=True, stop=True),
                      reads=[b_K[j], b_Q[j]], writes=[b_Z])

            def v_of(j, kb):
                return VV_[j][:, kb * 128:(kb + 1) * 128], b_V[j]

            def out_rows(hd, qt):
                s, h = hd
                return (SBH + h) * 128, s * S + qt * TW, "MIXT%d" % (SBH + h), s * TPS + qt
            softmax_heads(heads, load_head, tile_jobs, z_mms, v_of, mask_in, 128 ** -0.5, out_rows)
            att_end()

        FAv = ARENA[:].rearrange("p (k n) -> p k n", n=512)

        def phase_p3(l, hsrc, hsrc_name, wout2d, wout_rd):
            wgu = wb["ffn_w_gu"][l]
            wdn = wb["ffn_w_down"][l]
            rd_gu = [wbuf[("ffn_w_gu", l)]]
            rd_dn = [wbuf[("ffn_w_down", l)]]
            load_gain(ffn_norm_g[l])
            for tt in range(NT):
                t0 = tt * TW
                S_.dma("sync", ATv, MIXT[:, t0:t0 + TW].rearrange("(k p) t -> p k t", p=128), b_AT,
                       reads=[db("MIXT%d" % hh, tt) for hh in range(16)], writes=[b_AT])
                for cc in range(4):
                    wv, b_w = load_w(wsrc(wout2d, cc * 512, 512), KC, 512, wout_rd)

                    pend = {}

                    def ld_hsl(blk, cc=cc, pend=pend):
                        hsl, b_hsl = next_evh()
                        r0 = t0 + blk * 128
                        S_.dma(STQ, hsl[:, :], hsrc[r0:r0 + 128, cc * 512:(cc + 1) * 512], b_hsl, reads=[db(hsrc_name, tt)], writes=[b_hsl])
                        pend[blk] = (hsl, b_hsl)

                    def evac(blk, bank, b_bank, cc=cc, pend=pend, ld_hsl=ld_hsl):
                        if blk == 0:
                            ld_hsl(0)
                        if blk + 1 < TW // 128:
                            ld_hsl(blk + 1)
                        hsl, b_hsl = pend[blk]
                        r0 = t0 + blk * 128
                        ev, b_ev = next_evf()
                        S_.op("vector", "tensor_tensor", dict(out=ev[:, :], in0=bank[:, :], in1=hsl[:, :], op=ALU.add),
                              reads=[b_bank, b_hsl], writes=[b_ev])
                        S_.dma(STQ, H1[r0:r0 + 128, cc * 512:(cc + 1) * 512], ev[:, :], b_ev, reads=[b_ev], writes=[db("H1", tt)])
                    lin_as(wv, b_w, KC, 512, ATv, b_AT, TW // 128, evac)
                pre_g = load_w(wsrc(wgu, 0, 512), KC, 512, rd_gu)
                pre_u = load_w(wsrc(wgu, FF, 512), KC, 512, rd_gu)
                norm_tile(H1, t0, TW, [db("H1", tt)], ATv, b_AT)
                for g4 in range(FC // 4):
                    if g4 == 0:
                        (wg, b_wg), (wu, b_wu) = pre_g, pre_u
                    else:
                        wg, b_wg = load_w(wsrc(wgu, g4 * 512, 512), KC, 512, rd_gu)
                        wu, b_wu = load_w(wsrc(wgu, FF + g4 * 512, 512), KC, 512, rd_gu)
                    for c in range(4):
                        fc = g4 * 4 + c
                        G, b_G = next_pb()
                        U, b_U = next_pb()
                        for (wv_, b_w_, bank_, b_bank_) in ((wg, b_wg, G, b_G), (wu, b_wu, U, b_U)):
                            for k in range(KC):
                                l_ap = wv_[:, k, c * 128:(c + 1) * 128]
                                r_ap = ATv[:, k, :]
                                S_.op("tensor", "matmul", dict(out=bank_[:, :], lhsT=l_ap, rhs=r_ap, start=(k == 0), stop=(k == KC - 1)),
                                      reads=[b_w_, b_AT], writes=[b_bank_], signal=(k == KC - 1))
                        sg, b_sg = next_evf()
                        S_.op("scalar", "activation", dict(out=sg[:, :], in_=G[:, :], func=AF.Silu), reads=[b_G], writes=[b_sg])
                        S_.op("vector", "tensor_tensor", dict(out=FAv[:, fc, :], in0=U[:, :], in1=sg[:, :], op=ALU.mult),
                              reads=[b_U, b_sg], writes=[b_FA])
                for cc in range(4):
                    banks = [next_pb() for _ in range(4)]
                    pieces = [(0, 16), (16, 16), (32, 12)]
                    for pi, (f0, nf) in enumerate(pieces):
                        wv, b_w = load_w(wsrc(wdn, cc * 512, 512, r0=f0 * 128, kc=nf), nf, 512, rd_dn)
                        for blk in range(4):
                            bank, b_bank = banks[blk]
                            for k in range(nf):
                                l_ap = FAv[:, f0 + k, blk * 128:(blk + 1) * 128]
                                r_ap = wv[:, k, :]
                                last = (pi == len(pieces) - 1 and k == nf - 1)
                                first = (pi == 0 and k == 0)
                                S_.op("tensor", "matmul", dict(out=bank[:, :], lhsT=l_ap, rhs=r_ap, start=first, stop=last),
                                      reads=[b_w, b_FA], writes=[b_bank], signal=(k == nf - 1))
                    pend = {}

                    def ld_h1(blk):
                        hsl, b_hsl = next_evh()
                        r0 = t0 + blk * 128
                        S_.dma(STQ, hsl[:, :], H1[r0:r0 + 128, cc * 512:(cc + 1) * 512], b_hsl, reads=[db("H1", tt)], writes=[b_hsl])
                        pend[blk] = (hsl, b_hsl)
                    ld_h1(0)
                    for blk in range(4):
                        bank, b_bank = banks[blk]
                        if blk + 1 < 4:
                            ld_h1(blk + 1)
                        hsl, b_hsl = pend[blk]
                        r0 = t0 + blk * 128
                        ev, b_ev = next_evf()
                        S_.op("vector", "tensor_tensor", dict(out=ev[:, :], in0=bank[:, :], in1=hsl[:, :], op=ALU.add),
                              reads=[b_bank, b_hsl], writes=[b_ev])
                        S_.dma(STQ, Hs[r0:r0 + 128, cc * 512:(cc + 1) * 512], ev[:, :], b_ev, reads=[b_ev], writes=[db("Hs", tt)])

        CQv = AT
        CT = sb("ct", [128, 4, 512], BF16); b_CT = Buf("ct")
        XQ = [sb("xq%d" % i, [128, QL], BF16) for i in range(2)]; b_XQ = [Buf("xq0"), Buf("xq1")]
        CS = sb("cs", [128, 512], F32); b_CS = Buf("cs")
        SN = sb("sn", [128, 512], F32); b_SN = Buf("sn")

        def load_rope(tt):
            t0 = tt * TW
            S_.dma("sync", CS[:, :], COS[:, t0:t0 + TW], b_CS, reads=[db("rope", tt)], writes=[b_CS])
            S_.dma("sync", SN[:, :], SIN[:, t0:t0 + TW], b_SN, reads=[db("rope", tt)], writes=[b_SN])

        def rope_evac(bankA, b_A, bankB, b_B, out_ap, b_out):
            t1, b_t1 = next_evf()
            t2, b_t2 = next_evf()
            S_.op("vector", "tensor_tensor", dict(out=t1[:, :], in0=bankA[:, :], in1=CS[:, :], op=ALU.mult), reads=[b_A, b_CS], writes=[b_t1])
            S_.op("vector", "tensor_tensor", dict(out=t2[:, :], in0=bankB[:, :], in1=SN[:, :], op=ALU.mult), reads=[b_B, b_SN], writes=[b_t2])
            S_.op("gpsimd", "tensor_tensor", dict(out=out_ap, in0=t1[:, :], in1=t2[:, :], op=ALU.add), reads=[b_t1, b_t2], writes=[b_out])

        def latent_norm_T(bank, b_bank, blk, gq_loaded):
            j = blk % 2
            rstd_of(bank[:, :], QL, b_bank, j, XQ[j][:, :], b_XQ[j])
            S_.op("vector", "scalar_tensor_tensor", dict(out=XQ[j][:, :], in0=bank[:, :], scalar=RS[j][:, 0:1], in1=GQ[:, :],
                                                             op0=ALU.mult, op1=ALU.mult),
                  reads=[b_bank, b_RS[j], b_GQ], writes=[b_XQ[j]])
            transpose_into(XQ[j], b_XQ[j], 4, CT, b_CT, blk * 128)

        def rope_weight_tiles(w2d, rd, col_of_pairchunk):
            raise NotImplementedError

        def phase_kv(hsrc, hsrc_name):
            load_gain(kv_norm_g[0])
            S_.dma("sync", GQ[:, :], kv_latent_g[0].partition_broadcast(128), b_GQ, writes=[b_GQ])
            wd = wb["w_dkv"]
            rd = [wbuf[("w_dkv", None)]]
            wu = wb["w_ukv"]
            rdu = [wbuf[("w_ukv", None)]]
            for tt in range(NT):
                t0 = tt * TW
                norm_tile(hsrc, t0, TW, [db(hsrc_name, tt)], ATv, b_AT)
                load_rope(tt)
                wv, b_w = load_w(wsrc(wd, 0, 512), KC, 512, rd)

                def evac_lat(blk, bank, b_bank):
                    latent_norm_T(bank, b_bank, blk, True)
                lin_as(wv, b_w, KC, 512, ATv, b_AT, TW // 128, evac_lat)
                i = wctr[0] % NW
                wctr[0] += 1
                wr = WS[i][:, 0:KC * 256].rearrange("p (k n) -> p k n", n=256)
                for q, (dst0, src0) in enumerate([(0, 512), (64, 512), (128, 544), (160, 512), (192, 544), (224, 512)]):
                    n = 64 if q < 2 else 32
                    S_.dma("sync", wr[:, :, dst0:dst0 + n], wsrc(wd, src0, n), b_WS[i], reads=rd, writes=[b_WS[i]] if q == 0 else [], chain=False)
                b_WS[i].w = b_WS[i].dlast
                bA, b_bA = next_pb()
                bB, b_bB = next_pb()
                for (off, bank_, b_bank_) in ((0, bA, b_bA), (128, bB, b_bB)):
                    for k in range(KC):
                        l_ap = wr[:, k, off:off + 128]
                        r_ap = ATv[:, k, :]
                        S_.op("tensor", "matmul", dict(out=bank_[:, :], lhsT=l_ap, rhs=r_ap, start=(k == 0), stop=(k == KC - 1)),
                              reads=[b_WS[i], b_AT], writes=[b_bank_], signal=(k == KC - 1))
                ob, b_ob = OB[tt % 2], b_OB[tt % 2]
                rope_evac(bA, b_bA, bB, b_bB, ob[:, :], b_ob)
                S_.dma(STQ, KR[:, t0:t0 + TW], ob[:, :], b_ob, reads=[b_ob], writes=[db("KR", tt)])
                for g in range(3):
                    i2 = wctr[0] % NW
                    wctr[0] += 1
                    wn = WS[i2][:, 0:4 * 512].rearrange("p (k n) -> p k n", n=512)
                    for hh in range(4):
                        h = g * 4 + hh
                        S_.dma("sync", wn[:, :, hh * 128:(hh + 1) * 128], wsrc(wu, h * 256, 128, kc=4), b_WS[i2], reads=rdu,
                               writes=[b_WS[i2]] if hh == 0 else [], chain=False)
                    b_WS[i2].w = b_WS[i2].dlast
                    evb, b_evb = next_evb()

                    def evac(c, bank, b_bank, evb=evb, b_evb=b_evb):
                        copy_alt(evb[:, c, :], bank[:, :], [b_bank], [b_evb])
                    lin_ws(wn, b_WS[i2], 4, 4, CT, b_CT, TW, evac)
                    S_.dma(STQ, KN[g * 4:(g + 1) * 4, :, t0:t0 + TW].rearrange("h p t -> p h t"), evb[:], b_evb, reads=[b_evb], writes=[db("KN%d" % g, tt)])
                for g in range(3):
                    i2 = wctr[0] % NW
                    wctr[0] += 1
                    wn = WS[i2][:, 0:4 * 512].rearrange("p (k n) -> p k n", n=512)
                    for hh in range(4):
                        h = g * 4 + hh
                        S_.dma("sync", wn[:, :, hh * 128:(hh + 1) * 128], wsrc(wu, h * 256 + 128, 128, kc=4), b_WS[i2], reads=rdu,
                               writes=[b_WS[i2]] if hh == 0 else [], chain=False)
                    b_WS[i2].w = b_WS[i2].dlast
                    evb, b_evb = next_evb()

                    def evac(blk, bank, b_bank, evb=evb, b_evb=b_evb):
                        copy_alt(evb[:, blk, :], bank[:, :], [b_bank], [b_evb])
                    lin_as(wn, b_WS[i2], 4, 512, CT, b_CT, TW // 128, evac)
                    S_.dma(STQ, VL[t0:t0 + TW, g * 512:(g + 1) * 512].rearrange("(b p) c -> p b c", p=128), evb[:], b_evb,
                           reads=[b_evb], writes=[db("VL%d" % g, tt)])

        def phase_p1b(l, hsrc, hsrc_name):
            i_b = l - na
            load_gain(attn_norm_g[l])
            S_.dma("sync", GQ[:, :], b_q_norm_g[i_b].partition_broadcast(128), b_GQ, writes=[b_GQ])
            w2d = wb["b_w_in"][i_b]
            rd = [wbuf[("b_w_in", i_b)]]
            wq = wb["b_w_uq"][i_b]
            rdq = [wbuf[("b_w_uq", i_b)]]
            for tt in range(NT):
                t0 = tt * TW
                norm_tile(hsrc, t0, TW, [db(hsrc_name, tt)], ATv, b_AT)
                load_rope(tt)
                wv, b_w = load_w(wsrc(w2d, 0, 512), KC, 512, rd)

                def evac_lat(blk, bank, b_bank):
                    latent_norm_T(bank, b_bank, blk, True)
                lin_as(wv, b_w, KC, 512, ATv, b_AT, TW // 128, evac_lat)
                wv, b_w = load_w(wsrc(w2d, 512, 512), KC, 512, rd)
                evb, b_evb = next_evb()

                def evac(c, bank, b_bank, evb=evb, b_evb=b_evb):
                    copy_alt(evb[:, c, :], bank[:, :], [b_bank], [b_evb])
                lin_ws(wv, b_w, KC, 4, ATv, b_AT, TW, evac)
                S_.dma(STQ, QM[:, :, t0:t0 + TW].rearrange("h p t -> p h t"), evb[:], b_evb, reads=[b_evb], writes=[db("QMB", tt)])
                for g in range(3):
                    i2 = wctr[0] % NW
                    wctr[0] += 1
                    wn = WS[i2][:, 0:4 * 512].rearrange("p (k n) -> p k n", n=512)
                    for hh in range(4):
                        h = g * 4 + hh
                        S_.dma("sync", wn[:, :, hh * 128:(hh + 1) * 128], wsrc(wq, h * 192, 128, kc=4), b_WS[i2], reads=rdq,
                               writes=[b_WS[i2]] if hh == 0 else [], chain=False)
                    b_WS[i2].w = b_WS[i2].dlast
                    evb, b_evb = next_evb()

                    def evac(c, bank, b_bank, evb=evb, b_evb=b_evb):
                        copy_alt(evb[:, c, :], bank[:, :], [b_bank], [b_evb])
                    lin_ws(wn, b_WS[i2], 4, 4, CT, b_CT, TW, evac)
                    S_.dma(STQ, QA[g * 4:(g + 1) * 4, :, t0:t0 + TW].rearrange("h p t -> p h t"), evb[:], b_evb, reads=[b_evb], writes=[db("QA%d" % g, tt)])
                for pr in range(6):
                    i2 = wctr[0] % NW
                    wctr[0] += 1
                    wr = WS[i2][:, 0:4 * 256].rearrange("p (k n) -> p k n", n=256)
                    first = True
                    for hh in range(2):
                        h = pr * 2 + hh
                        cb = h * 192 + 128
                        for (dst0, src0, n) in ((hh * 64, cb, 64), (128 + hh * 64, cb + 32, 32), (128 + hh * 64 + 32, cb, 32)):
                            S_.dma("sync", wr[:, :, dst0:dst0 + n], wsrc(wq, src0, n, kc=4), b_WS[i2], reads=rdq,
                                   writes=[b_WS[i2]] if first else [], chain=False)
                            first = False
                    b_WS[i2].w = b_WS[i2].dlast
                    bA, b_bA = next_pb()
                    bB, b_bB = next_pb()
                    for (off, bank_, b_bank_) in ((0, bA, b_bA), (128, bB, b_bB)):
                        for k in range(4):
                            l_ap = wr[:, k, off:off + 128]
                            r_ap = CT[:, k, :]
                            S_.op("tensor", "matmul", dict(out=bank_[:, :], lhsT=l_ap, rhs=r_ap, start=(k == 0), stop=(k == 3)),
                                  reads=[b_WS[i2], b_CT], writes=[b_bank_], signal=(k == 3))
                    ob, b_ob = OB[pr % 2], b_OB[pr % 2]
                    rope_evac(bA, b_bA, bB, b_bB, ob[:, :], b_ob)
                    S_.dma(STQ, QR[pr, :, t0:t0 + TW], ob[:, :], b_ob, reads=[b_ob], writes=[db("QR%d" % pr, tt)])

        def phase_p2b(l):
            att_begin()
            heads = [(s, h) for s in range(NS) for h in range(SBH)]

            def load_head(hix):
                s, h = heads[hix]
                j = hix % 2
                c0 = s * S
                if h == 0:
                    S_.dma("sync", KR_[s % 2], KR[:, c0:c0 + S], b_KR[s % 2], reads=seq_reads(["KR"], s), writes=[b_KR[s % 2]])
                S_.dma("sync", QT_[j], QA[h, :, c0:c0 + S], b_Q[j], reads=seq_reads(["QA%d" % (h // 4)], s), writes=[b_Q[j]])
                S_.dma("sync", QR_[j], QR[h // 2, :, c0:c0 + S], b_QR[j], reads=seq_reads(["QR%d" % (h // 2)], s), writes=[b_QR[j]])
                S_.dma("sync", KT_[j], KN[h, :, c0:c0 + S], b_K[j], reads=seq_reads(["KN%d" % (h // 4)], s), writes=[b_K[j]])
                S_.dma("sync", VV_[j].rearrange("p (b d) -> p b d", d=128),
                       VL[c0:c0 + S, h * 128:(h + 1) * 128].rearrange("(b p) d -> p b d", p=128), b_V[j],
                       reads=seq_reads(["VL%d" % (h // 4)], s), writes=[b_V[j]])

            def tile_jobs(qt):
                r = []
                for kb in range((qt * TW + TW) // 128):
                    jj = kb - qt * 4
                    r.append((kb, max(0, jj) * 128, jj >= 0))
                return r

            def z_mms(hd, j, qt, kb, cs, Z, b_Z):
                s, h = hd
                q0 = qt * TW
                pb0 = (h % 2) * 64
                S_.op("tensor", "matmul", dict(out=Z[:, cs:TW], lhsT=KT_[j][:, kb * 128:(kb + 1) * 128], rhs=QT_[j][:, q0 + cs:q0 + TW],
                                               start=True, stop=False),
                      reads=[b_K[j], b_Q[j]], writes=[b_Z], signal=False)
                S_.op("tensor", "matmul", dict(out=Z[:, cs:TW], lhsT=KR_[s % 2][pb0:pb0 + 64, kb * 128:(kb + 1) * 128],
                                               rhs=QR_[j][pb0:pb0 + 64, q0 + cs:q0 + TW], start=False, stop=True),
                      reads=[b_KR[s % 2], b_QR[j]], writes=[b_Z])

            def v_of(j, kb):
                return VV_[j][:, kb * 128:(kb + 1) * 128], b_V[j]

            def out_rows(hd, qt):
                s, h = hd
                return h * 128, s * S + qt * TW, "MIXT%d" % h, s * TPS + qt
            softmax_heads(heads, load_head, tile_jobs, z_mms, v_of, mask_in, 192 ** -0.5, out_rows)
            att_end()

        def phase_final(hsrc, hsrc_name):
            load_gain(final_norm_g[0])
            for tt in range(NT):
                for blk in range(TW // 128):
                    j = hb_ctr[0] % 2
                    hb_ctr[0] += 1
                    r0 = tt * TW + blk * 128
                    S_.dma("sync", HB[j][:], hsrc[r0:r0 + 128, :], b_HB[j], reads=[db(hsrc_name, tt)], writes=[b_HB[j]])
                    rstd_of(HB[j][:], D, b_HB[j], j, XN[j][:], b_XN[j])
                    S_.op("vector", "scalar_tensor_tensor", dict(out=HB[j][:], in0=HB[j][:], scalar=RS[j][:, 0:1], in1=GB[:],
                                                                         op0=ALU.mult, op1=ALU.mult),
                          reads=[b_HB[j], b_RS[j], b_GB], writes=[b_HB[j]])
                    S_.dma(STQ, out[r0:r0 + 128, :], HB[j][:], b_HB[j], reads=[b_HB[j]], writes=[db("out", tt * 4 + blk)])

        def want(name):
            return (only is None) or (name in only)
        def early_phases():
            if depth > na and want("rope"):
                phase_rope()
            if want("mem"):
                phase_mem()
        if na == 0:
            early_phases()
        hsrc, hname = x, "x"
        for l in range(depth):
            if l == na and want("kv"):
                phase_kv(hsrc, hname)
            if l < na:
                if want("p1_%d" % l):
                    phase_p1a(l, hsrc, hname)
                if l == 0:
                    early_phases()
                if lazy_cast and l + 1 < depth:
                    cast_layer(l + 1)
                if want("p2_%d" % l):
                    phase_p2a(l)
                wout2d, wrd = wb["a_w_out"][l], [wbuf[("a_w_out", l)]]
            else:
                if want("p1_%d" % l):
                    phase_p1b(l, hsrc, hname)
                if lazy_cast and l + 1 < depth:
                    cast_layer(l + 1)
                if want("p2_%d" % l):
                    phase_p2b(l)
                wout2d, wrd = wb["b_w_out"][l - na], [wbuf[("b_w_out", l - na)]]
            if want("ma_%d" % l):
                phase_mem_attn(l)
            if want("p3_%d" % l):
                phase_p3(l, hsrc, hname, wout2d, wrd)
                hsrc, hname = Hs, "Hs"
        if want("final"):
            phase_final(hsrc, hname)
        allb = list(dbufs.values()) + list(wbuf.values())
        S_.final_wait("sync", allb)
        S_.run()
    return nc


_INPUT_ORDER = ["attn_norm_g", "ffn_norm_g", "a_w_in", "a_w_out", "b_w_in", "b_q_norm_g", "b_w_uq", "b_w_out",
                "mem_norm_g", "w_mem_kv", "kv_norm_g", "w_dkv", "kv_latent_g", "w_ukv", "ffn_w_gu", "ffn_w_down",
                "final_norm_g"]


def make_in_maps(inputs, n_cores, ns):
    cst, rc = host_consts()
    shared = {}
    for k in _INPUT_ORDER:
        a = np.ascontiguousarray(np.asarray(inputs[k], dtype=np.float32))
        if a.ndim == 1:
            a = a.reshape(1, -1)
        shared[k] = a
    shared["cst"] = cst
    shared["rc"] = rc
    x = np.asarray(inputs["x"], dtype=np.float32)
    mem = np.asarray(inputs["mem"], dtype=np.float32)
    pos = np.asarray(inputs["positions"], dtype=np.int32)
    maps = []
    for c in range(n_cores):
        m = dict(shared)
        m["x"] = np.ascontiguousarray(x[c * ns:(c + 1) * ns].reshape(-1, D))
        m["mem"] = np.ascontiguousarray(mem[c * ns:(c + 1) * ns].reshape(-1, D))
        m["positions"] = np.ascontiguousarray(pos[c * ns:(c + 1) * ns].reshape(-1))
        maps.append(m)
    return maps


def kernel(**inputs):
    x = np.asarray(inputs["x"])
    B, S, _ = x.shape
    ML = np.asarray(inputs["mem"]).shape[1]
    ns = B // N_CORES
    nc = build(ns, S, ML)
    in_maps = make_in_maps(inputs, N_CORES, ns)
    res = run_bass_kernel_spmd(nc, in_maps, core_ids=list(range(N_CORES)))
    outs = [np.asarray(r["out"]).reshape(ns, S, D) for r in res.results]
    return np.concatenate(outs, axis=0).astype(np.float32)
```

```python
import contextlib
import math
import numpy as np
import ml_dtypes
import concourse.bass as bass
import concourse.mybir as mybir
from concourse.bass_utils import run_bass_kernel_spmd

F32 = mybir.dt.float32
F32R = mybir.dt.float32r
BF16 = mybir.dt.bfloat16
I32 = mybir.dt.int32
AF = mybir.ActivationFunctionType
ALU = mybir.AluOpType

D = 2048
KC = 16
HD = 128
SBH = 12
MH = 4
FF = 5632
FC = 44
QL = 512
DEPTH = 4
NA = 2
EPS = 1e-6
N_CORES = 8

SAME_ENGINE_SYNC = True


class Buf:
    __slots__ = ("name", "w", "r", "dsem", "dcount", "dlast")

    def __init__(self, name=""):
        self.name = name
        self.w = None
        self.r = []
        self.dsem = None
        self.dcount = 0
        self.dlast = None


class Eng:
    def __init__(self, name, sem, inorder_skip=False):
        self.name = name
        self.sem = sem
        self.count = 0
        self.ops = []
        self.waited = {}
        self.inorder_skip = inorder_skip


class Sched:
    def __init__(self, nc, stack):
        self.nc = nc
        self.stack = stack
        self.nsem = 0
        self.E = {}
        for n in ("tensor", "vector", "scalar", "gpsimd", "sync"):
            self.E[n] = Eng(n, self.new_sem("e_" + n), inorder_skip=(n == "tensor"))

    def new_sem(self, name):
        self.nsem += 1
        assert self.nsem < 230, "too many semaphores"
        return self.stack.enter_context(self.nc.semaphore(name))

    def _deps(self, eng, reads, writes):
        deps = []
        for b in reads:
            if b.w is not None:
                deps.append(b.w)
        for b in writes:
            deps.extend(b.r)
            if b.w is not None:
                deps.append(b.w)
        need = {}
        for (s, v) in deps:
            if s is eng.sem and (eng.inorder_skip or not SAME_ENGINE_SYNC):
                continue
            k = id(s)
            if eng.waited.get(k, 0) >= v:
                continue
            if k not in need or need[k][1] < v:
                need[k] = (s, v)
        for k, (s, v) in need.items():
            eng.waited[k] = v
        return list(need.values())

    def op(self, engname, method, kwargs, reads=(), writes=(), signal=True):
        eng = self.E[engname]
        waits = self._deps(eng, reads, writes)
        if signal:
            eng.count += 1
            tok = (eng.sem, eng.count)
        else:
            tok = (eng.sem, eng.count + 1)
        sem = eng.sem

        def emit(e, waits=waits, method=method, kwargs=kwargs, signal=signal, sem=sem):
            for (s, v) in waits:
                e.wait_ge(s, v)
            ins = getattr(e, method)(**kwargs)
            if signal:
                ins.then_inc(sem, 1)
        eng.ops.append(emit)
        for b in reads:
            b.r.append(tok)
        for b in writes:
            b.w = tok
            b.r = []
        return tok

    def dma(self, engname, out_ap, in_ap, sembuf, reads=(), writes=(), chain=True):
        eng = self.E[engname]
        if sembuf.dsem is None:
            sembuf.dsem = self.new_sem("d_" + sembuf.name)
        waits = self._deps(eng, reads, writes)
        if chain and sembuf.dlast is not None:
            s, v = sembuf.dlast
            if eng.waited.get(id(s), 0) < v:
                eng.waited[id(s)] = v
                waits.append((s, v))
        sembuf.dcount += 16
        tok = (sembuf.dsem, sembuf.dcount)
        sembuf.dlast = tok
        dsem = sembuf.dsem

        def emit(e, waits=waits, dsem=dsem, out_ap=out_ap, in_ap=in_ap):
            for (s, v) in waits:
                e.wait_ge(s, v)
            e.dma_start(out=out_ap, in_=in_ap).then_inc(dsem, 16)
        eng.ops.append(emit)
        for b in reads:
            b.r.append(tok)
        for b in writes:
            b.w = tok
            b.r = []
        return tok

    def fence(self, old_bufs, new_bufs):
        best = {}
        for b in old_bufs:
            for t in list(b.r) + ([b.w] if b.w is not None else []):
                k = id(t[0])
                if k not in best or best[k][1] < t[1]:
                    best[k] = t
        for b in new_bufs:
            mine = dict(best)
            for t in b.r:
                k = id(t[0])
                if k not in mine or mine[k][1] < t[1]:
                    mine[k] = t
            b.r = list(mine.values())

    def final_wait(self, engname, bufs):
        eng = self.E[engname]
        toks = []
        for b in bufs:
            if b.w is not None:
                toks.append(b.w)

        def emit(e, toks=toks):
            for (s, v) in toks:
                e.wait_ge(s, v)
        eng.ops.append(emit)

    def run(self):
        nc = self.nc
        E = self.E
        with nc.Block() as block:
            @block.sync
            def _(e):
                for f in E["sync"].ops:
                    f(e)

            @block.scalar
            def _(e):
                for f in E["scalar"].ops:
                    f(e)

            @block.vector
            def _(e):
                for f in E["vector"].ops:
                    f(e)

            @block.gpsimd
            def _(e):
                for f in E["gpsimd"].ops:
                    f(e)

            @block.tensor
            def _(e):
                for f in E["tensor"].ops:
                    f(e)


def host_consts():
    p = np.arange(128)
    cst = np.zeros((128, 7, 128), np.float32)
    cst[:, 0, :] = np.eye(128)
    cst[:, 1, :] = (p[:, None] > p[None, :])
    cst[:, 2, :] = (p[:, None] <= p[None, :])
    cst[:, 3, :] = 1.0
    cst[:, 4, :] = 0.0
    cst[:, 5, :] = (p[:, None] < p[None, :])
    cst[:, 6, :] = (p[:, None] <= p[None, :])
    half = 32
    inv_freq = (10000.0 ** (-np.arange(half, dtype=np.float32) / half)).astype(np.float32)
    rc = np.zeros((128, 4), np.float32)
    rc[:, 0] = inv_freq[p % 32]
    rc[:, 1] = np.where((p % 64) < 32, -1.0, 1.0)
    return cst.astype(ml_dtypes.bfloat16), rc


def build(NS, S, ML, depth=DEPTH, dbg=None, only=None, cast_filter=None):
    T = NS * S
    TW = 512
    assert S % TW == 0
    NT = T // TW
    TPS = S // TW
    NBS = S // 128
    MT = NS * ML
    assert MT <= 512 and MT % 128 == 0
    na = depth // 2

    nc = bass.Bass("TRN2", target_bir_lowering=False)

    def din(name, shape, dt=F32):
        return nc.dram_tensor(name, list(shape), dt, kind="ExternalInput").ap()

    def dscr(name, shape, dt):
        kind = "ExternalOutput" if (dbg and not name.startswith("wb_")) else "Internal"
        return nc.dram_tensor(name, list(shape), dt, kind=kind).ap()

    x = din("x", [T, D])
    mem = din("mem", [MT, D])
    pos = din("positions", [T], I32)
    attn_norm_g = din("attn_norm_g", [DEPTH, D])
    ffn_norm_g = din("ffn_norm_g", [DEPTH, D])
    a_w_in = din("a_w_in", [NA, D, 5120])
    a_w_out = din("a_w_out", [NA, D, D])
    b_w_in = din("b_w_in", [NA, D, 1024])
    b_q_norm_g = din("b_q_norm_g", [NA, QL])
    b_w_uq = din("b_w_uq", [NA, QL, 2304])
    b_w_out = din("b_w_out", [NA, D, D])
    mem_norm_g = din("mem_norm_g", [1, D])
    w_mem_kv = din("w_mem_kv", [DEPTH, D, 1024])
    kv_norm_g = din("kv_norm_g", [1, D])
    w_dkv = din("w_dkv", [D, 576])
    kv_latent_g = din("kv_latent_g", [1, QL])
    w_ukv = din("w_ukv", [QL, 3072])
    ffn_w_gu = din("ffn_w_gu", [DEPTH, D, 2 * FF])
    ffn_w_down = din("ffn_w_down", [DEPTH, FF, D])
    final_norm_g = din("final_norm_g", [1, D])
    cst_d = din("cst", [128, 7, 128], BF16)
    rc_d = din("rc", [128, 4], F32)
    out = nc.dram_tensor("out", [T, D], F32, kind="ExternalOutput").ap()

    wb = {}
    wb["a_w_in"] = dscr("wb_a_w_in", [NA, D, 5120], BF16)
    wb["a_w_out"] = dscr("wb_a_w_out", [NA, D, D], BF16)
    wb["b_w_in"] = dscr("wb_b_w_in", [NA, D, 1024], BF16)
    wb["b_w_uq"] = dscr("wb_b_w_uq", [NA, QL, 2304], BF16)
    wb["b_w_out"] = dscr("wb_b_w_out", [NA, D, D], BF16)
    wb["w_mem_kv"] = dscr("wb_w_mem_kv", [DEPTH, D, 1024], BF16)
    wb["w_dkv"] = dscr("wb_w_dkv", [D, 576], BF16)
    wb["w_ukv"] = dscr("wb_w_ukv", [QL, 3072], BF16)
    wb["ffn_w_gu"] = dscr("wb_ffn_w_gu", [DEPTH, D, 2 * FF], BF16)
    wb["ffn_w_down"] = dscr("wb_ffn_w_down", [DEPTH, FF, D], BF16)

    Hs = dscr("Hs", [T, D], F32)
    H1 = dscr("H1", [T, D], F32)
    QA = dscr("QA", [SBH, 128, T], BF16)
    KA = dscr("KA", [SBH, 128, T], BF16)
    VA = dscr("VA", [T, SBH * 128], BF16)
    KN = dscr("KN", [SBH, 128, T], BF16)
    VL = dscr("VL", [T, SBH * 128], BF16)
    KR = dscr("KR", [128, T], BF16)
    QR = dscr("QR", [6, 128, T], BF16)
    QM = dscr("QM", [MH, 128, T], BF16)
    MK = dscr("MK", [DEPTH, MH, 128, MT], BF16)
    MV = dscr("MV", [DEPTH, MT, MH * 128], BF16)
    MIXT = dscr("MIXT", [D, T], BF16)
    CQ = dscr("CQ", [QL, T], BF16)
    COS = dscr("COS", [128, T], F32)
    SIN = dscr("SIN", [128, T], F32)

    with contextlib.ExitStack() as st:
        S_ = Sched(nc, st)

        def sb(name, shape, dt):
            return st.enter_context(nc.sbuf_tensor(name, list(shape), dt))

        def ps(name, shape, dt):
            return st.enter_context(nc.psum_tensor(name, list(shape), dt))

        cst = sb("cst_sb", [128, 7, 128], BF16); b_cst = Buf("cst")
        rc = sb("rc_sb", [128, 4], F32); b_rc = Buf("rc")
        NW = 4
        WS = [sb("ws%d" % i, [128, 8192], BF16) for i in range(NW)]
        b_WS = [Buf("ws%d" % i) for i in range(NW)]
        AT = sb("AT", [128, 8192], BF16); b_AT = Buf("AT")
        ARENA = sb("arena", [128, 22528], BF16)
        b_FA = Buf("FA")
        HB = [sb("hb%d" % i, [128, D], F32) for i in range(2)]
        b_HB = [Buf("hb%d" % i) for i in range(2)]
        XN = [sb("xn%d" % i, [128, D], BF16) for i in range(2)]
        b_XN = [Buf("xn%d" % i) for i in range(2)]
        GB = sb("gb", [128, D], F32); b_GB = Buf("gb")
        GQ = sb("gq", [128, QL], F32); b_GQ = Buf("gq")
        SS = [sb("ss%d" % i, [128, 1], F32) for i in range(2)]
        b_SS = [Buf("ss%d" % i) for i in range(2)]
        RS = [sb("rs%d" % i, [128, 1], F32) for i in range(2)]
        b_RS = [Buf("rs%d" % i) for i in range(2)]
        EVF = [sb("evf%d" % i, [128, 512], F32) for i in range(4)]
        b_EVF = [Buf("evf%d" % i) for i in range(4)]
        EVH = [sb("evh%d" % i, [128, 512], F32) for i in range(2)]
        b_EVH = [Buf("evh%d" % i) for i in range(2)]
        EVB = [sb("evb%d" % i, [128, 4, 512], BF16) for i in range(2)]
        b_EVB = [Buf("evb%d" % i) for i in range(2)]

        NPB = 6
        PB = [ps("pb%d" % i, [128, 512], F32) for i in range(NPB)]
        b_PB = [Buf("pb%d" % i) for i in range(NPB)]
        PTS = [ps("pt%d" % i, [128, 8, 128], BF16) for i in range(2)]
        b_PT = [Buf("pt0"), Buf("pt1")]
        ptctr = [0]

        ident = cst[:, 0, :]
        tri_lo = cst[:, 1, :]
        tri_up = cst[:, 2, :]
        ones_m = cst[:, 3, :]
        zeros_m = cst[:, 4, :]
        mask_st = cst[:, 5, :]
        mask_in = cst[:, 6, :]

        S_.dma("sync", cst[:], cst_d[:, :, :], b_cst, writes=[b_cst])
        S_.dma("sync", rc[:], rc_d[:, :], b_rc, writes=[b_rc])

        wbuf = {}

        cast_sink = [None]
        pending_casts = []

        def cast(name, idx, src2d, dst2d, rows, rchunk):
            b = Buf("c_%s%s" % (name, "" if idx is None else str(idx)))
            wbuf[(name, idx)] = b
            if cast_filter is not None and name not in cast_filter:
                return
            if cast_sink[0] is not None:
                rchunk = max(128, rchunk // 2)
            r = 0
            while r < rows:
                r1 = min(rows, r + rchunk)

                def th(r=r, r1=r1, b=b):
                    S_.dma("gpsimd", dst2d[r:r1, :], src2d[r:r1, :], b, writes=[b], chain=False)
                if cast_sink[0] is None:
                    th()
                else:
                    cast_sink[0].append(th)
                r = r1

        def trickle(n=1):
            for _ in range(n):
                if pending_casts:
                    pending_casts.pop(0)()

        def cast_layer(l):
            if l < na:
                if l > 0:
                    cast("a_w_in", l, a_w_in[l], wb["a_w_in"][l], D, 512)
                cast("a_w_out", l, a_w_out[l], wb["a_w_out"][l], D, 1024)
            else:
                i = l - na
                if i == 0:
                    cast("w_dkv", None, w_dkv, wb["w_dkv"], D, 2048)
                    cast("w_ukv", None, w_ukv, wb["w_ukv"], QL, 512)
                cast("b_w_in", i, b_w_in[i], wb["b_w_in"][i], D, 2048)
                cast("b_w_uq", i, b_w_uq[i], wb["b_w_uq"][i], QL, 512)
                cast("b_w_out", i, b_w_out[i], wb["b_w_out"][i], D, 1024)
            cast("ffn_w_gu", l, ffn_w_gu[l], wb["ffn_w_gu"][l], D, 256)
            cast("ffn_w_down", l, ffn_w_down[l], wb["ffn_w_down"][l], FF, 1408)

        if na > 0:
            cast("a_w_in", 0, a_w_in[0], wb["a_w_in"][0], D, 512)
        for l in range(depth):
            cast("w_mem_kv", l, w_mem_kv[l], wb["w_mem_kv"][l], D, 2048)
        lazy_cast = (only is None)
        cast_layer(0)
        if not lazy_cast:
            for l in range(1, depth):
                cast_layer(l)

        STQ = "gpsimd"
        wctr = [0]

        def wslot_view(i, kc, n):
            return WS[i][:, 0:kc * n].rearrange("p (k n) -> p k n", n=n)

        def load_w(src3, kc, n, reads):
            i = wctr[0] % NW
            wctr[0] += 1
            v = wslot_view(i, kc, n)
            S_.dma("sync", v, src3, b_WS[i], reads=reads, writes=[b_WS[i]])
            return v, b_WS[i]

        def wsrc(w2d, c0, n, r0=0, kc=KC):
            return w2d[r0:r0 + kc * 128, c0:c0 + n].rearrange("(k p) n -> p k n", p=128)

        pbctr = [0]

        def next_pb():
            i = pbctr[0] % NPB
            pbctr[0] += 1
            return PB[i], b_PB[i]

        evf_ctr = [0]

        def next_evf():
            i = evf_ctr[0] % 4
            evf_ctr[0] += 1
            return EVF[i], b_EVF[i]

        evh_ctr = [0]

        def next_evh():
            i = evh_ctr[0] % 2
            evh_ctr[0] += 1
            return EVH[i], b_EVH[i]

        evb_ctr = [0]

        def next_evb():
            i = evb_ctr[0] % 2
            evb_ctr[0] += 1
            return EVB[i], b_EVB[i]

        alt = [0]

        def copy_alt(out_ap, in_ap, reads, writes):
            alt[0] ^= 1
            if alt[0]:
                S_.op("scalar", "activation", dict(out=out_ap, in_=in_ap, func=AF.Copy), reads=reads, writes=writes)
            else:
                S_.op("vector", "tensor_copy", dict(out=out_ap, in_=in_ap), reads=reads, writes=writes)

        def rstd_of(src_ap, n, b_src, j, junk_ap, b_junk):
            S_.op("scalar", "activation", dict(out=junk_ap, in_=src_ap, func=AF.Square, accum_out=SS[j][:]),
                  reads=[b_src], writes=[b_junk, b_SS[j]])
            S_.op("scalar", "activation", dict(out=RS[j][:], in_=SS[j][:], func=AF.Ln, scale=1.0 / n, bias=EPS),
                  reads=[b_SS[j]], writes=[b_RS[j]])
            S_.op("scalar", "activation", dict(out=RS[j][:], in_=RS[j][:], func=AF.Exp, scale=-0.5),
                  reads=[b_RS[j]], writes=[b_RS[j]])

        hb_ctr = [0]

        def transpose_into(src_bf, b_src, nk, dstT, b_dst, col0, ncol=128):
            for k0 in range(0, nk, 8):
                h = ptctr[0] % 2
                ptctr[0] += 1
                PT = PTS[h]
                kk = min(8, nk - k0)
                for k in range(k0, k0 + kk):
                    pt_ap = PT[:, k - k0, 0:ncol]
                    in_ap = src_bf[0:ncol, k * 128:(k + 1) * 128]
                    S_.op("tensor", "transpose", dict(out=pt_ap, in_=in_ap, identity=ident[0:ncol, 0:ncol]),
                          reads=[b_src, b_cst], writes=[b_PT[h]], signal=(k == k0 + kk - 1))
                o_ap = dstT[:, k0:k0 + kk, col0:col0 + ncol]
                i_ap = PT[:, 0:kk, 0:ncol]
                S_.op("vector", "tensor_copy", dict(out=o_ap, in_=i_ap),
                      reads=[b_PT[h]], writes=[b_dst])

        def load_gain(g_row, n=D):
            S_.dma("sync", GB[:, 0:n], g_row.partition_broadcast(128), b_GB, writes=[b_GB])

        def norm_tile(src, row0, nrows, b_srcs, dstT, b_dst):
            for blk in range(nrows // 128):
                j = hb_ctr[0] % 2
                hb_ctr[0] += 1
                r0 = row0 + blk * 128
                S_.dma("sync", HB[j][:], src[r0:r0 + 128, :], b_HB[j], reads=b_srcs, writes=[b_HB[j]])
                rstd_of(HB[j][:], D, b_HB[j], j, XN[j][:], b_XN[j])
                S_.op("vector", "scalar_tensor_tensor", dict(out=XN[j][:], in0=HB[j][:], scalar=RS[j][:, 0:1], in1=GB[:],
                                                                     op0=ALU.mult, op1=ALU.mult),
                      reads=[b_HB[j], b_RS[j], b_GB], writes=[b_XN[j]])
                transpose_into(XN[j], b_XN[j], KC, dstT, b_dst, blk * 128)

        def lin_ws(wv, b_w, kc, nch, actT, b_act, ncols, evac, col_off=0, m=128):
            for c in range(nch):
                bank, b_bank = next_pb()
                for k in range(kc):
                    l_ap = wv[:, k, col_off + c * m:col_off + (c + 1) * m]
                    r_ap = actT[:, k, 0:ncols]
                    o_ap = bank[0:m, 0:ncols]
                    S_.op("tensor", "matmul", dict(out=o_ap, lhsT=l_ap, rhs=r_ap, start=(k == 0), stop=(k == kc - 1)),
                          reads=[b_w, b_act], writes=[b_bank], signal=(k == kc - 1))
                evac(c, bank, b_bank)

        def lin_as(wv, b_w, kc, n, actT, b_act, nblk, evac, k_off=0):
            for blk in range(nblk):
                bank, b_bank = next_pb()
                for k in range(kc):
                    l_ap = actT[:, k_off + k, blk * 128:(blk + 1) * 128]
                    r_ap = wv[:, k, 0:n]
                    o_ap = bank[:, 0:n]
                    S_.op("tensor", "matmul", dict(out=o_ap, lhsT=l_ap, rhs=r_ap, start=(k == 0), stop=(k == kc - 1)),
                          reads=[b_w, b_act], writes=[b_bank], signal=(k == kc - 1))
                evac(blk, bank, b_bank)

        dbufs = {}

        def db(name, i=0):
            k = (name, i)
            if k not in dbufs:
                dbufs[k] = Buf("%s_%s" % (name, i))
            return dbufs[k]

        ATv = AT[:].rearrange("p (k n) -> p k n", n=512)

        def phase_rope():
            for tt in range(NT):
                t0 = tt * TW
                pi_t, b_pi = next_evf()
                S_.dma("sync", pi_t[:].bitcast(I32), pos[t0:t0 + TW].partition_broadcast(128), b_pi, writes=[b_pi])
                ang, b_ang = next_evf()
                S_.op("vector", "tensor_copy", dict(out=ang[:], in_=pi_t[:].bitcast(I32)),
                      reads=[b_pi], writes=[b_ang])
                S_.op("vector", "tensor_scalar", dict(out=ang[:], in0=ang[:], scalar1=rc[:, 0:1], scalar2=None, op0=ALU.mult),
                      reads=[b_ang, b_rc], writes=[b_ang])
                for which, shift, dst in ((0, 0.0, SIN), (1, 0.5 * math.pi, COS)):
                    y, b_y = next_evf()
                    kf, b_kf = next_evf()
                    TWO_PI = 2.0 * math.pi
                    S_.op("vector", "tensor_scalar", dict(out=y[:], in0=ang[:], scalar1=shift, scalar2=None, op0=ALU.add),
                          reads=[b_ang], writes=[b_y])
                    S_.op("vector", "tensor_scalar", dict(out=kf[:], in0=y[:], scalar1=1.0 / TWO_PI, scalar2=None, op0=ALU.mult),
                          reads=[b_y], writes=[b_kf])
                    S_.op("vector", "tensor_copy", dict(out=kf[:].bitcast(I32), in_=kf[:]), reads=[b_kf], writes=[b_kf])
                    S_.op("vector", "tensor_copy", dict(out=kf[:], in_=kf[:].bitcast(I32)), reads=[b_kf], writes=[b_kf])
                    S_.op("vector", "scalar_tensor_tensor", dict(out=y[:], in0=kf[:], scalar=-TWO_PI, in1=y[:], op0=ALU.mult, op1=ALU.add),
                          reads=[b_kf, b_y], writes=[b_y])
                    S_.op("vector", "tensor_scalar", dict(out=kf[:], in0=y[:], scalar1=math.pi, scalar2=TWO_PI, op0=ALU.is_gt, op1=ALU.mult),
                          reads=[b_y], writes=[b_kf])
                    S_.op("vector", "tensor_tensor", dict(out=y[:], in0=y[:], in1=kf[:], op=ALU.subtract), reads=[b_y, b_kf], writes=[b_y])
                    S_.op("vector", "tensor_scalar", dict(out=y[:], in0=y[:], scalar1=3.1415925, scalar2=-3.1415925, op0=ALU.min, op1=ALU.max),
                          reads=[b_y], writes=[b_y])
                    S_.op("scalar", "activation", dict(out=y[:], in_=y[:], func=AF.Sin), reads=[b_y], writes=[b_y])
                    if which == 0:
                        S_.op("vector", "tensor_scalar", dict(out=y[:], in0=y[:], scalar1=rc[:, 1:2], scalar2=None, op0=ALU.mult),
                              reads=[b_y, b_rc], writes=[b_y])
                    S_.dma(STQ, dst[:, t0:t0 + TW], y[:], b_y, reads=[b_y], writes=[db("rope", tt)])

        def phase_mem():
            load_gain(mem_norm_g[0])
            norm_tile(mem, 0, MT, [], ATv, b_AT)
            for l in range(depth):
                wsrc2d = wb["w_mem_kv"][l]
                rd = [wbuf[("w_mem_kv", l)]]
                wk, b_wk = load_w(wsrc(wsrc2d, 0, 512), KC, 512, rd)
                evb, b_evb = next_evb()

                def evac_k(c, bank, b_bank, evb=evb, b_evb=b_evb):
                    copy_alt(evb[:, c, 0:MT], bank[:, 0:MT], [b_bank], [b_evb])
                lin_ws(wk, b_wk, KC, MH, ATv, b_AT, MT, evac_k)
                S_.dma(STQ, MK[l].rearrange("h p t -> p h t"), evb[:, :, 0:MT], b_evb, reads=[b_evb], writes=[db("MK", l)])
                wvv, b_wv = load_w(wsrc(wsrc2d, 512, 512), KC, 512, rd)
                evb2, b_evb2 = next_evb()

                def evac_v(blk, bank, b_bank, evb2=evb2, b_evb2=b_evb2):
                    copy_alt(evb2[:, blk, :], bank[:, :], [b_bank], [b_evb2])
                lin_as(wvv, b_wv, KC, 512, ATv, b_AT, MT // 128, evac_v)
                S_.dma(STQ, MV[l].rearrange("(b p) c -> p b c", p=128), evb2[:, 0:MT // 128, :], b_evb2, reads=[b_evb2], writes=[db("MV", l)])

        def phase_p1a(l, hsrc, hsrc_name):
            load_gain(attn_norm_g[l])
            w2d = wb["a_w_in"][l]
            rd = [wbuf[("a_w_in", l)]]
            for tt in range(NT):
                t0 = tt * TW
                norm_tile(hsrc, t0, TW, [db(hsrc_name, tt)], ATv, b_AT)
                for wt_i in list(range(6)) + [9]:
                    wv, b_w = load_w(wsrc(w2d, wt_i * 512, 512), KC, 512, rd)
                    evb, b_evb = next_evb()

                    def evac(c, bank, b_bank, evb=evb, b_evb=b_evb):
                        copy_alt(evb[:, c, :], bank[:, :], [b_bank], [b_evb])
                    lin_ws(wv, b_w, KC, 4, ATv, b_AT, TW, evac)
                    if wt_i < 3:
                        dst = QA[wt_i * 4:(wt_i + 1) * 4, :, t0:t0 + TW]; dn = "QA"
                    elif wt_i < 6:
                        dst = KA[(wt_i - 3) * 4:(wt_i - 2) * 4, :, t0:t0 + TW]; dn = "KA"
                    else:
                        dst = QM[:, :, t0:t0 + TW]; dn = "QM"
                    S_.dma(STQ, dst.rearrange("h p t -> p h t"), evb[:], b_evb, reads=[b_evb], writes=[db(dn + str(wt_i), tt)])
                for wt_i in range(6, 9):
                    wv, b_w = load_w(wsrc(w2d, wt_i * 512, 512), KC, 512, rd)
                    evb, b_evb = next_evb()

                    def evac(blk, bank, b_bank, evb=evb, b_evb=b_evb):
                        copy_alt(evb[:, blk, :], bank[:, :], [b_bank], [b_evb])
                    lin_as(wv, b_w, KC, 512, ATv, b_AT, TW // 128, evac)
                    c0 = (wt_i - 6) * 512
                    S_.dma(STQ, VA[t0:t0 + TW, c0:c0 + 512].rearrange("(b p) c -> p b c", p=128), evb[:], b_evb,
                           reads=[b_evb], writes=[db("VA" + str(wt_i), tt)])

        def seq_reads(names, s):
            r = []
            for n in names:
                for tt in range(s * TPS, (s + 1) * TPS):
                    r.append(db(n, tt))
            return r

        SM = 2048
        assert S <= SM

        def arena_view(slot, n=S):
            return ARENA[:, slot * SM:slot * SM + n]
        QT_ = [arena_view(i) for i in range(2)]
        KT_ = [arena_view(2 + i) for i in range(2)]
        VV_ = [arena_view(4 + i) for i in range(2)]
        QR_ = [arena_view(6 + i) for i in range(2)]
        KR_ = [arena_view(8), arena_view(9)]
        SPSR_t = [sb("spsr%d" % q, [128, 512], F32R) for q in range(2)]
        SPSR = [t[:, :] for t in SPSR_t]
        SPS = [t[:, :].bitcast(F32) for t in SPSR_t]
        b_SPS = [Buf("sps%d" % q) for q in range(2)]
        b_Q = [Buf("aq0"), Buf("aq1")]
        b_K = [Buf("ak0"), Buf("ak1")]
        b_V = [Buf("av0"), Buf("av1")]
        b_QR = [Buf("aqr0"), Buf("aqr1")]
        b_KR = [Buf("akr0"), Buf("akr1")]
        TE = [ARENA[:, 18432 + i * 1024:18432 + (i + 1) * 1024].bitcast(F32) for i in range(2)]; b_TE = [Buf("te0"), Buf("te1")]
        TSPR_t = [sb("tspr%d" % i, [128, 512], F32R) for i in range(2)]; b_TSP = [Buf("tsp0"), Buf("tsp1")]
        TSPR = [t[:, :] for t in TSPR_t]
        TSP = [t[:, :].bitcast(F32) for t in TSPR_t]
        TA2_t = sb("ta2", [128, 512], F32)
        TA = [ARENA[:, 20480 + i * 1024:20480 + (i + 1) * 1024].bitcast(F32) for i in range(2)] + [TA2_t[:, :]]
        b_TA = [Buf("ta0"), Buf("ta1"), Buf("ta2")]
        TW_ = [sb("tw%d" % i, [128, 512], BF16) for i in range(2)]; b_TW = [Buf("tw0"), Buf("tw1")]
        RD = sb("rden", [128, 512], F32); b_RD = Buf("rden")
        OB = [sb("ob%d" % i, [128, 512], BF16) for i in range(2)]; b_OB = [Buf("ob0"), Buf("ob1")]
        C32 = sb("c32", [128, 2, 128], F32R); b_C32 = Buf("c32")
        S_.op("vector", "tensor_copy", dict(out=C32[:, 0, :], in_=tri_lo), reads=[b_cst], writes=[b_C32])
        S_.op("vector", "tensor_copy", dict(out=C32[:, 1, :], in_=ones_m), reads=[b_cst], writes=[b_C32])
        TRI32R = C32[:, 0, :]
        ONES32R = C32[:, 1, :]
        att_bufs = b_Q + b_K + b_V + b_QR + b_KR + b_TE + b_TA

        def att_begin():
            S_.fence([b_FA] + att_bufs, att_bufs)

        def att_end():
            S_.fence(att_bufs, [b_FA])

        def phase_p2a(l):
            scale = 128 ** -0.5
            att_begin()
            heads = [(s, h) for s in range(NS) for h in range(SBH)]

            def load_head(hix):
                s, h = heads[hix]
                j = hix % 2
                c0 = s * S
                S_.dma("sync", QT_[j], QA[h, :, c0:c0 + S], b_Q[j], reads=seq_reads(["QA%d" % (h // 4)], s), writes=[b_Q[j]])
                S_.dma("sync", KT_[j], KA[h, :, c0:c0 + S], b_K[j], reads=seq_reads(["KA%d" % (3 + h // 4)], s), writes=[b_K[j]])
                S_.dma("sync", VV_[j].rearrange("p (b d) -> p b d", d=128),
                       VA[c0:c0 + S, h * 128:(h + 1) * 128].rearrange("(b p) d -> p b d", p=128), b_V[j],
                       reads=seq_reads(["VA%d" % (6 + h // 4)], s), writes=[b_V[j]])

            jobs = []
            tix = 0
            for hix in range(len(heads)):
                for qt in range(TPS):
                    nkb = (qt * TW + TW) // 128
                    NP = nkb
                    for n, kb in enumerate(range(nkb - 1, -1, -1)):
                        jj = kb - qt * 4
                        cs = max(0, jj) * 128
                        jobs.append((hix, qt, n, kb, cs, jj, NP, tix))
                    tix += 1
            NJ = len(jobs)
            CUMS = [(PB[2], b_PB[2]), (PB[5], b_PB[5])]
            load_head(0)
            if len(heads) > 1:
                load_head(1)

            def after_job(g):
                hix = jobs[g][0]
                if (g + 1 == NJ or jobs[g + 1][0] != hix) and hix + 2 < len(heads):
                    load_head(hix + 2)

            def stA(g):
                hix, qt, n, kb, cs, jj, NP, tix = jobs[g]
                j = hix % 2
                q0 = qt * TW
                if n == 0:
                    OO, b_OO = PB[3 + (tix % 2)], b_PB[3 + (tix % 2)]
                    S_.op("tensor", "matmul", dict(out=OO[:, :], lhsT=zeros_m, rhs=QT_[j][:, q0:q0 + TW], start=True, stop=False),
                          reads=[b_cst, b_Q[j]], writes=[b_OO], signal=False)
                Z, b_Z = PB[g % 2], b_PB[g % 2]
                S_.op("tensor", "matmul", dict(out=Z[:, cs:TW], lhsT=KT_[j][:, kb * 128:(kb + 1) * 128], rhs=QT_[j][:, q0 + cs:q0 + TW],
                                               start=True, stop=True),
                      reads=[b_K[j], b_Q[j]], writes=[b_Z])

            def stB(g):
                hix, qt, n, kb, cs, jj, NP, tix = jobs[g]
                m = g % 2
                Z, b_Z = PB[m], b_PB[m]
                S_.op("scalar", "activation", dict(out=TE[m][:, cs:TW], in_=Z[:, cs:TW], func=AF.Exp, scale=scale),
                      reads=[b_Z], writes=[b_TE[m]])
                S_.op("scalar", "activation", dict(out=TSPR[m][:, cs:TW], in_=TE[m][:, cs:TW], func=AF.Ln, bias=1.0, scale=1.0),
                      reads=[b_TE[m]], writes=[b_TSP[m]])
                S_.op("vector", "scalar_tensor_tensor", dict(out=TA[g % 3][:, cs:TW], in0=Z[:, cs:TW], scalar=scale, in1=TSP[m][:, cs:TW],
                                                             op0=ALU.mult, op1=ALU.subtract),
                      reads=[b_Z, b_TSP[m]], writes=[b_TA[g % 3]])
                if jj >= 0:
                    S_.op("gpsimd", "tensor_tensor", dict(out=TSPR[m][:, cs:cs + 128], in0=TSP[m][:, cs:cs + 128], in1=mask_st, op=ALU.mult),
                          reads=[b_TSP[m], b_cst, b_TA[g % 3]], writes=[b_TSP[m]])

            def stC(g):
                hix, qt, n, kb, cs, jj, NP, tix = jobs[g]
                m = g % 2
                CUM, b_CUM = CUMS[m]
                sps = SPS
                spsr = SPSR
                b_sps = b_SPS
                if n == 0:
                    for q in range(2):
                        S_.op("vector", "tensor_scalar", dict(out=SPSR[q], in0=cst[:, 3:7, :].rearrange("p a b -> p (a b)"), scalar1=0.0, scalar2=None, op0=ALU.mult),
                              reads=[b_cst], writes=[b_SPS[q]])
                S_.op("tensor", "matmul", dict(out=CUM[:, cs:TW], lhsT=TRI32R, rhs=TSPR[m][:, cs:TW], start=True, stop=(n == 0)),
                      reads=[b_C32, b_TSP[m]], writes=[b_CUM], signal=(n == 0))
                if n > 0:
                    S_.op("tensor", "matmul", dict(out=CUM[:, cs:TW], lhsT=ONES32R, rhs=spsr[n % 2][:, cs:TW], start=False, stop=True),
                          reads=[b_C32, b_sps[n % 2]], writes=[b_CUM])
                if n + 1 < NP:
                    S_.op("gpsimd", "tensor_tensor", dict(out=spsr[(n + 1) % 2][:, cs:TW], in0=sps[n % 2][:, cs:TW], in1=TSP[m][:, cs:TW], op=ALU.add),
                          reads=[b_sps[n % 2], b_TSP[m]], writes=[b_sps[(n + 1) % 2]])

            def stD(g):
                hix, qt, n, kb, cs, jj, NP, tix = jobs[g]
                m = g % 2
                CUM, b_CUM = CUMS[m]
                S_.op("vector", "tensor_tensor", dict(out=TA[g % 3][:, cs:TW], in0=TA[g % 3][:, cs:TW], in1=CUM[:, cs:TW], op=ALU.subtract),
                      reads=[b_TA[g % 3], b_CUM], writes=[b_TA[g % 3]])

            def stF(g):
                hix, qt, n, kb, cs, jj, NP, tix = jobs[g]
                m = g % 2
                S_.op("scalar", "activation", dict(out=TW_[m][:, cs:TW], in_=TA[g % 3][:, cs:TW], func=AF.Exp),
                      reads=[b_TA[g % 3]], writes=[b_TW[m]])
                if jj >= 0:
                    S_.op("gpsimd", "tensor_tensor", dict(out=TW_[m][:, cs:cs + 128], in0=TW_[m][:, cs:cs + 128], in1=mask_st, op=ALU.mult),
                          reads=[b_TW[m], b_cst], writes=[b_TW[m]])

            def stG(g):
                hix, qt, n, kb, cs, jj, NP, tix = jobs[g]
                m = g % 2
                j = hix % 2
                s, h = heads[hix]
                OO, b_OO = PB[3 + (tix % 2)], b_PB[3 + (tix % 2)]
                S_.op("tensor", "matmul", dict(out=OO[:, cs:TW], lhsT=VV_[j][:, kb * 128:(kb + 1) * 128], rhs=TW_[m][:, cs:TW],
                                               start=False, stop=(n == NP - 1)),
                      reads=[b_V[j], b_TW[m]], writes=[b_OO], signal=True)
                if n == NP - 1:
                    ob, b_ob = OB[tix % 2], b_OB[tix % 2]
                    copy_alt(ob[:, :], OO[:, :], [b_OO], [b_ob])
                    c0 = s * S + qt * TW
                    S_.dma("sync", MIXT[h * 128:(h + 1) * 128, c0:c0 + TW], ob[:, :], b_ob, reads=[b_ob],
                           writes=[db("MIXT%d" % h, s * TPS + qt)])

            for i in range(-2, NJ + 1):
                if 0 <= i + 2 < NJ:
                    stA(i + 2)
                    stB(i + 2)
                if 0 <= i + 1 < NJ:
                    stC(i + 1)
                    stD(i + 1)
                if 0 <= i - 1 < NJ:
                    stG(i - 1)
                    after_job(i - 1)
                if 0 <= i < NJ:
                    stF(i)
                if pending_casts and i >= 0 and i % 16 == 8:
                    trickle()
            att_end()

        def softmax_heads(heads, load_head, tile_jobs, z_mms, v_of, mask, scale, out_rows):
            jobs = []
            tix = 0
            for hix in range(len(heads)):
                for qt in range(TPS):
                    kbs = tile_jobs(qt)
                    for idx, (kb, cs, dg) in enumerate(kbs):
                        jobs.append((hix, qt, kb, cs, dg, idx == 0, idx == len(kbs) - 1, tix))
                    tix += 1
            NJ = len(jobs)
            DNS = [(PB[5], b_PB[5]), (PB[2], b_PB[2])]
            load_head(0)
            if len(heads) > 1:
                load_head(1)

            def after_job(g):
                hix = jobs[g][0]
                if (g + 1 == NJ or jobs[g + 1][0] != hix) and hix + 2 < len(heads):
                    load_head(hix + 2)

            def stZ(g):
                hix, qt, kb, cs, dg, first, last, tix = jobs[g]
                z_mms(heads[hix], hix % 2, qt, kb, cs, PB[g % 2], b_PB[g % 2])

            def stE(g):
                hix, qt, kb, cs, dg, first, last, tix = jobs[g]
                m = g % 2
                Z, b_Z = PB[m], b_PB[m]
                S_.op("scalar", "activation", dict(out=TW_[m][:, cs:TW], in_=Z[:, cs:TW], func=AF.Exp, scale=scale),
                      reads=[b_Z], writes=[b_TW[m]])
                if dg:
                    S_.op("gpsimd", "tensor_tensor", dict(out=TW_[m][:, cs:cs + 128], in0=TW_[m][:, cs:cs + 128], in1=mask, op=ALU.mult),
                          reads=[b_TW[m], b_cst], writes=[b_TW[m]])

            def stP(g):
                hix, qt, kb, cs, dg, first, last, tix = jobs[g]
                m = g % 2
                j = hix % 2
                OO, b_OO = PB[3 + (tix % 2)], b_PB[3 + (tix % 2)]
                DN, b_DN = DNS[tix % 2]
                v_ap, b_v = v_of(j, kb)
                S_.op("tensor", "matmul", dict(out=OO[:, cs:TW], lhsT=v_ap, rhs=TW_[m][:, cs:TW], start=first, stop=last),
                      reads=[b_v, b_TW[m]], writes=[b_OO], signal=False)
                S_.op("tensor", "matmul", dict(out=DN[:, cs:TW], lhsT=ones_m, rhs=TW_[m][:, cs:TW], start=first, stop=last),
                      reads=[b_cst, b_TW[m]], writes=[b_DN, b_OO], signal=True)
                if last:
                    S_.op("vector", "reciprocal", dict(out=RD[:, :], in_=DN[:, :]), reads=[b_DN], writes=[b_RD])
                    ob, b_ob = OB[tix % 2], b_OB[tix % 2]
                    S_.op("vector", "tensor_tensor", dict(out=ob[:, :], in0=OO[:, :], in1=RD[:, :], op=ALU.mult),
                          reads=[b_OO, b_RD], writes=[b_ob])
                    r0, c0, nm, ti = out_rows(heads[hix], qt)
                    S_.dma("sync", MIXT[r0:r0 + 128, c0:c0 + TW], ob[:, :], b_ob, reads=[b_ob], writes=[db(nm, ti)])

            for i in range(-1, NJ):
                if i + 1 < NJ:
                    stZ(i + 1)
                if i >= 0:
                    stP(i)
                    after_job(i)
                if i + 1 < NJ:
                    stE(i + 1)
                if pending_casts and i >= 0 and i % 16 == 8:
                    trickle()

        def phase_mem_attn(l):
            att_begin()
            nmb = ML // 128
            heads = [(s, h) for s in range(NS) for h in range(MH)]

            def load_head(hix):
                s, h = heads[hix]
                j = hix % 2
                c0 = s * S
                S_.dma("sync", QT_[j], QM[h, :, c0:c0 + S], b_Q[j], reads=seq_reads(["QM9", "QMB"], s), writes=[b_Q[j]])
                S_.dma("sync", KT_[j][:, 0:ML], MK[l, h, :, s * ML:(s + 1) * ML], b_K[j], reads=[db("MK", l)], writes=[b_K[j]])
                S_.dma("sync", VV_[j][:, 0:nmb * 128].rearrange("p (b d) -> p b d", d=128),
                       MV[l, s * ML:(s + 1) * ML, h * 128:(h + 1) * 128].rearrange("(b p) d -> p b d", p=128), b_V[j],
                       reads=[db("MV", l)], writes=[b_V[j]])

            def tile_jobs(qt):
                return [(mb, 0, False) for mb in range(nmb)]

            def z_mms(hd, j, qt, kb, cs, Z, b_Z):
                q0 = qt * TW
                S_.op("tensor", "matmul", dict(out=Z[:, :], lhsT=KT_[j][:, kb * 128:(kb + 1) * 128], rhs=QT_[j][:, q0:q0 + TW], start=True, stop=True),
                      reads=[b_K[j], b_Q[j]], writes=[b_Z])

            def v_of(j, kb):
                return VV_[j][:, kb * 128:(kb + 1) * 128], b_V[j]

            def out_rows(hd, qt):
                s, h = hd
                return (SBH + h) * 128, s * S + qt * TW, "MIXT%d" % (SBH + h), s * TPS + qt
            softmax_heads(heads, load_head, tile_jobs, z_mms, v_of, mask_in, 128 ** -0.5, out_rows)
            att_end()

        FAv = ARENA[:].rearrange("p (k n) -> p k n", n=512)

        def phase_p3(l, hsrc, hsrc_name, wout2d, wout_rd):
            wgu = wb["ffn_w_gu"][l]
            wdn = wb["ffn_w_down"][l]
            rd_gu = [wbuf[("ffn_w_gu", l)]]
            rd_dn = [wbuf[("ffn_w_down", l)]]
            load_gain(ffn_norm_g[l])
            trickle(len(pending_casts))
            for tt in range(NT):
                t0 = tt * TW
                S_.dma("sync", ATv, MIXT[:, t0:t0 + TW].rearrange("(k p) t -> p k t", p=128), b_AT,
                       reads=[db("MIXT%d" % hh, tt) for hh in range(16)], writes=[b_AT])
                for cc in range(4):
                    wv, b_w = load_w(wsrc(wout2d, cc * 512, 512), KC, 512, wout_rd)

                    pend = {}

                    def ld_hsl(blk, cc=cc, pend=pend):
                        hsl, b_hsl = next_evh()
                        r0 = t0 + blk * 128
                        S_.dma(STQ, hsl[:, :], hsrc[r0:r0 + 128, cc * 512:(cc + 1) * 512], b_hsl, reads=[db(hsrc_name, tt)], writes=[b_hsl])
                        pend[blk] = (hsl, b_hsl)

                    def evac(blk, bank, b_bank, cc=cc, pend=pend, ld_hsl=ld_hsl):
                        if blk == 0:
                            ld_hsl(0)
                        if blk + 1 < TW // 128:
                            ld_hsl(blk + 1)
                        hsl, b_hsl = pend[blk]
                        r0 = t0 + blk * 128
                        ev, b_ev = next_evf()
                        S_.op("vector", "tensor_tensor", dict(out=ev[:, :], in0=bank[:, :], in1=hsl[:, :], op=ALU.add),
                              reads=[b_bank, b_hsl], writes=[b_ev])
                        S_.dma(STQ, H1[r0:r0 + 128, cc * 512:(cc + 1) * 512], ev[:, :], b_ev, reads=[b_ev], writes=[db("H1", tt)])
                    lin_as(wv, b_w, KC, 512, ATv, b_AT, TW // 128, evac)
                pre_g = load_w(wsrc(wgu, 0, 512), KC, 512, rd_gu)
                pre_u = load_w(wsrc(wgu, FF, 512), KC, 512, rd_gu)
                norm_tile(H1, t0, TW, [db("H1", tt)], ATv, b_AT)
                for g4 in range(FC // 4):
                    if g4 == 0:
                        (wg, b_wg), (wu, b_wu) = pre_g, pre_u
                    else:
                        wg, b_wg = load_w(wsrc(wgu, g4 * 512, 512), KC, 512, rd_gu)
                        wu, b_wu = load_w(wsrc(wgu, FF + g4 * 512, 512), KC, 512, rd_gu)
                    for c in range(4):
                        fc = g4 * 4 + c
                        G, b_G = next_pb()
                        U, b_U = next_pb()
                        for (wv_, b_w_, bank_, b_bank_) in ((wg, b_wg, G, b_G), (wu, b_wu, U, b_U)):
                            for k in range(KC):
                                l_ap = wv_[:, k, c * 128:(c + 1) * 128]
                                r_ap = ATv[:, k, :]
                                S_.op("tensor", "matmul", dict(out=bank_[:, :], lhsT=l_ap, rhs=r_ap, start=(k == 0), stop=(k == KC - 1)),
                                      reads=[b_w_, b_AT], writes=[b_bank_], signal=(k == KC - 1))
                        sg, b_sg = next_evf()
                        S_.op("scalar", "activation", dict(out=sg[:, :], in_=G[:, :], func=AF.Silu), reads=[b_G], writes=[b_sg])
                        S_.op("vector", "tensor_tensor", dict(out=FAv[:, fc, :], in0=U[:, :], in1=sg[:, :], op=ALU.mult),
                              reads=[b_U, b_sg], writes=[b_FA])
                for cc in range(4):
                    banks = [next_pb() for _ in range(4)]
                    pieces = [(0, 16), (16, 16), (32, 12)]
                    for pi, (f0, nf) in enumerate(pieces):
                        wv, b_w = load_w(wsrc(wdn, cc * 512, 512, r0=f0 * 128, kc=nf), nf, 512, rd_dn)
                        for blk in range(4):
                            bank, b_bank = banks[blk]
                            for k in range(nf):
                                l_ap = FAv[:, f0 + k, blk * 128:(blk + 1) * 128]
                                r_ap = wv[:, k, :]
                                last = (pi == len(pieces) - 1 and k == nf - 1)
                                first = (pi == 0 and k == 0)
                                S_.op("tensor", "matmul", dict(out=bank[:, :], lhsT=l_ap, rhs=r_ap, start=first, stop=last),
                                      reads=[b_w, b_FA], writes=[b_bank], signal=(k == nf - 1))
                    pend = {}

                    def ld_h1(blk):
                        hsl, b_hsl = next_evh()
                        r0 = t0 + blk * 128
                        S_.dma(STQ, hsl[:, :], H1[r0:r0 + 128, cc * 512:(cc + 1) * 512], b_hsl, reads=[db("H1", tt)], writes=[b_hsl])
                        pend[blk] = (hsl, b_hsl)
                    ld_h1(0)
                    for blk in range(4):
                        bank, b_bank = banks[blk]
                        if blk + 1 < 4:
                            ld_h1(blk + 1)
                        hsl, b_hsl = pend[blk]
                        r0 = t0 + blk * 128
                        ev, b_ev = next_evf()
                        S_.op("vector", "tensor_tensor", dict(out=ev[:, :], in0=bank[:, :], in1=hsl[:, :], op=ALU.add),
                              reads=[b_bank, b_hsl], writes=[b_ev])
                        S_.dma(STQ, Hs[r0:r0 + 128, cc * 512:(cc + 1) * 512], ev[:, :], b_ev, reads=[b_ev], writes=[db("Hs", tt)])

        CQv = AT
        CT = sb("ct", [128, 4, 512], BF16); b_CT = Buf("ct")
        XQ = [sb("xq%d" % i, [128, QL], BF16) for i in range(2)]; b_XQ = [Buf("xq0"), Buf("xq1")]
        CS = sb("cs", [128, 512], F32); b_CS = Buf("cs")
        SN = sb("sn", [128, 512], F32); b_SN = Buf("sn")

        def load_rope(tt):
            t0 = tt * TW
            S_.dma("sync", CS[:, :], COS[:, t0:t0 + TW], b_CS, reads=[db("rope", tt)], writes=[b_CS])
            S_.dma("sync", SN[:, :], SIN[:, t0:t0 + TW], b_SN, reads=[db("rope", tt)], writes=[b_SN])

        def rope_evac(bankA, b_A, bankB, b_B, out_ap, b_out):
            t1, b_t1 = next_evf()
            t2, b_t2 = next_evf()
            S_.op("vector", "tensor_tensor", dict(out=t1[:, :], in0=bankA[:, :], in1=CS[:, :], op=ALU.mult), reads=[b_A, b_CS], writes=[b_t1])
            S_.op("vector", "tensor_tensor", dict(out=t2[:, :], in0=bankB[:, :], in1=SN[:, :], op=ALU.mult), reads=[b_B, b_SN], writes=[b_t2])
            S_.op("gpsimd", "tensor_tensor", dict(out=out_ap, in0=t1[:, :], in1=t2[:, :], op=ALU.add), reads=[b_t1, b_t2], writes=[b_out])

        def latent_norm_T(bank, b_bank, blk, gq_loaded):
            j = blk % 2
            rstd_of(bank[:, :], QL, b_bank, j, XQ[j][:, :], b_XQ[j])
            S_.op("vector", "scalar_tensor_tensor", dict(out=XQ[j][:, :], in0=bank[:, :], scalar=RS[j][:, 0:1], in1=GQ[:, :],
                                                             op0=ALU.mult, op1=ALU.mult),
                  reads=[b_bank, b_RS[j], b_GQ], writes=[b_XQ[j]])
            transpose_into(XQ[j], b_XQ[j], 4, CT, b_CT, blk * 128)

        def rope_weight_tiles(w2d, rd, col_of_pairchunk):
            raise NotImplementedError

        def phase_kv(hsrc, hsrc_name):
            load_gain(kv_norm_g[0])
            S_.dma("sync", GQ[:, :], kv_latent_g[0].partition_broadcast(128), b_GQ, writes=[b_GQ])
            wd = wb["w_dkv"]
            rd = [wbuf[("w_dkv", None)]]
            wu = wb["w_ukv"]
            rdu = [wbuf[("w_ukv", None)]]
            for tt in range(NT):
                t0 = tt * TW
                norm_tile(hsrc, t0, TW, [db(hsrc_name, tt)], ATv, b_AT)
                load_rope(tt)
                wv, b_w = load_w(wsrc(wd, 0, 512), KC, 512, rd)

                def evac_lat(blk, bank, b_bank):
                    latent_norm_T(bank, b_bank, blk, True)
                lin_as(wv, b_w, KC, 512, ATv, b_AT, TW // 128, evac_lat)
                i = wctr[0] % NW
                wctr[0] += 1
                wr = WS[i][:, 0:KC * 256].rearrange("p (k n) -> p k n", n=256)
                for q, (dst0, src0) in enumerate([(0, 512), (64, 512), (128, 544), (160, 512), (192, 544), (224, 512)]):
                    n = 64 if q < 2 else 32
                    S_.dma("sync", wr[:, :, dst0:dst0 + n], wsrc(wd, src0, n), b_WS[i], reads=rd, writes=[b_WS[i]] if q == 0 else [], chain=False)
                b_WS[i].w = b_WS[i].dlast
                bA, b_bA = next_pb()
                bB, b_bB = next_pb()
                for (off, bank_, b_bank_) in ((0, bA, b_bA), (128, bB, b_bB)):
                    for k in range(KC):
                        l_ap = wr[:, k, off:off + 128]
                        r_ap = ATv[:, k, :]
                        S_.op("tensor", "matmul", dict(out=bank_[:, :], lhsT=l_ap, rhs=r_ap, start=(k == 0), stop=(k == KC - 1)),
                              reads=[b_WS[i], b_AT], writes=[b_bank_], signal=(k == KC - 1))
                ob, b_ob = OB[tt % 2], b_OB[tt % 2]
                rope_evac(bA, b_bA, bB, b_bB, ob[:, :], b_ob)
                S_.dma(STQ, KR[:, t0:t0 + TW], ob[:, :], b_ob, reads=[b_ob], writes=[db("KR", tt)])
                for g in range(3):
                    i2 = wctr[0] % NW
                    wctr[0] += 1
                    wn = WS[i2][:, 0:4 * 512].rearrange("p (k n) -> p k n", n=512)
                    for hh in range(4):
                        h = g * 4 + hh
                        S_.dma("sync", wn[:, :, hh * 128:(hh + 1) * 128], wsrc(wu, h * 256, 128, kc=4), b_WS[i2], reads=rdu,
                               writes=[b_WS[i2]] if hh == 0 else [], chain=False)
                    b_WS[i2].w = b_WS[i2].dlast
                    evb, b_evb = next_evb()

                    def evac(c, bank, b_bank, evb=evb, b_evb=b_evb):
                        copy_alt(evb[:, c, :], bank[:, :], [b_bank], [b_evb])
                    lin_ws(wn, b_WS[i2], 4, 4, CT, b_CT, TW, evac)
                    S_.dma(STQ, KN[g * 4:(g + 1) * 4, :, t0:t0 + TW].rearrange("h p t -> p h t"), evb[:], b_evb, reads=[b_evb], writes=[db("KN%d" % g, tt)])
                for g in range(3):
                    i2 = wctr[0] % NW
                    wctr[0] += 1
                    wn = WS[i2][:, 0:4 * 512].rearrange("p (k n) -> p k n", n=512)
                    for hh in range(4):
                        h = g * 4 + hh
                        S_.dma("sync", wn[:, :, hh * 128:(hh + 1) * 128], wsrc(wu, h * 256 + 128, 128, kc=4), b_WS[i2], reads=rdu,
                               writes=[b_WS[i2]] if hh == 0 else [], chain=False)
                    b_WS[i2].w = b_WS[i2].dlast
                    evb, b_evb = next_evb()

                    def evac(blk, bank, b_bank, evb=evb, b_evb=b_evb):
                        copy_alt(evb[:, blk, :], bank[:, :], [b_bank], [b_evb])
                    lin_as(wn, b_WS[i2], 4, 512, CT, b_CT, TW // 128, evac)
                    S_.dma(STQ, VL[t0:t0 + TW, g * 512:(g + 1) * 512].rearrange("(b p) c -> p b c", p=128), evb[:], b_evb,
                           reads=[b_evb], writes=[db("VL%d" % g, tt)])

        def phase_p1b(l, hsrc, hsrc_name):
            i_b = l - na
            load_gain(attn_norm_g[l])
            S_.dma("sync", GQ[:, :], b_q_norm_g[i_b].partition_broadcast(128), b_GQ, writes=[b_GQ])
            w2d = wb["b_w_in"][i_b]
            rd = [wbuf[("b_w_in", i_b)]]
            wq = wb["b_w_uq"][i_b]
            rdq = [wbuf[("b_w_uq", i_b)]]
            for tt in range(NT):
                t0 = tt * TW
                norm_tile(hsrc, t0, TW, [db(hsrc_name, tt)], ATv, b_AT)
                load_rope(tt)
                wv, b_w = load_w(wsrc(w2d, 0, 512), KC, 512, rd)

                def evac_lat(blk, bank, b_bank):
                    latent_norm_T(bank, b_bank, blk, True)
                lin_as(wv, b_w, KC, 512, ATv, b_AT, TW // 128, evac_lat)
                wv, b_w = load_w(wsrc(w2d, 512, 512), KC, 512, rd)
                evb, b_evb = next_evb()

                def evac(c, bank, b_bank, evb=evb, b_evb=b_evb):
                    copy_alt(evb[:, c, :], bank[:, :], [b_bank], [b_evb])
                lin_ws(wv, b_w, KC, 4, ATv, b_AT, TW, evac)
                S_.dma(STQ, QM[:, :, t0:t0 + TW].rearrange("h p t -> p h t"), evb[:], b_evb, reads=[b_evb], writes=[db("QMB", tt)])
                for g in range(3):
                    i2 = wctr[0] % NW
                    wctr[0] += 1
                    wn = WS[i2][:, 0:4 * 512].rearrange("p (k n) -> p k n", n=512)
                    for hh in range(4):
                        h = g * 4 + hh
                        S_.dma("sync", wn[:, :, hh * 128:(hh + 1) * 128], wsrc(wq, h * 192, 128, kc=4), b_WS[i2], reads=rdq,
                               writes=[b_WS[i2]] if hh == 0 else [], chain=False)
                    b_WS[i2].w = b_WS[i2].dlast
                    evb, b_evb = next_evb()

                    def evac(c, bank, b_bank, evb=evb, b_evb=b_evb):
                        copy_alt(evb[:, c, :], bank[:, :], [b_bank], [b_evb])
                    lin_ws(wn, b_WS[i2], 4, 4, CT, b_CT, TW, evac)
                    S_.dma(STQ, QA[g * 4:(g + 1) * 4, :, t0:t0 + TW].rearrange("h p t -> p h t"), evb[:], b_evb, reads=[b_evb], writes=[db("QA%d" % g, tt)])
                for pr in range(6):
                    i2 = wctr[0] % NW
                    wctr[0] += 1
                    wr = WS[i2][:, 0:4 * 256].rearrange("p (k n) -> p k n", n=256)
                    first = True
                    for hh in range(2):
                        h = pr * 2 + hh
                        cb = h * 192 + 128
                        for (dst0, src0, n) in ((hh * 64, cb, 64), (128 + hh * 64, cb + 32, 32), (128 + hh * 64 + 32, cb, 32)):
                            S_.dma("sync", wr[:, :, dst0:dst0 + n], wsrc(wq, src0, n, kc=4), b_WS[i2], reads=rdq,
                                   writes=[b_WS[i2]] if first else [], chain=False)
                            first = False
                    b_WS[i2].w = b_WS[i2].dlast
                    bA, b_bA = next_pb()
                    bB, b_bB = next_pb()
                    for (off, bank_, b_bank_) in ((0, bA, b_bA), (128, bB, b_bB)):
                        for k in range(4):
                            l_ap = wr[:, k, off:off + 128]
                            r_ap = CT[:, k, :]
                            S_.op("tensor", "matmul", dict(out=bank_[:, :], lhsT=l_ap, rhs=r_ap, start=(k == 0), stop=(k == 3)),
                                  reads=[b_WS[i2], b_CT], writes=[b_bank_], signal=(k == 3))
                    ob, b_ob = OB[pr % 2], b_OB[pr % 2]
                    rope_evac(bA, b_bA, bB, b_bB, ob[:, :], b_ob)
                    S_.dma(STQ, QR[pr, :, t0:t0 + TW], ob[:, :], b_ob, reads=[b_ob], writes=[db("QR%d" % pr, tt)])

        def phase_p2b(l):
            att_begin()
            heads = [(s, h) for s in range(NS) for h in range(SBH)]

            def load_head(hix):
                s, h = heads[hix]
                j = hix % 2
                c0 = s * S
                if h == 0:
                    S_.dma("sync", KR_[s % 2], KR[:, c0:c0 + S], b_KR[s % 2], reads=seq_reads(["KR"], s), writes=[b_KR[s % 2]])
                S_.dma("sync", QT_[j], QA[h, :, c0:c0 + S], b_Q[j], reads=seq_reads(["QA%d" % (h // 4)], s), writes=[b_Q[j]])
                S_.dma("sync", QR_[j], QR[h // 2, :, c0:c0 + S], b_QR[j], reads=seq_reads(["QR%d" % (h // 2)], s), writes=[b_QR[j]])
                S_.dma("sync", KT_[j], KN[h, :, c0:c0 + S], b_K[j], reads=seq_reads(["KN%d" % (h // 4)], s), writes=[b_K[j]])
                S_.dma("sync", VV_[j].rearrange("p (b d) -> p b d", d=128),
                       VL[c0:c0 + S, h * 128:(h + 1) * 128].rearrange("(b p) d -> p b d", p=128), b_V[j],
                       reads=seq_reads(["VL%d" % (h // 4)], s), writes=[b_V[j]])

            def tile_jobs(qt):
                r = []
                for kb in range((qt * TW + TW) // 128):
                    jj = kb - qt * 4
                    r.append((kb, max(0, jj) * 128, jj >= 0))
                return r

            def z_mms(hd, j, qt, kb, cs, Z, b_Z):
                s, h = hd
                q0 = qt * TW
                pb0 = (h % 2) * 64
                S_.op("tensor", "matmul", dict(out=Z[:, cs:TW], lhsT=KT_[j][:, kb * 128:(kb + 1) * 128], rhs=QT_[j][:, q0 + cs:q0 + TW],
                                               start=True, stop=False),
                      reads=[b_K[j], b_Q[j]], writes=[b_Z], signal=False)
                S_.op("tensor", "matmul", dict(out=Z[:, cs:TW], lhsT=KR_[s % 2][pb0:pb0 + 64, kb * 128:(kb + 1) * 128],
                                               rhs=QR_[j][pb0:pb0 + 64, q0 + cs:q0 + TW], start=False, stop=True),
                      reads=[b_KR[s % 2], b_QR[j]], writes=[b_Z])

            def v_of(j, kb):
                return VV_[j][:, kb * 128:(kb + 1) * 128], b_V[j]

            def out_rows(hd, qt):
                s, h = hd
                return h * 128, s * S + qt * TW, "MIXT%d" % h, s * TPS + qt
            softmax_heads(heads, load_head, tile_jobs, z_mms, v_of, mask_in, 192 ** -0.5, out_rows)
            att_end()

        def phase_final(hsrc, hsrc_name):
            load_gain(final_norm_g[0])
            for tt in range(NT):
                for blk in range(TW // 128):
                    j = hb_ctr[0] % 2
                    hb_ctr[0] += 1
                    r0 = tt * TW + blk * 128
                    S_.dma("sync", HB[j][:], hsrc[r0:r0 + 128, :], b_HB[j], reads=[db(hsrc_name, tt)], writes=[b_HB[j]])
                    rstd_of(HB[j][:], D, b_HB[j], j, XN[j][:], b_XN[j])
                    S_.op("vector", "scalar_tensor_tensor", dict(out=HB[j][:], in0=HB[j][:], scalar=RS[j][:, 0:1], in1=GB[:],
                                                                         op0=ALU.mult, op1=ALU.mult),
                          reads=[b_HB[j], b_RS[j], b_GB], writes=[b_HB[j]])
                    S_.dma(STQ, out[r0:r0 + 128, :], HB[j][:], b_HB[j], reads=[b_HB[j]], writes=[db("out", tt * 4 + blk)])

        def want(name):
            return (only is None) or (name in only)
        def early_phases():
            if depth > na and want("rope"):
                phase_rope()
            if want("mem"):
                phase_mem()
        if na == 0:
            early_phases()
        hsrc, hname = x, "x"
        for l in range(depth):
            if l == na and want("kv"):
                phase_kv(hsrc, hname)
            if l < na:
                if want("p1_%d" % l):
                    phase_p1a(l, hsrc, hname)
                if l == 0:
                    early_phases()
                if lazy_cast and l + 1 < depth:
                    cast_sink[0] = pending_casts
                    cast_layer(l + 1)
                    cast_sink[0] = None
                if want("p2_%d" % l):
                    phase_p2a(l)
                wout2d, wrd = wb["a_w_out"][l], [wbuf[("a_w_out", l)]]
            else:
                if want("p1_%d" % l):
                    phase_p1b(l, hsrc, hname)
                if lazy_cast and l + 1 < depth:
                    cast_sink[0] = pending_casts
                    cast_layer(l + 1)
                    cast_sink[0] = None
                if want("p2_%d" % l):
                    phase_p2b(l)
                wout2d, wrd = wb["b_w_out"][l - na], [wbuf[("b_w_out", l - na)]]
            if want("ma_%d" % l):
                phase_mem_attn(l)
            if want("p3_%d" % l):
                phase_p3(l, hsrc, hname, wout2d, wrd)
                hsrc, hname = Hs, "Hs"
        if want("final"):
            phase_final(hsrc, hname)
        allb = list(dbufs.values()) + list(wbuf.values())
        S_.final_wait("sync", allb)
        S_.run()
    return nc


_INPUT_ORDER = ["attn_norm_g", "ffn_norm_g", "a_w_in", "a_w_out", "b_w_in", "b_q_norm_g", "b_w_uq", "b_w_out",
                "mem_norm_g", "w_mem_kv", "kv_norm_g", "w_dkv", "kv_latent_g", "w_ukv", "ffn_w_gu", "ffn_w_down",
                "final_norm_g"]


def make_in_maps(inputs, n_cores, ns):
    cst, rc = host_consts()
    shared = {}
    for k in _INPUT_ORDER:
        a = np.ascontiguousarray(np.asarray(inputs[k], dtype=np.float32))
        if a.ndim == 1:
            a = a.reshape(1, -1)
        shared[k] = a
    shared["cst"] = cst
    shared["rc"] = rc
    x = np.asarray(inputs["x"], dtype=np.float32)
    mem = np.asarray(inputs["mem"], dtype=np.float32)
    pos = np.asarray(inputs["positions"], dtype=np.int32)
    maps = []
    for c in range(n_cores):
        m = dict(shared)
        m["x"] = np.ascontiguousarray(x[c * ns:(c + 1) * ns].reshape(-1, D))
        m["mem"] = np.ascontiguousarray(mem[c * ns:(c + 1) * ns].reshape(-1, D))
        m["positions"] = np.ascontiguousarray(pos[c * ns:(c + 1) * ns].reshape(-1))
        maps.append(m)
    return maps


def kernel(**inputs):
    x = np.asarray(inputs["x"])
    B, S, _ = x.shape
    ML = np.asarray(inputs["mem"]).shape[1]
    ns = B // N_CORES
    nc = build(ns, S, ML)
    in_maps = make_in_maps(inputs, N_CORES, ns)
    res = run_bass_kernel_spmd(nc, in_maps, core_ids=list(range(N_CORES)))
    outs = [np.asarray(r["out"]).reshape(ns, S, D) for r in res.results]
    return np.concatenate(outs, axis=0).astype(np.float32)
```

```python
import contextlib
import math
import numpy as np
import ml_dtypes
import concourse.bass as bass
import concourse.mybir as mybir
from concourse.bass_utils import run_bass_kernel_spmd

F32 = mybir.dt.float32
F32R = mybir.dt.float32r
BF16 = mybir.dt.bfloat16
I32 = mybir.dt.int32
AF = mybir.ActivationFunctionType
ALU = mybir.AluOpType

D = 2048
KC = 16
HD = 128
SBH = 12
MH = 4
FF = 5632
FC = 44
QL = 512
DEPTH = 4
NA = 2
EPS = 1e-6
N_CORES = 8

SAME_ENGINE_SYNC = True


class Buf:
    __slots__ = ("name", "w", "r", "dsem", "dcount", "dlast")

    def __init__(self, name=""):
        self.name = name
        self.w = None
        self.r = []
        self.dsem = None
        self.dcount = 0
        self.dlast = None


class Eng:
    def __init__(self, name, sem, inorder_skip=False):
        self.name = name
        self.sem = sem
        self.count = 0
        self.ops = []
        self.waited = {}
        self.inorder_skip = inorder_skip


class Sched:
    def __init__(self, nc, stack):
        self.nc = nc
        self.stack = stack
        self.nsem = 0
        self.E = {}
        for n in ("tensor", "vector", "scalar", "gpsimd", "sync"):
            self.E[n] = Eng(n, self.new_sem("e_" + n), inorder_skip=(n == "tensor"))

    def new_sem(self, name):
        self.nsem += 1
        assert self.nsem < 230, "too many semaphores"
        return self.stack.enter_context(self.nc.semaphore(name))

    def _deps(self, eng, reads, writes):
        deps = []
        for b in reads:
            if b.w is not None:
                deps.append(b.w)
        for b in writes:
            deps.extend(b.r)
            if b.w is not None:
                deps.append(b.w)
        need = {}
        for (s, v) in deps:
            if s is eng.sem and (eng.inorder_skip or not SAME_ENGINE_SYNC):
                continue
            k = id(s)
            if eng.waited.get(k, 0) >= v:
                continue
            if k not in need or need[k][1] < v:
                need[k] = (s, v)
        for k, (s, v) in need.items():
            eng.waited[k] = v
        return list(need.values())

    def op(self, engname, method, kwargs, reads=(), writes=(), signal=True):
        eng = self.E[engname]
        waits = self._deps(eng, reads, writes)
        if signal:
            eng.count += 1
            tok = (eng.sem, eng.count)
        else:
            tok = (eng.sem, eng.count + 1)
        sem = eng.sem

        def emit(e, waits=waits, method=method, kwargs=kwargs, signal=signal, sem=sem):
            for (s, v) in waits:
                e.wait_ge(s, v)
            ins = getattr(e, method)(**kwargs)
            if signal:
                ins.then_inc(sem, 1)
        eng.ops.append(emit)
        for b in reads:
            b.r.append(tok)
        for b in writes:
            b.w = tok
            b.r = []
        return tok

    def dma(self, engname, out_ap, in_ap, sembuf, reads=(), writes=(), chain=True):
        eng = self.E[engname]
        if sembuf.dsem is None:
            sembuf.dsem = self.new_sem("d_" + sembuf.name)
        waits = self._deps(eng, reads, writes)
        if chain and sembuf.dlast is not None:
            s, v = sembuf.dlast
            if eng.waited.get(id(s), 0) < v:
                eng.waited[id(s)] = v
                waits.append((s, v))
        sembuf.dcount += 16
        tok = (sembuf.dsem, sembuf.dcount)
        sembuf.dlast = tok
        dsem = sembuf.dsem

        def emit(e, waits=waits, dsem=dsem, out_ap=out_ap, in_ap=in_ap):
            for (s, v) in waits:
                e.wait_ge(s, v)
            e.dma_start(out=out_ap, in_=in_ap).then_inc(dsem, 16)
        eng.ops.append(emit)
        for b in reads:
            b.r.append(tok)
        for b in writes:
            b.w = tok
            b.r = []
        return tok

    def fence(self, old_bufs, new_bufs):
        best = {}
        for b in old_bufs:
            for t in list(b.r) + ([b.w] if b.w is not None else []):
                k = id(t[0])
                if k not in best or best[k][1] < t[1]:
                    best[k] = t
        for b in new_bufs:
            mine = dict(best)
            for t in b.r:
                k = id(t[0])
                if k not in mine or mine[k][1] < t[1]:
                    mine[k] = t
            b.r = list(mine.values())

    def final_wait(self, engname, bufs):
        eng = self.E[engname]
        toks = []
        for b in bufs:
            if b.w is not None:
                toks.append(b.w)

        def emit(e, toks=toks):
            for (s, v) in toks:
                e.wait_ge(s, v)
        eng.ops.append(emit)

    def run(self):
        nc = self.nc
        E = self.E
        with nc.Block() as block:
            @block.sync
            def _(e):
                for f in E["sync"].ops:
                    f(e)

            @block.scalar
            def _(e):
                for f in E["scalar"].ops:
                    f(e)

            @block.vector
            def _(e):
                for f in E["vector"].ops:
                    f(e)

            @block.gpsimd
            def _(e):
                for f in E["gpsimd"].ops:
                    f(e)

            @block.tensor
            def _(e):
                for f in E["tensor"].ops:
                    f(e)


def host_consts():
    p = np.arange(128)
    cst = np.zeros((128, 7, 128), np.float32)
    cst[:, 0, :] = np.eye(128)
    cst[:, 1, :] = (p[:, None] > p[None, :])
    cst[:, 2, :] = (p[:, None] <= p[None, :])
    cst[:, 3, :] = 1.0
    cst[:, 4, :] = 0.0
    cst[:, 5, :] = (p[:, None] < p[None, :])
    cst[:, 6, :] = (p[:, None] <= p[None, :])
    half = 32
    inv_freq = (10000.0 ** (-np.arange(half, dtype=np.float32) / half)).astype(np.float32)
    rc = np.zeros((128, 4), np.float32)
    rc[:, 0] = inv_freq[p % 32]
    rc[:, 1] = np.where((p % 64) < 32, -1.0, 1.0)
    return cst.astype(ml_dtypes.bfloat16), rc


def build(NS, S, ML, depth=DEPTH, dbg=None, only=None, cast_filter=None):
    T = NS * S
    TW = 512
    assert S % TW == 0
    NT = T // TW
    TPS = S // TW
    NBS = S // 128
    MT = NS * ML
    assert MT <= 512 and MT % 128 == 0
    na = depth // 2

    nc = bass.Bass("TRN2", target_bir_lowering=False)

    def din(name, shape, dt=F32):
        return nc.dram_tensor(name, list(shape), dt, kind="ExternalInput").ap()

    def dscr(name, shape, dt):
        kind = "ExternalOutput" if (dbg and not name.startswith("wb_")) else "Internal"
        return nc.dram_tensor(name, list(shape), dt, kind=kind).ap()

    x = din("x", [T, D])
    mem = din("mem", [MT, D])
    pos = din("positions", [T], I32)
    attn_norm_g = din("attn_norm_g", [DEPTH, D])
    ffn_norm_g = din("ffn_norm_g", [DEPTH, D])
    a_w_in = din("a_w_in", [NA, D, 5120])
    a_w_out = din("a_w_out", [NA, D, D])
    b_w_in = din("b_w_in", [NA, D, 1024])
    b_q_norm_g = din("b_q_norm_g", [NA, QL])
    b_w_uq = din("b_w_uq", [NA, QL, 2304])
    b_w_out = din("b_w_out", [NA, D, D])
    mem_norm_g = din("mem_norm_g", [1, D])
    w_mem_kv = din("w_mem_kv", [DEPTH, D, 1024])
    kv_norm_g = din("kv_norm_g", [1, D])
    w_dkv = din("w_dkv", [D, 576])
    kv_latent_g = din("kv_latent_g", [1, QL])
    w_ukv = din("w_ukv", [QL, 3072])
    ffn_w_gu = din("ffn_w_gu", [DEPTH, D, 2 * FF])
    ffn_w_down = din("ffn_w_down", [DEPTH, FF, D])
    final_norm_g = din("final_norm_g", [1, D])
    cst_d = din("cst", [128, 7, 128], BF16)
    rc_d = din("rc", [128, 4], F32)
    out = nc.dram_tensor("out", [T, D], F32, kind="ExternalOutput").ap()

    wb = {}
    wb["a_w_in"] = dscr("wb_a_w_in", [NA, D, 5120], BF16)
    wb["a_w_out"] = dscr("wb_a_w_out", [NA, D, D], BF16)
    wb["b_w_in"] = dscr("wb_b_w_in", [NA, D, 1024], BF16)
    wb["b_w_uq"] = dscr("wb_b_w_uq", [NA, QL, 2304], BF16)
    wb["b_w_out"] = dscr("wb_b_w_out", [NA, D, D], BF16)
    wb["w_mem_kv"] = dscr("wb_w_mem_kv", [DEPTH, D, 1024], BF16)
    wb["w_dkv"] = dscr("wb_w_dkv", [D, 576], BF16)
    wb["w_ukv"] = dscr("wb_w_ukv", [QL, 3072], BF16)
    wb["ffn_w_gu"] = dscr("wb_ffn_w_gu", [DEPTH, D, 2 * FF], BF16)
    wb["ffn_w_down"] = dscr("wb_ffn_w_down", [DEPTH, FF, D], BF16)

    Hs = dscr("Hs", [T, D], F32)
    H1 = dscr("H1", [T, D], F32)
    QA = dscr("QA", [SBH, 128, T], BF16)
    KA = dscr("KA", [SBH, 128, T], BF16)
    VA = dscr("VA", [T, SBH * 128], BF16)
    KN = dscr("KN", [SBH, 128, T], BF16)
    VL = dscr("VL", [T, SBH * 128], BF16)
    KR = dscr("KR", [128, T], BF16)
    QR = dscr("QR", [6, 128, T], BF16)
    QM = dscr("QM", [MH, 128, T], BF16)
    MK = dscr("MK", [DEPTH, MH, 128, MT], BF16)
    MV = dscr("MV", [DEPTH, MT, MH * 128], BF16)
    MIXT = dscr("MIXT", [D, T], BF16)
    CQ = dscr("CQ", [QL, T], BF16)
    COS = dscr("COS", [128, T], F32)
    SIN = dscr("SIN", [128, T], F32)

    with contextlib.ExitStack() as st:
        S_ = Sched(nc, st)

        def sb(name, shape, dt):
            return st.enter_context(nc.sbuf_tensor(name, list(shape), dt))

        def ps(name, shape, dt):
            return st.enter_context(nc.psum_tensor(name, list(shape), dt))

        cst = sb("cst_sb", [128, 7, 128], BF16); b_cst = Buf("cst")
        rc = sb("rc_sb", [128, 4], F32); b_rc = Buf("rc")
        NW = 4
        WS = [sb("ws%d" % i, [128, 8192], BF16) for i in range(NW)]
        b_WS = [Buf("ws%d" % i) for i in range(NW)]
        AT = sb("AT", [128, 8192], BF16); b_AT = Buf("AT")
        ARENA = sb("arena", [128, 22528], BF16)
        b_FA = Buf("FA")
        HB = [sb("hb%d" % i, [128, D], F32) for i in range(2)]
        b_HB = [Buf("hb%d" % i) for i in range(2)]
        XN = [sb("xn%d" % i, [128, D], BF16) for i in range(2)]
        b_XN = [Buf("xn%d" % i) for i in range(2)]
        GB = sb("gb", [128, D], F32); b_GB = Buf("gb")
        GQ = sb("gq", [128, QL], F32); b_GQ = Buf("gq")
        SS = [sb("ss%d" % i, [128, 1], F32) for i in range(2)]
        b_SS = [Buf("ss%d" % i) for i in range(2)]
        RS = [sb("rs%d" % i, [128, 1], F32) for i in range(2)]
        b_RS = [Buf("rs%d" % i) for i in range(2)]
        EVF = [sb("evf%d" % i, [128, 512], F32) for i in range(4)]
        b_EVF = [Buf("evf%d" % i) for i in range(4)]
        EVH = [sb("evh%d" % i, [128, 512], F32) for i in range(2)]
        b_EVH = [Buf("evh%d" % i) for i in range(2)]
        EVB = [sb("evb%d" % i, [128, 4, 512], BF16) for i in range(2)]
        b_EVB = [Buf("evb%d" % i) for i in range(2)]

        NPB = 6
        PB = [ps("pb%d" % i, [128, 512], F32) for i in range(NPB)]
        b_PB = [Buf("pb%d" % i) for i in range(NPB)]
        PTS = [ps("pt%d" % i, [128, 8, 128], BF16) for i in range(2)]
        b_PT = [Buf("pt0"), Buf("pt1")]
        ptctr = [0]

        ident = cst[:, 0, :]
        tri_lo = cst[:, 1, :]
        tri_up = cst[:, 2, :]
        ones_m = cst[:, 3, :]
        zeros_m = cst[:, 4, :]
        mask_st = cst[:, 5, :]
        mask_in = cst[:, 6, :]

        S_.dma("sync", cst[:], cst_d[:, :, :], b_cst, writes=[b_cst])
        S_.dma("sync", rc[:], rc_d[:, :], b_rc, writes=[b_rc])

        wbuf = {}

        cast_sink = [None]
        pending_casts = []

        def cast(name, idx, src2d, dst2d, rows, rchunk):
            b = Buf("c_%s%s" % (name, "" if idx is None else str(idx)))
            wbuf[(name, idx)] = b
            if cast_filter is not None and name not in cast_filter:
                return
            if cast_sink[0] is not None:
                rchunk = max(128, rchunk // 2)
            r = 0
            while r < rows:
                r1 = min(rows, r + rchunk)

                def th(r=r, r1=r1, b=b):
                    S_.dma("gpsimd", dst2d[r:r1, :], src2d[r:r1, :], b, writes=[b], chain=False)
                if cast_sink[0] is None:
                    th()
                else:
                    cast_sink[0].append(th)
                r = r1

        def trickle(n=1):
            for _ in range(n):
                if pending_casts:
                    pending_casts.pop(0)()

        def cast_layer(l):
            if l < na:
                if l > 0:
                    cast("a_w_in", l, a_w_in[l], wb["a_w_in"][l], D, 512)
                cast("a_w_out", l, a_w_out[l], wb["a_w_out"][l], D, 1024)
            else:
                i = l - na
                if i == 0:
                    cast("w_dkv", None, w_dkv, wb["w_dkv"], D, 2048)
                    cast("w_ukv", None, w_ukv, wb["w_ukv"], QL, 512)
                cast("b_w_in", i, b_w_in[i], wb["b_w_in"][i], D, 2048)
                cast("b_w_uq", i, b_w_uq[i], wb["b_w_uq"][i], QL, 512)
                cast("b_w_out", i, b_w_out[i], wb["b_w_out"][i], D, 1024)
            cast("ffn_w_gu", l, ffn_w_gu[l], wb["ffn_w_gu"][l], D, 256)
            cast("ffn_w_down", l, ffn_w_down[l], wb["ffn_w_down"][l], FF, 1408)

        if na > 0:
            cast("a_w_in", 0, a_w_in[0], wb["a_w_in"][0], D, 512)
        for l in range(depth):
            cast("w_mem_kv", l, w_mem_kv[l], wb["w_mem_kv"][l], D, 2048)
        lazy_cast = (only is None)
        cast_layer(0)
        if not lazy_cast:
            for l in range(1, depth):
                cast_layer(l)

        STQ = "gpsimd"
        wctr = [0]

        def wslot_view(i, kc, n):
            return WS[i][:, 0:kc * n].rearrange("p (k n) -> p k n", n=n)

        def load_w(src3, kc, n, reads):
            i = wctr[0] % NW
            wctr[0] += 1
            v = wslot_view(i, kc, n)
            S_.dma("sync", v, src3, b_WS[i], reads=reads, writes=[b_WS[i]])
            return v, b_WS[i]

        def wsrc(w2d, c0, n, r0=0, kc=KC):
            return w2d[r0:r0 + kc * 128, c0:c0 + n].rearrange("(k p) n -> p k n", p=128)

        pbctr = [0]

        def next_pb():
            i = pbctr[0] % NPB
            pbctr[0] += 1
            return PB[i], b_PB[i]

        evf_ctr = [0]

        def next_evf():
            i = evf_ctr[0] % 4
            evf_ctr[0] += 1
            return EVF[i], b_EVF[i]

        evh_ctr = [0]

        def next_evh():
            i = evh_ctr[0] % 2
            evh_ctr[0] += 1
            return EVH[i], b_EVH[i]

        evb_ctr = [0]

        def next_evb():
            i = evb_ctr[0] % 2
            evb_ctr[0] += 1
            return EVB[i], b_EVB[i]

        alt = [0]

        def copy_alt(out_ap, in_ap, reads, writes):
            alt[0] ^= 1
            if alt[0]:
                S_.op("scalar", "activation", dict(out=out_ap, in_=in_ap, func=AF.Copy), reads=reads, writes=writes)
            else:
                S_.op("vector", "tensor_copy", dict(out=out_ap, in_=in_ap), reads=reads, writes=writes)

        def rstd_of(src_ap, n, b_src, j, junk_ap, b_junk):
            S_.op("scalar", "activation", dict(out=junk_ap, in_=src_ap, func=AF.Square, accum_out=SS[j][:]),
                  reads=[b_src], writes=[b_junk, b_SS[j]])
            S_.op("scalar", "activation", dict(out=RS[j][:], in_=SS[j][:], func=AF.Ln, scale=1.0 / n, bias=EPS),
                  reads=[b_SS[j]], writes=[b_RS[j]])
            S_.op("scalar", "activation", dict(out=RS[j][:], in_=RS[j][:], func=AF.Exp, scale=-0.5),
                  reads=[b_RS[j]], writes=[b_RS[j]])

        hb_ctr = [0]

        def transpose_into(src_bf, b_src, nk, dstT, b_dst, col0, ncol=128):
            for k0 in range(0, nk, 8):
                h = ptctr[0] % 2
                ptctr[0] += 1
                PT = PTS[h]
                kk = min(8, nk - k0)
                for k in range(k0, k0 + kk):
                    pt_ap = PT[:, k - k0, 0:ncol]
                    in_ap = src_bf[0:ncol, k * 128:(k + 1) * 128]
                    S_.op("tensor", "transpose", dict(out=pt_ap, in_=in_ap, identity=ident[0:ncol, 0:ncol]),
                          reads=[b_src, b_cst], writes=[b_PT[h]], signal=(k == k0 + kk - 1))
                o_ap = dstT[:, k0:k0 + kk, col0:col0 + ncol]
                i_ap = PT[:, 0:kk, 0:ncol]
                S_.op("vector", "tensor_copy", dict(out=o_ap, in_=i_ap),
                      reads=[b_PT[h]], writes=[b_dst])

        def load_gain(g_row, n=D):
            S_.dma("sync", GB[:, 0:n], g_row.partition_broadcast(128), b_GB, writes=[b_GB])

        def norm_tile(src, row0, nrows, b_srcs, dstT, b_dst):
            for blk in range(nrows // 128):
                j = hb_ctr[0] % 2
                hb_ctr[0] += 1
                r0 = row0 + blk * 128
                S_.dma("sync", HB[j][:], src[r0:r0 + 128, :], b_HB[j], reads=b_srcs, writes=[b_HB[j]])
                rstd_of(HB[j][:], D, b_HB[j], j, XN[j][:], b_XN[j])
                S_.op("vector", "scalar_tensor_tensor", dict(out=XN[j][:], in0=HB[j][:], scalar=RS[j][:, 0:1], in1=GB[:],
                                                                     op0=ALU.mult, op1=ALU.mult),
                      reads=[b_HB[j], b_RS[j], b_GB], writes=[b_XN[j]])
                transpose_into(XN[j], b_XN[j], KC, dstT, b_dst, blk * 128)

        def lin_ws(wv, b_w, kc, nch, actT, b_act, ncols, evac, col_off=0, m=128):
            for c in range(nch):
                bank, b_bank = next_pb()
                for k in range(kc):
                    l_ap = wv[:, k, col_off + c * m:col_off + (c + 1) * m]
                    r_ap = actT[:, k, 0:ncols]
                    o_ap = bank[0:m, 0:ncols]
                    S_.op("tensor", "matmul", dict(out=o_ap, lhsT=l_ap, rhs=r_ap, start=(k == 0), stop=(k == kc - 1)),
                          reads=[b_w, b_act], writes=[b_bank], signal=(k == kc - 1))
                evac(c, bank, b_bank)

        def lin_as(wv, b_w, kc, n, actT, b_act, nblk, evac, k_off=0):
            for blk in range(nblk):
                bank, b_bank = next_pb()
                for k in range(kc):
                    l_ap = actT[:, k_off + k, blk * 128:(blk + 1) * 128]
                    r_ap = wv[:, k, 0:n]
                    o_ap = bank[:, 0:n]
                    S_.op("tensor", "matmul", dict(out=o_ap, lhsT=l_ap, rhs=r_ap, start=(k == 0), stop=(k == kc - 1)),
                          reads=[b_w, b_act], writes=[b_bank], signal=(k == kc - 1))
                evac(blk, bank, b_bank)

        dbufs = {}

        def db(name, i=0):
            k = (name, i)
            if k not in dbufs:
                dbufs[k] = Buf("%s_%s" % (name, i))
            return dbufs[k]

        ATv = AT[:].rearrange("p (k n) -> p k n", n=512)

        def phase_rope():
            for tt in range(NT):
                t0 = tt * TW
                pi_t, b_pi = next_evf()
                S_.dma("sync", pi_t[:].bitcast(I32), pos[t0:t0 + TW].partition_broadcast(128), b_pi, writes=[b_pi])
                ang, b_ang = next_evf()
                S_.op("vector", "tensor_copy", dict(out=ang[:], in_=pi_t[:].bitcast(I32)),
                      reads=[b_pi], writes=[b_ang])
                S_.op("vector", "tensor_scalar", dict(out=ang[:], in0=ang[:], scalar1=rc[:, 0:1], scalar2=None, op0=ALU.mult),
                      reads=[b_ang, b_rc], writes=[b_ang])
                for which, shift, dst in ((0, 0.0, SIN), (1, 0.5 * math.pi, COS)):
                    y, b_y = next_evf()
                    kf, b_kf = next_evf()
                    TWO_PI = 2.0 * math.pi
                    S_.op("vector", "tensor_scalar", dict(out=y[:], in0=ang[:], scalar1=shift, scalar2=None, op0=ALU.add),
                          reads=[b_ang], writes=[b_y])
                    S_.op("vector", "tensor_scalar", dict(out=kf[:], in0=y[:], scalar1=1.0 / TWO_PI, scalar2=None, op0=ALU.mult),
                          reads=[b_y], writes=[b_kf])
                    S_.op("vector", "tensor_copy", dict(out=kf[:].bitcast(I32), in_=kf[:]), reads=[b_kf], writes=[b_kf])
                    S_.op("vector", "tensor_copy", dict(out=kf[:], in_=kf[:].bitcast(I32)), reads=[b_kf], writes=[b_kf])
                    S_.op("vector", "scalar_tensor_tensor", dict(out=y[:], in0=kf[:], scalar=-TWO_PI, in1=y[:], op0=ALU.mult, op1=ALU.add),
                          reads=[b_kf, b_y], writes=[b_y])
                    S_.op("vector", "tensor_scalar", dict(out=kf[:], in0=y[:], scalar1=math.pi, scalar2=TWO_PI, op0=ALU.is_gt, op1=ALU.mult),
                          reads=[b_y], writes=[b_kf])
                    S_.op("vector", "tensor_tensor", dict(out=y[:], in0=y[:], in1=kf[:], op=ALU.subtract), reads=[b_y, b_kf], writes=[b_y])
                    S_.op("vector", "tensor_scalar", dict(out=y[:], in0=y[:], scalar1=3.1415925, scalar2=-3.1415925, op0=ALU.min, op1=ALU.max),
                          reads=[b_y], writes=[b_y])
                    S_.op("scalar", "activation", dict(out=y[:], in_=y[:], func=AF.Sin), reads=[b_y], writes=[b_y])
                    if which == 0:
                        S_.op("vector", "tensor_scalar", dict(out=y[:], in0=y[:], scalar1=rc[:, 1:2], scalar2=None, op0=ALU.mult),
                              reads=[b_y, b_rc], writes=[b_y])
                    S_.dma(STQ, dst[:, t0:t0 + TW], y[:], b_y, reads=[b_y], writes=[db("rope", tt)])

        def phase_mem():
            load_gain(mem_norm_g[0])
            norm_tile(mem, 0, MT, [], ATv, b_AT)
            for l in range(depth):
                wsrc2d = wb["w_mem_kv"][l]
                rd = [wbuf[("w_mem_kv", l)]]
                wk, b_wk = load_w(wsrc(wsrc2d, 0, 512), KC, 512, rd)
                evb, b_evb = next_evb()

                def evac_k(c, bank, b_bank, evb=evb, b_evb=b_evb):
                    copy_alt(evb[:, c, 0:MT], bank[:, 0:MT], [b_bank], [b_evb])
                lin_ws(wk, b_wk, KC, MH, ATv, b_AT, MT, evac_k)
                S_.dma(STQ, MK[l].rearrange("h p t -> p h t"), evb[:, :, 0:MT], b_evb, reads=[b_evb], writes=[db("MK", l)])
                wvv, b_wv = load_w(wsrc(wsrc2d, 512, 512), KC, 512, rd)
                evb2, b_evb2 = next_evb()

                def evac_v(blk, bank, b_bank, evb2=evb2, b_evb2=b_evb2):
                    copy_alt(evb2[:, blk, :], bank[:, :], [b_bank], [b_evb2])
                lin_as(wvv, b_wv, KC, 512, ATv, b_AT, MT // 128, evac_v)
                S_.dma(STQ, MV[l].rearrange("(b p) c -> p b c", p=128), evb2[:, 0:MT // 128, :], b_evb2, reads=[b_evb2], writes=[db("MV", l)])

        def phase_p1a(l, hsrc, hsrc_name):
            load_gain(attn_norm_g[l])
            w2d = wb["a_w_in"][l]
            rd = [wbuf[("a_w_in", l)]]
            for tt in range(NT):
                t0 = tt * TW
                norm_tile(hsrc, t0, TW, [db(hsrc_name, tt)], ATv, b_AT)
                for wt_i in list(range(6)) + [9]:
                    wv, b_w = load_w(wsrc(w2d, wt_i * 512, 512), KC, 512, rd)
                    evb, b_evb = next_evb()

                    def evac(c, bank, b_bank, evb=evb, b_evb=b_evb):
                        copy_alt(evb[:, c, :], bank[:, :], [b_bank], [b_evb])
                    lin_ws(wv, b_w, KC, 4, ATv, b_AT, TW, evac)
                    if wt_i < 3:
                        dst = QA[wt_i * 4:(wt_i + 1) * 4, :, t0:t0 + TW]; dn = "QA"
                    elif wt_i < 6:
                        dst = KA[(wt_i - 3) * 4:(wt_i - 2) * 4, :, t0:t0 + TW]; dn = "KA"
                    else:
                        dst = QM[:, :, t0:t0 + TW]; dn = "QM"
                    S_.dma(STQ, dst.rearrange("h p t -> p h t"), evb[:], b_evb, reads=[b_evb], writes=[db(dn + str(wt_i), tt)])
                for wt_i in range(6, 9):
                    wv, b_w = load_w(wsrc(w2d, wt_i * 512, 512), KC, 512, rd)
                    evb, b_evb = next_evb()

                    def evac(blk, bank, b_bank, evb=evb, b_evb=b_evb):
                        copy_alt(evb[:, blk, :], bank[:, :], [b_bank], [b_evb])
                    lin_as(wv, b_w, KC, 512, ATv, b_AT, TW // 128, evac)
                    c0 = (wt_i - 6) * 512
                    S_.dma(STQ, VA[t0:t0 + TW, c0:c0 + 512].rearrange("(b p) c -> p b c", p=128), evb[:], b_evb,
                           reads=[b_evb], writes=[db("VA" + str(wt_i), tt)])

        def seq_reads(names, s):
            r = []
            for n in names:
                for tt in range(s * TPS, (s + 1) * TPS):
                    r.append(db(n, tt))
            return r

        SM = 2048
        assert S <= SM

        def arena_view(slot, n=S):
            return ARENA[:, slot * SM:slot * SM + n]
        QT_ = [arena_view(i) for i in range(2)]
        KT_ = [arena_view(2 + i) for i in range(2)]
        VV_ = [arena_view(4 + i) for i in range(2)]
        QR_ = [arena_view(6 + i) for i in range(2)]
        KR_ = [arena_view(8), arena_view(9)]
        SPSR_t = [sb("spsr%d" % q, [128, 512], F32R) for q in range(2)]
        SPSR = [t[:, :] for t in SPSR_t]
        SPS = [t[:, :].bitcast(F32) for t in SPSR_t]
        b_SPS = [Buf("sps%d" % q) for q in range(2)]
        b_Q = [Buf("aq0"), Buf("aq1")]
        b_K = [Buf("ak0"), Buf("ak1")]
        b_V = [Buf("av0"), Buf("av1")]
        b_QR = [Buf("aqr0"), Buf("aqr1")]
        b_KR = [Buf("akr0"), Buf("akr1")]
        TE = [ARENA[:, 18432 + i * 1024:18432 + (i + 1) * 1024].bitcast(F32) for i in range(2)]; b_TE = [Buf("te0"), Buf("te1")]
        TSPR_t = [sb("tspr%d" % i, [128, 512], F32R) for i in range(2)]; b_TSP = [Buf("tsp0"), Buf("tsp1")]
        TSPR = [t[:, :] for t in TSPR_t]
        TSP = [t[:, :].bitcast(F32) for t in TSPR_t]
        TA2_t = sb("ta2", [128, 512], F32)
        TA = [ARENA[:, 20480 + i * 1024:20480 + (i + 1) * 1024].bitcast(F32) for i in range(2)] + [TA2_t[:, :]]
        b_TA = [Buf("ta0"), Buf("ta1"), Buf("ta2")]
        TW_ = [sb("tw%d" % i, [128, 512], BF16) for i in range(2)]; b_TW = [Buf("tw0"), Buf("tw1")]
        RD = sb("rden", [128, 512], F32); b_RD = Buf("rden")
        OB = [sb("ob%d" % i, [128, 512], BF16) for i in range(2)]; b_OB = [Buf("ob0"), Buf("ob1")]
        C32 = sb("c32", [128, 2, 128], F32R); b_C32 = Buf("c32")
        S_.op("vector", "tensor_copy", dict(out=C32[:, 0, :], in_=tri_lo), reads=[b_cst], writes=[b_C32])
        S_.op("vector", "tensor_copy", dict(out=C32[:, 1, :], in_=ones_m), reads=[b_cst], writes=[b_C32])
        TRI32R = C32[:, 0, :]
        ONES32R = C32[:, 1, :]
        att_bufs = b_Q + b_K + b_V + b_QR + b_KR + b_TE + b_TA

        def att_begin():
            S_.fence([b_FA] + att_bufs, att_bufs)

        def att_end():
            S_.fence(att_bufs, [b_FA])

        def phase_p2a(l):
            scale = 128 ** -0.5
            att_begin()
            heads = [(s, h) for s in range(NS) for h in range(SBH)]

            def load_head(hix):
                s, h = heads[hix]
                j = hix % 2
                c0 = s * S
                S_.dma("sync", QT_[j], QA[h, :, c0:c0 + S], b_Q[j], reads=seq_reads(["QA%d" % (h // 4)], s), writes=[b_Q[j]])
                S_.dma("sync", KT_[j], KA[h, :, c0:c0 + S], b_K[j], reads=seq_reads(["KA%d" % (3 + h // 4)], s), writes=[b_K[j]])
                S_.dma("sync", VV_[j].rearrange("p (b d) -> p b d", d=128),
                       VA[c0:c0 + S, h * 128:(h + 1) * 128].rearrange("(b p) d -> p b d", p=128), b_V[j],
                       reads=seq_reads(["VA%d" % (6 + h // 4)], s), writes=[b_V[j]])

            jobs = []
            tix = 0
            for hix in range(len(heads)):
                for qt in range(TPS):
                    nkb = (qt * TW + TW) // 128
                    NP = nkb
                    for n, kb in enumerate(range(nkb - 1, -1, -1)):
                        jj = kb - qt * 4
                        cs = max(0, jj) * 128
                        jobs.append((hix, qt, n, kb, cs, jj, NP, tix))
                    tix += 1
            NJ = len(jobs)
            CUMS = [(PB[2], b_PB[2]), (PB[5], b_PB[5])]
            load_head(0)
            if len(heads) > 1:
                load_head(1)

            def after_job(g):
                hix = jobs[g][0]
                if (g + 1 == NJ or jobs[g + 1][0] != hix) and hix + 2 < len(heads):
                    load_head(hix + 2)

            def stA(g):
                hix, qt, n, kb, cs, jj, NP, tix = jobs[g]
                j = hix % 2
                q0 = qt * TW
                if n == 0:
                    OO, b_OO = PB[3 + (tix % 2)], b_PB[3 + (tix % 2)]
                    S_.op("tensor", "matmul", dict(out=OO[:, :], lhsT=zeros_m, rhs=QT_[j][:, q0:q0 + TW], start=True, stop=False),
                          reads=[b_cst, b_Q[j]], writes=[b_OO], signal=False)
                Z, b_Z = PB[g % 2], b_PB[g % 2]
                S_.op("tensor", "matmul", dict(out=Z[:, cs:TW], lhsT=KT_[j][:, kb * 128:(kb + 1) * 128], rhs=QT_[j][:, q0 + cs:q0 + TW],
                                               start=True, stop=True),
                      reads=[b_K[j], b_Q[j]], writes=[b_Z])

            def stB(g):
                hix, qt, n, kb, cs, jj, NP, tix = jobs[g]
                m = g % 2
                Z, b_Z = PB[m], b_PB[m]
                S_.op("scalar", "activation", dict(out=TE[m][:, cs:TW], in_=Z[:, cs:TW], func=AF.Exp, scale=scale),
                      reads=[b_Z], writes=[b_TE[m]])
                S_.op("scalar", "activation", dict(out=TSPR[m][:, cs:TW], in_=TE[m][:, cs:TW], func=AF.Ln, bias=1.0, scale=1.0),
                      reads=[b_TE[m]], writes=[b_TSP[m]])
                S_.op("vector", "scalar_tensor_tensor", dict(out=TA[g % 3][:, cs:TW], in0=Z[:, cs:TW], scalar=scale, in1=TSP[m][:, cs:TW],
                                                             op0=ALU.mult, op1=ALU.subtract),
                      reads=[b_Z, b_TSP[m]], writes=[b_TA[g % 3]])
                if jj >= 0:
                    S_.op("gpsimd", "tensor_tensor", dict(out=TSPR[m][:, cs:cs + 128], in0=TSP[m][:, cs:cs + 128], in1=mask_st, op=ALU.mult),
                          reads=[b_TSP[m], b_cst, b_TA[g % 3]], writes=[b_TSP[m]])

            def stC(g):
                hix, qt, n, kb, cs, jj, NP, tix = jobs[g]
                m = g % 2
                CUM, b_CUM = CUMS[m]
                sps = SPS
                spsr = SPSR
                b_sps = b_SPS
                if n == 0:
                    for q in range(2):
                        S_.op("vector", "tensor_scalar", dict(out=SPSR[q], in0=cst[:, 3:7, :].rearrange("p a b -> p (a b)"), scalar1=0.0, scalar2=None, op0=ALU.mult),
                              reads=[b_cst], writes=[b_SPS[q]])
                S_.op("tensor", "matmul", dict(out=CUM[:, cs:TW], lhsT=TRI32R, rhs=TSPR[m][:, cs:TW], start=True, stop=(n == 0)),
                      reads=[b_C32, b_TSP[m]], writes=[b_CUM], signal=(n == 0))
                if n > 0:
                    S_.op("tensor", "matmul", dict(out=CUM[:, cs:TW], lhsT=ONES32R, rhs=spsr[n % 2][:, cs:TW], start=False, stop=True),
                          reads=[b_C32, b_sps[n % 2]], writes=[b_CUM])
                if n + 1 < NP:
                    S_.op("gpsimd", "tensor_tensor", dict(out=spsr[(n + 1) % 2][:, cs:TW], in0=sps[n % 2][:, cs:TW], in1=TSP[m][:, cs:TW], op=ALU.add),
                          reads=[b_sps[n % 2], b_TSP[m]], writes=[b_sps[(n + 1) % 2]])

            def stD(g):
                hix, qt, n, kb, cs, jj, NP, tix = jobs[g]
                m = g % 2
                CUM, b_CUM = CUMS[m]
                S_.op("vector", "tensor_tensor", dict(out=TA[g % 3][:, cs:TW], in0=TA[g % 3][:, cs:TW], in1=CUM[:, cs:TW], op=ALU.subtract),
                      reads=[b_TA[g % 3], b_CUM], writes=[b_TA[g % 3]])

            def stF(g):
                hix, qt, n, kb, cs, jj, NP, tix = jobs[g]
                m = g % 2
                S_.op("scalar", "activation", dict(out=TW_[m][:, cs:TW], in_=TA[g % 3][:, cs:TW], func=AF.Exp),
                      reads=[b_TA[g % 3]], writes=[b_TW[m]])
                if jj >= 0:
                    S_.op("gpsimd", "tensor_tensor", dict(out=TW_[m][:, cs:cs + 128], in0=TW_[m][:, cs:cs + 128], in1=mask_st, op=ALU.mult),
                          reads=[b_TW[m], b_cst], writes=[b_TW[m]])

            def stG(g):
                hix, qt, n, kb, cs, jj, NP, tix = jobs[g]
                m = g % 2
                j = hix % 2
                s, h = heads[hix]
                OO, b_OO = PB[3 + (tix % 2)], b_PB[3 + (tix % 2)]
                S_.op("tensor", "matmul", dict(out=OO[:, cs:TW], lhsT=VV_[j][:, kb * 128:(kb + 1) * 128], rhs=TW_[m][:, cs:TW],
                                               start=False, stop=(n == NP - 1)),
                      reads=[b_V[j], b_TW[m]], writes=[b_OO], signal=True)
                if n == NP - 1:
                    ob, b_ob = OB[tix % 2], b_OB[tix % 2]
                    copy_alt(ob[:, :], OO[:, :], [b_OO], [b_ob])
                    c0 = s * S + qt * TW
                    S_.dma("sync", MIXT[h * 128:(h + 1) * 128, c0:c0 + TW], ob[:, :], b_ob, reads=[b_ob],
                           writes=[db("MIXT%d" % h, s * TPS + qt)])

            for i in range(-2, NJ + 1):
                if 0 <= i + 2 < NJ:
                    stA(i + 2)
                    stB(i + 2)
                if 0 <= i + 1 < NJ:
                    stC(i + 1)
                    stD(i + 1)
                if 0 <= i - 1 < NJ:
                    stG(i - 1)
                    after_job(i - 1)
                if 0 <= i < NJ:
                    stF(i)
                if pending_casts and i >= 0 and i % 16 == 8:
                    trickle()
            att_end()

        def softmax_heads(heads, load_head, tile_jobs, z_mms, v_of, mask, scale, out_rows):
            jobs = []
            tix = 0
            for hix in range(len(heads)):
                for qt in range(TPS):
                    kbs = tile_jobs(qt)
                    for idx, (kb, cs, dg) in enumerate(kbs):
                        jobs.append((hix, qt, kb, cs, dg, idx == 0, idx == len(kbs) - 1, tix))
                    tix += 1
            NJ = len(jobs)
            DNS = [(PB[5], b_PB[5]), (PB[2], b_PB[2])]
            load_head(0)
            if len(heads) > 1:
                load_head(1)

            def after_job(g):
                hix = jobs[g][0]
                if (g + 1 == NJ or jobs[g + 1][0] != hix) and hix + 2 < len(heads):
                    load_head(hix + 2)

            def stZ(g):
                hix, qt, kb, cs, dg, first, last, tix = jobs[g]
                z_mms(heads[hix], hix % 2, qt, kb, cs, PB[g % 2], b_PB[g % 2])

            def stE(g):
                hix, qt, kb, cs, dg, first, last, tix = jobs[g]
                m = g % 2
                Z, b_Z = PB[m], b_PB[m]
                S_.op("scalar", "activation", dict(out=TW_[m][:, cs:TW], in_=Z[:, cs:TW], func=AF.Exp, scale=scale),
                      reads=[b_Z], writes=[b_TW[m]])
                if dg:
                    S_.op("gpsimd", "tensor_tensor", dict(out=TW_[m][:, cs:cs + 128], in0=TW_[m][:, cs:cs + 128], in1=mask, op=ALU.mult),
                          reads=[b_TW[m], b_cst], writes=[b_TW[m]])

            def stP(g):
                hix, qt, kb, cs, dg, first, last, tix = jobs[g]
                m = g % 2
                j = hix % 2
                OO, b_OO = PB[3 + (tix % 2)], b_PB[3 + (tix % 2)]
                DN, b_DN = DNS[tix % 2]
                v_ap, b_v = v_of(j, kb)
                S_.op("tensor", "matmul", dict(out=OO[:, cs:TW], lhsT=v_ap, rhs=TW_[m][:, cs:TW], start=first, stop=last),
                      reads=[b_v, b_TW[m]], writes=[b_OO], signal=False)
                S_.op("tensor", "matmul", dict(out=DN[:, cs:TW], lhsT=ones_m, rhs=TW_[m][:, cs:TW], start=first, stop=last),
                      reads=[b_cst, b_TW[m]], writes=[b_DN, b_OO], signal=True)
                if last:
                    S_.op("vector", "reciprocal", dict(out=RD[:, :], in_=DN[:, :]), reads=[b_DN], writes=[b_RD])
                    ob, b_ob = OB[tix % 2], b_OB[tix % 2]
                    S_.op("vector", "tensor_tensor", dict(out=ob[:, :], in0=OO[:, :], in1=RD[:, :], op=ALU.mult),
                          reads=[b_OO, b_RD], writes=[b_ob])
                    r0, c0, nm, ti = out_rows(heads[hix], qt)
                    S_.dma("sync", MIXT[r0:r0 + 128, c0:c0 + TW], ob[:, :], b_ob, reads=[b_ob], writes=[db(nm, ti)])

            for i in range(-1, NJ):
                if i + 1 < NJ:
                    stZ(i + 1)
                if i >= 0:
                    stP(i)
                    after_job(i)
                if i + 1 < NJ:
                    stE(i + 1)
                if pending_casts and i >= 0 and i % 16 == 8:
                    trickle()

        def phase_mem_attn(l):
            att_begin()
            nmb = ML // 128
            heads = [(s, h) for s in range(NS) for h in range(MH)]

            def load_head(hix):
                s, h = heads[hix]
                j = hix % 2
                c0 = s * S
                S_.dma("sync", QT_[j], QM[h, :, c0:c0 + S], b_Q[j], reads=seq_reads(["QM9", "QMB"], s), writes=[b_Q[j]])
                S_.dma("sync", KT_[j][:, 0:ML], MK[l, h, :, s * ML:(s + 1) * ML], b_K[j], reads=[db("MK", l)], writes=[b_K[j]])
                S_.dma("sync", VV_[j][:, 0:nmb * 128].rearrange("p (b d) -> p b d", d=128),
                       MV[l, s * ML:(s + 1) * ML, h * 128:(h + 1) * 128].rearrange("(b p) d -> p b d", p=128), b_V[j],
                       reads=[db("MV", l)], writes=[b_V[j]])

            def tile_jobs(qt):
                return [(mb, 0, False) for mb in range(nmb)]

            def z_mms(hd, j, qt, kb, cs, Z, b_Z):
                q0 = qt * TW
                S_.op("tensor", "matmul", dict(out=Z[:, :], lhsT=KT_[j][:, kb * 128:(kb + 1) * 128], rhs=QT_[j][:, q0:q0 + TW], start=True, stop=True),
                      reads=[b_K[j], b_Q[j]], writes=[b_Z])

            def v_of(j, kb):
                return VV_[j][:, kb * 128:(kb + 1) * 128], b_V[j]

            def out_rows(hd, qt):
                s, h = hd
                return (SBH + h) * 128, s * S + qt * TW, "MIXT%d" % (SBH + h), s * TPS + qt
            softmax_heads(heads, load_head, tile_jobs, z_mms, v_of, mask_in, 128 ** -0.5, out_rows)
            att_end()

        FAv = ARENA[:].rearrange("p (k n) -> p k n", n=512)

        def phase_p3(l, hsrc, hsrc_name, wout2d, wout_rd):
            wgu = wb["ffn_w_gu"][l]
            wdn = wb["ffn_w_down"][l]
            rd_gu = [wbuf[("ffn_w_gu", l)]]
            rd_dn = [wbuf[("ffn_w_down", l)]]
            load_gain(ffn_norm_g[l])
            trickle(len(pending_casts))
            for tt in range(NT):
                t0 = tt * TW
                S_.dma("sync", ATv, MIXT[:, t0:t0 + TW].rearrange("(k p) t -> p k t", p=128), b_AT,
                       reads=[db("MIXT%d" % hh, tt) for hh in range(16)], writes=[b_AT])
                for cc in range(4):
                    wv, b_w = load_w(wsrc(wout2d, cc * 512, 512), KC, 512, wout_rd)

                    pend = {}

                    def ld_hsl(blk, cc=cc, pend=pend):
                        hsl, b_hsl = next_evh()
                        r0 = t0 + blk * 128
                        S_.dma("scalar", hsl[:, :], hsrc[r0:r0 + 128, cc * 512:(cc + 1) * 512], b_hsl, reads=[db(hsrc_name, tt)], writes=[b_hsl])
                        pend[blk] = (hsl, b_hsl)

                    def evac(blk, bank, b_bank, cc=cc, pend=pend, ld_hsl=ld_hsl):
                        if blk == 0:
                            ld_hsl(0)
                        if blk + 1 < TW // 128:
                            ld_hsl(blk + 1)
                        hsl, b_hsl = pend[blk]
                        r0 = t0 + blk * 128
                        ev, b_ev = next_evf()
                        S_.op("vector", "tensor_tensor", dict(out=ev[:, :], in0=bank[:, :], in1=hsl[:, :], op=ALU.add),
                              reads=[b_bank, b_hsl], writes=[b_ev])
                        S_.dma("scalar", H1[r0:r0 + 128, cc * 512:(cc + 1) * 512], ev[:, :], b_ev, reads=[b_ev], writes=[db("H1", tt)])
                    lin_as(wv, b_w, KC, 512, ATv, b_AT, TW // 128, evac)
                pre_g = load_w(wsrc(wgu, 0, 512), KC, 512, rd_gu)
                pre_u = load_w(wsrc(wgu, FF, 512), KC, 512, rd_gu)
                norm_tile(H1, t0, TW, [db("H1", tt)], ATv, b_AT)
                for g4 in range(FC // 4):
                    if g4 == 0:
                        (wg, b_wg), (wu, b_wu) = pre_g, pre_u
                    else:
                        wg, b_wg = load_w(wsrc(wgu, g4 * 512, 512), KC, 512, rd_gu)
                        wu, b_wu = load_w(wsrc(wgu, FF + g4 * 512, 512), KC, 512, rd_gu)
                    for c in range(4):
                        fc = g4 * 4 + c
                        G, b_G = next_pb()
                        U, b_U = next_pb()
                        for (wv_, b_w_, bank_, b_bank_) in ((wg, b_wg, G, b_G), (wu, b_wu, U, b_U)):
                            for k in range(KC):
                                l_ap = wv_[:, k, c * 128:(c + 1) * 128]
                                r_ap = ATv[:, k, :]
                                S_.op("tensor", "matmul", dict(out=bank_[:, :], lhsT=l_ap, rhs=r_ap, start=(k == 0), stop=(k == KC - 1)),
                                      reads=[b_w_, b_AT], writes=[b_bank_], signal=(k == KC - 1))
                        sg, b_sg = next_evf()
                        S_.op("scalar", "activation", dict(out=sg[:, :], in_=G[:, :], func=AF.Silu), reads=[b_G], writes=[b_sg])
                        S_.op("vector", "tensor_tensor", dict(out=FAv[:, fc, :], in0=U[:, :], in1=sg[:, :], op=ALU.mult),
                              reads=[b_U, b_sg], writes=[b_FA])
                for cc in range(4):
                    banks = [next_pb() for _ in range(4)]
                    pieces = [(0, 16), (16, 16), (32, 12)]
                    for pi, (f0, nf) in enumerate(pieces):
                        wv, b_w = load_w(wsrc(wdn, cc * 512, 512, r0=f0 * 128, kc=nf), nf, 512, rd_dn)
                        for blk in range(4):
                            bank, b_bank = banks[blk]
                            for k in range(nf):
                                l_ap = FAv[:, f0 + k, blk * 128:(blk + 1) * 128]
                                r_ap = wv[:, k, :]
                                last = (pi == len(pieces) - 1 and k == nf - 1)
                                first = (pi == 0 and k == 0)
                                S_.op("tensor", "matmul", dict(out=bank[:, :], lhsT=l_ap, rhs=r_ap, start=first, stop=last),
                                      reads=[b_w, b_FA], writes=[b_bank], signal=(k == nf - 1))
                    pend = {}

                    def ld_h1(blk):
                        hsl, b_hsl = next_evh()
                        r0 = t0 + blk * 128
                        S_.dma("scalar", hsl[:, :], H1[r0:r0 + 128, cc * 512:(cc + 1) * 512], b_hsl, reads=[db("H1", tt)], writes=[b_hsl])
                        pend[blk] = (hsl, b_hsl)
                    ld_h1(0)
                    for blk in range(4):
                        bank, b_bank = banks[blk]
                        if blk + 1 < 4:
                            ld_h1(blk + 1)
                        hsl, b_hsl = pend[blk]
                        r0 = t0 + blk * 128
                        ev, b_ev = next_evf()
                        S_.op("vector", "tensor_tensor", dict(out=ev[:, :], in0=bank[:, :], in1=hsl[:, :], op=ALU.add),
                              reads=[b_bank, b_hsl], writes=[b_ev])
                        S_.dma("scalar", Hs[r0:r0 + 128, cc * 512:(cc + 1) * 512], ev[:, :], b_ev, reads=[b_ev], writes=[db("Hs", tt)])

        CQv = AT
        CT = sb("ct", [128, 4, 512], BF16); b_CT = Buf("ct")
        XQ = [sb("xq%d" % i, [128, QL], BF16) for i in range(2)]; b_XQ = [Buf("xq0"), Buf("xq1")]
        CS = sb("cs", [128, 512], F32); b_CS = Buf("cs")
        SN = sb("sn", [128, 512], F32); b_SN = Buf("sn")

        def load_rope(tt):
            t0 = tt * TW
            S_.dma("sync", CS[:, :], COS[:, t0:t0 + TW], b_CS, reads=[db("rope", tt)], writes=[b_CS])
            S_.dma("sync", SN[:, :], SIN[:, t0:t0 + TW], b_SN, reads=[db("rope", tt)], writes=[b_SN])

        def rope_evac(bankA, b_A, bankB, b_B, out_ap, b_out):
            t1, b_t1 = next_evf()
            t2, b_t2 = next_evf()
            S_.op("vector", "tensor_tensor", dict(out=t1[:, :], in0=bankA[:, :], in1=CS[:, :], op=ALU.mult), reads=[b_A, b_CS], writes=[b_t1])
            S_.op("vector", "tensor_tensor", dict(out=t2[:, :], in0=bankB[:, :], in1=SN[:, :], op=ALU.mult), reads=[b_B, b_SN], writes=[b_t2])
            S_.op("gpsimd", "tensor_tensor", dict(out=out_ap, in0=t1[:, :], in1=t2[:, :], op=ALU.add), reads=[b_t1, b_t2], writes=[b_out])

        def latent_norm_T(bank, b_bank, blk, gq_loaded):
            j = blk % 2
            rstd_of(bank[:, :], QL, b_bank, j, XQ[j][:, :], b_XQ[j])
            S_.op("vector", "scalar_tensor_tensor", dict(out=XQ[j][:, :], in0=bank[:, :], scalar=RS[j][:, 0:1], in1=GQ[:, :],
                                                             op0=ALU.mult, op1=ALU.mult),
                  reads=[b_bank, b_RS[j], b_GQ], writes=[b_XQ[j]])
            transpose_into(XQ[j], b_XQ[j], 4, CT, b_CT, blk * 128)

        def rope_weight_tiles(w2d, rd, col_of_pairchunk):
            raise NotImplementedError

        def phase_kv(hsrc, hsrc_name):
            load_gain(kv_norm_g[0])
            S_.dma("sync", GQ[:, :], kv_latent_g[0].partition_broadcast(128), b_GQ, writes=[b_GQ])
            wd = wb["w_dkv"]
            rd = [wbuf[("w_dkv", None)]]
            wu = wb["w_ukv"]
            rdu = [wbuf[("w_ukv", None)]]
            for tt in range(NT):
                t0 = tt * TW
                norm_tile(hsrc, t0, TW, [db(hsrc_name, tt)], ATv, b_AT)
                load_rope(tt)
                wv, b_w = load_w(wsrc(wd, 0, 512), KC, 512, rd)

                def evac_lat(blk, bank, b_bank):
                    latent_norm_T(bank, b_bank, blk, True)
                lin_as(wv, b_w, KC, 512, ATv, b_AT, TW // 128, evac_lat)
                i = wctr[0] % NW
                wctr[0] += 1
                wr = WS[i][:, 0:KC * 256].rearrange("p (k n) -> p k n", n=256)
                for q, (dst0, src0) in enumerate([(0, 512), (64, 512), (128, 544), (160, 512), (192, 544), (224, 512)]):
                    n = 64 if q < 2 else 32
                    S_.dma("sync", wr[:, :, dst0:dst0 + n], wsrc(wd, src0, n), b_WS[i], reads=rd, writes=[b_WS[i]] if q == 0 else [], chain=False)
                b_WS[i].w = b_WS[i].dlast
                bA, b_bA = next_pb()
                bB, b_bB = next_pb()
                for (off, bank_, b_bank_) in ((0, bA, b_bA), (128, bB, b_bB)):
                    for k in range(KC):
                        l_ap = wr[:, k, off:off + 128]
                        r_ap = ATv[:, k, :]
                        S_.op("tensor", "matmul", dict(out=bank_[:, :], lhsT=l_ap, rhs=r_ap, start=(k == 0), stop=(k == KC - 1)),
                              reads=[b_WS[i], b_AT], writes=[b_bank_], signal=(k == KC - 1))
                ob, b_ob = OB[tt % 2], b_OB[tt % 2]
                rope_evac(bA, b_bA, bB, b_bB, ob[:, :], b_ob)
                S_.dma(STQ, KR[:, t0:t0 + TW], ob[:, :], b_ob, reads=[b_ob], writes=[db("KR", tt)])
                for g in range(3):
                    i2 = wctr[0] % NW
                    wctr[0] += 1
                    wn = WS[i2][:, 0:4 * 512].rearrange("p (k n) -> p k n", n=512)
                    for hh in range(4):
                        h = g * 4 + hh
                        S_.dma("sync", wn[:, :, hh * 128:(hh + 1) * 128], wsrc(wu, h * 256, 128, kc=4), b_WS[i2], reads=rdu,
                               writes=[b_WS[i2]] if hh == 0 else [], chain=False)
                    b_WS[i2].w = b_WS[i2].dlast
                    evb, b_evb = next_evb()

                    def evac(c, bank, b_bank, evb=evb, b_evb=b_evb):
                        copy_alt(evb[:, c, :], bank[:, :], [b_bank], [b_evb])
                    lin_ws(wn, b_WS[i2], 4, 4, CT, b_CT, TW, evac)
                    S_.dma(STQ, KN[g * 4:(g + 1) * 4, :, t0:t0 + TW].rearrange("h p t -> p h t"), evb[:], b_evb, reads=[b_evb], writes=[db("KN%d" % g, tt)])
                for g in range(3):
                    i2 = wctr[0] % NW
                    wctr[0] += 1
                    wn = WS[i2][:, 0:4 * 512].rearrange("p (k n) -> p k n", n=512)
                    for hh in range(4):
                        h = g * 4 + hh
                        S_.dma("sync", wn[:, :, hh * 128:(hh + 1) * 128], wsrc(wu, h * 256 + 128, 128, kc=4), b_WS[i2], reads=rdu,
                               writes=[b_WS[i2]] if hh == 0 else [], chain=False)
                    b_WS[i2].w = b_WS[i2].dlast
                    evb, b_evb = next_evb()

                    def evac(blk, bank, b_bank, evb=evb, b_evb=b_evb):
                        copy_alt(evb[:, blk, :], bank[:, :], [b_bank], [b_evb])
                    lin_as(wn, b_WS[i2], 4, 512, CT, b_CT, TW // 128, evac)
                    S_.dma(STQ, VL[t0:t0 + TW, g * 512:(g + 1) * 512].rearrange("(b p) c -> p b c", p=128), evb[:], b_evb,
                           reads=[b_evb], writes=[db("VL%d" % g, tt)])

        def phase_p1b(l, hsrc, hsrc_name):
            i_b = l - na
            load_gain(attn_norm_g[l])
            S_.dma("sync", GQ[:, :], b_q_norm_g[i_b].partition_broadcast(128), b_GQ, writes=[b_GQ])
            w2d = wb["b_w_in"][i_b]
            rd = [wbuf[("b_w_in", i_b)]]
            wq = wb["b_w_uq"][i_b]
            rdq = [wbuf[("b_w_uq", i_b)]]
            for tt in range(NT):
                t0 = tt * TW
                norm_tile(hsrc, t0, TW, [db(hsrc_name, tt)], ATv, b_AT)
                load_rope(tt)
                wv, b_w = load_w(wsrc(w2d, 0, 512), KC, 512, rd)

                def evac_lat(blk, bank, b_bank):
                    latent_norm_T(bank, b_bank, blk, True)
                lin_as(wv, b_w, KC, 512, ATv, b_AT, TW // 128, evac_lat)
                wv, b_w = load_w(wsrc(w2d, 512, 512), KC, 512, rd)
                evb, b_evb = next_evb()

                def evac(c, bank, b_bank, evb=evb, b_evb=b_evb):
                    copy_alt(evb[:, c, :], bank[:, :], [b_bank], [b_evb])
                lin_ws(wv, b_w, KC, 4, ATv, b_AT, TW, evac)
                S_.dma(STQ, QM[:, :, t0:t0 + TW].rearrange("h p t -> p h t"), evb[:], b_evb, reads=[b_evb], writes=[db("QMB", tt)])
                for g in range(3):
                    i2 = wctr[0] % NW
                    wctr[0] += 1
                    wn = WS[i2][:, 0:4 * 512].rearrange("p (k n) -> p k n", n=512)
                    for hh in range(4):
                        h = g * 4 + hh
                        S_.dma("sync", wn[:, :, hh * 128:(hh + 1) * 128], wsrc(wq, h * 192, 128, kc=4), b_WS[i2], reads=rdq,
                               writes=[b_WS[i2]] if hh == 0 else [], chain=False)
                    b_WS[i2].w = b_WS[i2].dlast
                    evb, b_evb = next_evb()

                    def evac(c, bank, b_bank, evb=evb, b_evb=b_evb):
                        copy_alt(evb[:, c, :], bank[:, :], [b_bank], [b_evb])
                    lin_ws(wn, b_WS[i2], 4, 4, CT, b_CT, TW, evac)
                    S_.dma(STQ, QA[g * 4:(g + 1) * 4, :, t0:t0 + TW].rearrange("h p t -> p h t"), evb[:], b_evb, reads=[b_evb], writes=[db("QA%d" % g, tt)])
                for pr in range(6):
                    i2 = wctr[0] % NW
                    wctr[0] += 1
                    wr = WS[i2][:, 0:4 * 256].rearrange("p (k n) -> p k n", n=256)
                    first = True
                    for hh in range(2):
                        h = pr * 2 + hh
                        cb = h * 192 + 128
                        for (dst0, src0, n) in ((hh * 64, cb, 64), (128 + hh * 64, cb + 32, 32), (128 + hh * 64 + 32, cb, 32)):
                            S_.dma("sync", wr[:, :, dst0:dst0 + n], wsrc(wq, src0, n, kc=4), b_WS[i2], reads=rdq,
                                   writes=[b_WS[i2]] if first else [], chain=False)
                            first = False
                    b_WS[i2].w = b_WS[i2].dlast
                    bA, b_bA = next_pb()
                    bB, b_bB = next_pb()
                    for (off, bank_, b_bank_) in ((0, bA, b_bA), (128, bB, b_bB)):
                        for k in range(4):
                            l_ap = wr[:, k, off:off + 128]
                            r_ap = CT[:, k, :]
                            S_.op("tensor", "matmul", dict(out=bank_[:, :], lhsT=l_ap, rhs=r_ap, start=(k == 0), stop=(k == 3)),
                                  reads=[b_WS[i2], b_CT], writes=[b_bank_], signal=(k == 3))
                    ob, b_ob = OB[pr % 2], b_OB[pr % 2]
                    rope_evac(bA, b_bA, bB, b_bB, ob[:, :], b_ob)
                    S_.dma(STQ, QR[pr, :, t0:t0 + TW], ob[:, :], b_ob, reads=[b_ob], writes=[db("QR%d" % pr, tt)])

        def phase_p2b(l):
            att_begin()
            heads = [(s, h) for s in range(NS) for h in range(SBH)]

            def load_head(hix):
                s, h = heads[hix]
                j = hix % 2
                c0 = s * S
                if h == 0:
                    S_.dma("sync", KR_[s % 2], KR[:, c0:c0 + S], b_KR[s % 2], reads=seq_reads(["KR"], s), writes=[b_KR[s % 2]])
                S_.dma("sync", QT_[j], QA[h, :, c0:c0 + S], b_Q[j], reads=seq_reads(["QA%d" % (h // 4)], s), writes=[b_Q[j]])
                S_.dma("sync", QR_[j], QR[h // 2, :, c0:c0 + S], b_QR[j], reads=seq_reads(["QR%d" % (h // 2)], s), writes=[b_QR[j]])
                S_.dma("sync", KT_[j], KN[h, :, c0:c0 + S], b_K[j], reads=seq_reads(["KN%d" % (h // 4)], s), writes=[b_K[j]])
                S_.dma("sync", VV_[j].rearrange("p (b d) -> p b d", d=128),
                       VL[c0:c0 + S, h * 128:(h + 1) * 128].rearrange("(b p) d -> p b d", p=128), b_V[j],
                       reads=seq_reads(["VL%d" % (h // 4)], s), writes=[b_V[j]])

            def tile_jobs(qt):
                r = []
                for kb in range((qt * TW + TW) // 128):
                    jj = kb - qt * 4
                    r.append((kb, max(0, jj) * 128, jj >= 0))
                return r

            def z_mms(hd, j, qt, kb, cs, Z, b_Z):
                s, h = hd
                q0 = qt * TW
                pb0 = (h % 2) * 64
                S_.op("tensor", "matmul", dict(out=Z[:, cs:TW], lhsT=KT_[j][:, kb * 128:(kb + 1) * 128], rhs=QT_[j][:, q0 + cs:q0 + TW],
                                               start=True, stop=False),
                      reads=[b_K[j], b_Q[j]], writes=[b_Z], signal=False)
                S_.op("tensor", "matmul", dict(out=Z[:, cs:TW], lhsT=KR_[s % 2][pb0:pb0 + 64, kb * 128:(kb + 1) * 128],
                                               rhs=QR_[j][pb0:pb0 + 64, q0 + cs:q0 + TW], start=False, stop=True),
                      reads=[b_KR[s % 2], b_QR[j]], writes=[b_Z])

            def v_of(j, kb):
                return VV_[j][:, kb * 128:(kb + 1) * 128], b_V[j]

            def out_rows(hd, qt):
                s, h = hd
                return h * 128, s * S + qt * TW, "MIXT%d" % h, s * TPS + qt
            softmax_heads(heads, load_head, tile_jobs, z_mms, v_of, mask_in, 192 ** -0.5, out_rows)
            att_end()

        def phase_final(hsrc, hsrc_name):
            load_gain(final_norm_g[0])
            for tt in range(NT):
                for blk in range(TW // 128):
                    j = hb_ctr[0] % 2
                    hb_ctr[0] += 1
                    r0 = tt * TW + blk * 128
                    S_.dma("sync", HB[j][:], hsrc[r0:r0 + 128, :], b_HB[j], reads=[db(hsrc_name, tt)], writes=[b_HB[j]])
                    rstd_of(HB[j][:], D, b_HB[j], j, XN[j][:], b_XN[j])
                    S_.op("vector", "scalar_tensor_tensor", dict(out=HB[j][:], in0=HB[j][:], scalar=RS[j][:, 0:1], in1=GB[:],
                                                                         op0=ALU.mult, op1=ALU.mult),
                          reads=[b_HB[j], b_RS[j], b_GB], writes=[b_HB[j]])
                    S_.dma(STQ, out[r0:r0 + 128, :], HB[j][:], b_HB[j], reads=[b_HB[j]], writes=[db("out", tt * 4 + blk)])

        def want(name):
            return (only is None) or (name in only)
        def early_phases():
            if depth > na and want("rope"):
                phase_rope()
            if want("mem"):
                phase_mem()
        if na == 0:
            early_phases()
        hsrc, hname = x, "x"
        for l in range(depth):
            if l == na and want("kv"):
                phase_kv(hsrc, hname)
            if l < na:
                if want("p1_%d" % l):
                    phase_p1a(l, hsrc, hname)
                if l == 0:
                    early_phases()
                if lazy_cast and l + 1 < depth:
                    cast_sink[0] = pending_casts
                    cast_layer(l + 1)
                    cast_sink[0] = None
                if want("p2_%d" % l):
                    phase_p2a(l)
                wout2d, wrd = wb["a_w_out"][l], [wbuf[("a_w_out", l)]]
            else:
                if want("p1_%d" % l):
                    phase_p1b(l, hsrc, hname)
                if lazy_cast and l + 1 < depth:
                    cast_sink[0] = pending_casts
                    cast_layer(l + 1)
                    cast_sink[0] = None
                if want("p2_%d" % l):
                    phase_p2b(l)
                wout2d, wrd = wb["b_w_out"][l - na], [wbuf[("b_w_out", l - na)]]
            if want("ma_%d" % l):
                phase_mem_attn(l)
            if want("p3_%d" % l):
                phase_p3(l, hsrc, hname, wout2d, wrd)
                hsrc, hname = Hs, "Hs"
        if want("final"):
            phase_final(hsrc, hname)
        allb = list(dbufs.values()) + list(wbuf.values())
        S_.final_wait("sync", allb)
        S_.run()
    return nc


_INPUT_ORDER = ["attn_norm_g", "ffn_norm_g", "a_w_in", "a_w_out", "b_w_in", "b_q_norm_g", "b_w_uq", "b_w_out",
                "mem_norm_g", "w_mem_kv", "kv_norm_g", "w_dkv", "kv_latent_g", "w_ukv", "ffn_w_gu", "ffn_w_down",
                "final_norm_g"]


def make_in_maps(inputs, n_cores, ns):
    cst, rc = host_consts()
    shared = {}
    for k in _INPUT_ORDER:
        a = np.ascontiguousarray(np.asarray(inputs[k], dtype=np.float32))
        if a.ndim == 1:
            a = a.reshape(1, -1)
        shared[k] = a
    shared["cst"] = cst
    shared["rc"] = rc
    x = np.asarray(inputs["x"], dtype=np.float32)
    mem = np.asarray(inputs["mem"], dtype=np.float32)
    pos = np.asarray(inputs["positions"], dtype=np.int32)
    maps = []
    for c in range(n_cores):
        m = dict(shared)
        m["x"] = np.ascontiguousarray(x[c * ns:(c + 1) * ns].reshape(-1, D))
        m["mem"] = np.ascontiguousarray(mem[c * ns:(c + 1) * ns].reshape(-1, D))
        m["positions"] = np.ascontiguousarray(pos[c * ns:(c + 1) * ns].reshape(-1))
        maps.append(m)
    return maps


def kernel(**inputs):
    x = np.asarray(inputs["x"])
    B, S, _ = x.shape
    ML = np.asarray(inputs["mem"]).shape[1]
    ns = B // N_CORES
    nc = build(ns, S, ML)
    in_maps = make_in_maps(inputs, N_CORES, ns)
    res = run_bass_kernel_spmd(nc, in_maps, core_ids=list(range(N_CORES)))
    outs = [np.asarray(r["out"]).reshape(ns, S, D) for r in res.results]
    return np.concatenate(outs, axis=0).astype(np.float32)
```
